# Optimizing a Trainium2 kernel written in Bass

```python
import jax, jax.numpy as jnp
from jax import lax
import numpy as np

D_MODEL = 1024
BATCH = 4
SEQ = 8192
DEPTH = 1
DEC_BATCH = 8
DEC_SEQ = 2048
PAST_LEN = 128

MIX_WIDTH = D_MODEL
A_WIDTH = MIX_WIDTH // 2
B_WIDTH = MIX_WIDTH - A_WIDTH
POOL_WINDOWS = (2, 4, 8, 16)
POOL_GROUPS = len(POOL_WINDOWS)
POOL_CH = A_WIDTH // POOL_GROUPS
CHUNK = 128
SGU_HEADS = 4
SGU_HEAD_DIM = B_WIDTH // SGU_HEADS
IN_WIDTH = 2 * A_WIDTH + 3 * B_WIDTH
EPS = 1e-6

kernel_name = "hybrid_pool_sgu_encoder"


def rmsnorm(x, g):
    x32 = x.astype(jnp.float32)
    y = x32 * lax.rsqrt(jnp.mean(x32 * x32, axis=-1, keepdims=True) + EPS)
    return (y * g.astype(jnp.float32)).astype(x.dtype)


def layernorm(x, g, b):
    x32 = x.astype(jnp.float32)
    mu = jnp.mean(x32, axis=-1, keepdims=True)
    xc = x32 - mu
    var = jnp.mean(xc * xc, axis=-1, keepdims=True)
    y = xc * lax.rsqrt(var + EPS) * g.astype(jnp.float32) + b.astype(jnp.float32)
    return y.astype(x.dtype)


def pool_mixer(a, pool_w, pool_scale):
    bsz, s, _ = a.shape
    a32 = a.reshape(bsz, s, POOL_GROUPS, POOL_CH).astype(jnp.float32)
    csum = jnp.concatenate(
        [jnp.zeros((bsz, 1, POOL_GROUPS, POOL_CH), jnp.float32), jnp.cumsum(a32, axis=1)], axis=1)
    t = jnp.arange(s)
    pooled = []
    for g, w in enumerate(POOL_WINDOWS):
        lo = jnp.clip(t - w // 2, 0, s)
        hi = jnp.clip(t + w // 2, 0, s)
        win_sum = csum[:, hi, g] - csum[:, lo, g]
        cnt = (hi - lo).astype(jnp.float32)[None, :, None]
        pooled.append(win_sum / cnt)
    pooled = jnp.stack(pooled, axis=2)
    diff = (pooled - a32).astype(a.dtype)
    y = jnp.einsum('bsgc,gcd->bsgd', diff, pool_w)
    return y.reshape(bsz, s, A_WIDTH) * pool_scale


def spatial_gating(u, v, ln_g, ln_b, w_s, b_s):
    bsz, s, _ = u.shape
    vn = layernorm(v, ln_g, ln_b)
    vc = vn.reshape(bsz, s // CHUNK, CHUNK, SGU_HEADS, SGU_HEAD_DIM)
    mixed = jnp.einsum('hpq,bnqhd->bnphd', w_s, vc) + b_s.T[None, None, :, :, None]
    return u * mixed.reshape(bsz, s, B_WIDTH)


def trunk(x, norm_g, w_in, pool_w, pool_scale, sgu_ln_g, sgu_ln_b, w_spatial, b_spatial,
          w_out, final_g):
    for _ in range(DEPTH):
        h = rmsnorm(x, norm_g)
        p = jnp.einsum('bsd,de->bse', h, w_in)
        a = p[..., :A_WIDTH]
        gate_a = p[..., A_WIDTH:2 * A_WIDTH]
        o = 2 * A_WIDTH
        u = p[..., o:o + B_WIDTH]
        v = p[..., o + B_WIDTH:o + 2 * B_WIDTH]
        gate_b = p[..., o + 2 * B_WIDTH:]
        y_a = pool_mixer(a, pool_w, pool_scale) * jax.nn.silu(gate_a)
        y_b = spatial_gating(u, v, sgu_ln_g, sgu_ln_b, w_spatial, b_spatial) * jax.nn.silu(gate_b)
        mix = jnp.concatenate([y_a, y_b], axis=-1)
        x = x + jnp.einsum('bse,ed->bsd', mix, w_out)
    return rmsnorm(x, final_g)


def setup_inputs(seed: int = 0) -> dict:
    key = jax.random.key(seed)
    ks = jax.random.split(key, 12)
    f32 = jnp.float32
    return {
        "x_prompt": jax.random.normal(ks[0], (BATCH, SEQ, D_MODEL), f32),
        "x_sample": jax.random.normal(ks[1], (DEC_BATCH, DEC_SEQ, D_MODEL), f32),
        "norm_g": 1.0 + 0.1 * jax.random.normal(ks[2], (D_MODEL,), f32),
        "w_in": jax.random.normal(ks[3], (D_MODEL, IN_WIDTH), f32) * D_MODEL ** -0.5,
        "pool_w": jax.random.normal(ks[4], (POOL_GROUPS, POOL_CH, POOL_CH), f32) * POOL_CH ** -0.5,
        "pool_scale": 1.0 + 0.1 * jax.random.normal(ks[5], (A_WIDTH,), f32),
        "sgu_ln_g": 1.0 + 0.1 * jax.random.normal(ks[6], (B_WIDTH,), f32),
        "sgu_ln_b": 0.02 * jax.random.normal(ks[7], (B_WIDTH,), f32),
        "w_spatial": jax.random.normal(ks[8], (SGU_HEADS, CHUNK, CHUNK), f32) * CHUNK ** -0.5,
        "b_spatial": 1.0 + 0.1 * jax.random.normal(ks[9], (SGU_HEADS, CHUNK), f32),
        "w_out": jax.random.normal(ks[10], (MIX_WIDTH, D_MODEL), f32) * MIX_WIDTH ** -0.5,
        "final_g": 1.0 + 0.1 * jax.random.normal(ks[11], (D_MODEL,), f32),
    }


def reference(x_prompt, x_sample, norm_g, w_in, pool_w, pool_scale, sgu_ln_g, sgu_ln_b,
              w_spatial, b_spatial, w_out, final_g):
    y_prompt = trunk(x_prompt, norm_g, w_in, pool_w, pool_scale, sgu_ln_g, sgu_ln_b,
                     w_spatial, b_spatial, w_out, final_g)
    y_sample = trunk(x_sample, norm_g, w_in, pool_w, pool_scale, sgu_ln_g, sgu_ln_b,
                     w_spatial, b_spatial, w_out, final_g)
    return (y_prompt, y_sample)
```

```python
import numpy as np
from contextlib import ExitStack
import concourse.bass as bass
import concourse.mybir as mybir
from concourse.bass_utils import run_bass_kernel_spmd

F32 = mybir.dt.float32
BF16 = mybir.dt.bfloat16
AF = mybir.ActivationFunctionType
ALU = mybir.AluOpType

D = 1024
KD = 8
INW = 2560
EPS = 1e-6
WINDOWS = (2, 4, 8, 16)
NCORES = 8
SEG_CHUNKS = (32, 16)
LA = 2
NX = 13
NPM = 20
PMW = 144


class Sched:
    def __init__(self, nc, es):
        self.nc = nc
        self.es = es
        self.eng = dict(pe=nc.tensor, act=nc.scalar, dve=nc.vector, pool=nc.gpsimd, sp=nc.sync)
        self.sems = {}
        self.cnt = {}
        for e in ("pe", "act", "dve", "pool"):
            self.new_sem(e)
        self.waited = {e: {} for e in self.eng}
        self.lastw = {}
        self.readers = {}
        self.tags = {}
        self.nwaits = 0

    def new_sem(self, name):
        self.sems[name] = self.es.enter_context(self.nc.semaphore(name))
        self.cnt[name] = 0

    def _norm(self, reads, writes):
        rk, wk = [], []
        for it in reads:
            if isinstance(it, tuple):
                assert self.tags.get(it[0]) == it[1], f"stale read {it} has {self.tags.get(it[0])}"
                rk.append(it[0])
            else:
                rk.append(it)
        for it in writes:
            if isinstance(it, tuple):
                self.tags[it[0]] = it[1]
                wk.append(it[0])
            else:
                wk.append(it)
        return rk, wk

    def _waits(self, eng, reads, writes):
        need = {}

        def add(t):
            if t is None:
                return
            s, v = t
            if eng == "pe" and s == "pe":
                return
            if need.get(s, 0) < v:
                need[s] = v

        for k in reads:
            add(self.lastw.get(k))
        for k in writes:
            add(self.lastw.get(k))
            for s, v in self.readers.get(k, {}).items():
                add((s, v))
        E = self.eng[eng]
        w = self.waited[eng]
        for s, v in need.items():
            if w.get(s, 0) < v:
                E.wait_ge(self.sems[s], v)
                w[s] = v
                self.nwaits += 1
        return E

    def _record(self, t, reads, writes):
        for k in reads:
            r = self.readers.setdefault(k, {})
            if r.get(t[0], 0) < t[1]:
                r[t[0]] = t[1]
        for k in writes:
            self.lastw[k] = t
            self.readers[k] = {}

    def op(self, eng, fn, reads=(), writes=(), dsem=None):
        reads, writes = self._norm(reads, writes)
        E = self._waits(eng, reads, writes)
        ins = fn(E)
        if dsem is not None:
            if dsem not in self.sems:
                self.new_sem(dsem)
            sname, inc = dsem, 16
        else:
            sname, inc = eng, 1
        self.cnt[sname] += inc
        ins.then_inc(self.sems[sname], inc)
        t = (sname, self.cnt[sname])
        self._record(t, reads, writes)
        return t

    def dma_group(self, items, dsem, eng="sp"):
        if dsem not in self.sems:
            self.new_sem(dsem)
        for fn, reads, writes in items:
            E = self._waits(eng, reads, writes)
            ins = fn(E)
            self.cnt[dsem] += 16
            ins.then_inc(self.sems[dsem], 16)
        t = (dsem, self.cnt[dsem])
        for fn, reads, writes in items:
            self._record(t, reads, writes)
        return t

    def final_wait(self, eng):
        E = self.eng[eng]
        for s, v in self.cnt.items():
            if v > 0:
                E.wait_ge(self.sems[s], v)


def build_nc(seg_chunks=SEG_CHUNKS, nx=NX):
    nc = bass.Bass("TRN2", target_bir_lowering=False)
    chunks = []
    seg_first = []
    nreal = 0
    for s, n in enumerate(seg_chunks):
        assert n % 4 == 0
        seg_first.append(len(chunks))
        for e in range(n + 2):
            kind = "pre" if e == 0 else ("post" if e == n + 1 else "real")
            chunks.append(dict(seg=s, kind=kind, e=e, ridx=None))
            if kind == "real":
                chunks[-1]["ridx"] = nreal
                nreal += 1
    NCH = len(chunks)
    sts = []
    for s, n in enumerate(seg_chunks):
        for i in range(n // 4):
            sts.append(dict(seg=s, c0=seg_first[s] + 1 + 4 * i))
    NST = len(sts)
    for g, st in enumerate(sts):
        for j in range(4):
            chunks[st["c0"] + j]["st"] = g
            chunks[st["c0"] + j]["pos"] = j

    xin = nc.dram_tensor("xin", [NCH * 128, D], F32, kind="ExternalInput").ap()
    w_in = nc.dram_tensor("w_in", [D, INW], F32, kind="ExternalInput").ap()
    w_out = nc.dram_tensor("w_out", [D, D], F32, kind="ExternalInput").ap()
    d_pm = nc.dram_tensor("pm", [128, NPM * PMW], F32, kind="ExternalInput").ap()
    d_poolw = nc.dram_tensor("poolw", [128, 512], F32, kind="ExternalInput").ap()
    d_wst = nc.dram_tensor("wst", [128, 512], F32, kind="ExternalInput").ap()
    d_ident = nc.dram_tensor("ident", [128, 128], F32, kind="ExternalInput").ap()
    d_cols = nc.dram_tensor("cols", [128, 16], F32, kind="ExternalInput").ap()
    d_fg = nc.dram_tensor("fgb", [128, D], F32, kind="ExternalInput").ap()
    d_gb = nc.dram_tensor("gbc", [128, D], F32, kind="ExternalInput").ap()
    d_lnb = nc.dram_tensor("lnbb", [128, 512], F32, kind="ExternalInput").ap()
    d_bs = nc.dram_tensor("bsb", [128, 512], F32, kind="ExternalInput").ap()
    yout = nc.dram_tensor("yout", [nreal * 128, D], F32, kind="ExternalOutput").ap()

    es = ExitStack()
    with es:
        S = Sched(nc, es)

        def sb(name, shape, dt):
            return es.enter_context(nc.sbuf_tensor("s_" + name, shape, dt))

        xs = [sb(f"xs{i}", [128, D], F32) for i in range(nx)]
        w_in_bf = sb("w_in_bf", [128, KD, INW], BF16)
        w_out_bf = sb("w_out_bf", [128, KD, D], BF16)
        pm_bf = sb("pm_bf", [128, NPM, PMW], BF16)
        poolw_bf = sb("poolw_bf", [128, 4, 128], BF16)
        wst_bf = sb("wst_bf", [128, 4, 128], BF16)
        ident_bf = sb("ident_bf", [128, 128], BF16)
        lnb_bf = sb("lnb_bf", [128, 512], BF16)
        cols = sb("cols_sb", [128, 16], F32)
        fgb = sb("fgb_sb", [128, D], F32)
        gbc = sb("gbc_sb", [128, D], F32)
        cmat = sb("cmat", [128, 4, 128], F32)
        neghalf = sb("neghalf", [128, 1], F32)
        stat = sb("stat", [128, NCH * 16], F32)
        junk = sb("junk", [128, D], BF16)
        htok = [sb(f"htok{i}", [128, D], BF16) for i in range(2)]
        hT = [sb(f"hT{i}", [128, KD, 512], BF16) for i in range(2)]
        hTh = sb("hTh", [128, KD, 128], BF16)
        NA = 6
        a_tok = [sb(f"a_tok{i}", [128, 512], BF16) for i in range(NA)]
        NZ = 4
        z_tok = [sb(f"z_tok{i}", [128, 512], BF16) for i in range(NZ)]
        sga = sb("sga", [128, 4, 512], F32)
        sgb = [sb(f"sgb{i}", [128, 512], F32) for i in range(1)]
        usg = sb("usg", [128, 4, 512], F32)
        t1 = [sb(f"t1_{i}", [128, 512], F32) for i in range(2)]
        dT = sb("dT", [128, 4, 512], BF16)
        mixT = [sb(f"mixT{i}", [128, KD, 512], BF16) for i in range(2)]

        tp = es.enter_context(nc.psum_tensor("tp", [128, D], BF16))
        NTOK, NFM = 4, 3
        tokb = [es.enter_context(nc.psum_tensor(f"tokb{i}", [128, 512], F32)) for i in range(NTOK)]
        fmb = [es.enter_context(nc.psum_tensor(f"fmb{i}", [128, 512], F32)) for i in range(NFM)]
        ring = dict(tok=0, fm=0, xs=0, t1=0)

        def nxt(name, n):
            v = ring[name]
            ring[name] = (v + 1) % n
            return v

        S.op("pool", lambda E: E.memset(neghalf[:], -0.5), writes=["neghalf"])
        S.op("act", lambda E: E.activation(out=junk[:, 0:1], in_=neghalf[:], func=AF.Silu),
             reads=["neghalf"], writes=["junk"])
        S.dma_group([
            (lambda E: E.dma_start(out=cols[:], in_=d_cols[:, :]), [], ["cols"]),
            (lambda E: E.dma_start(out=gbc[:], in_=d_gb[:, :]), [], ["gbc"]),
        ], "cst")

        win_keys = {cg: [f"win{cg}"] for cg in range(5)}
        wout_keys = ["wout"]
        pm_keys = ["pm"]

        def load_win(cg):
            items = []
            for kp in range(0, KD, 2):
                src = w_in[kp * 128:(kp + 2) * 128, cg * 512:(cg + 1) * 512].rearrange("(k p) c -> p k c", p=128)
                dst = w_in_bf[:, kp:kp + 2, cg * 512:(cg + 1) * 512]
                items.append((lambda E, dst=dst, src=src: E.dma_start(out=dst, in_=src), [], [f"win{cg}"]))
            S.dma_group(items, f"w{cg}", eng="pool")

        def load_win_half(cg, hf):
            items = []
            c0_ = cg * 512 + hf * 256
            for kp in range(0, KD, 4):
                src = w_in[kp * 128:(kp + 4) * 128, c0_:c0_ + 256].rearrange("(k p) c -> p k c", p=128)
                dst = w_in_bf[:, kp:kp + 4, c0_:c0_ + 256]
                items.append((lambda E, dst=dst, src=src: E.dma_start(out=dst, in_=src), [], [f"win{cg}h{hf}"]))
            S.dma_group(items, f"w{cg}h{hf}", eng="pool")

        def load_wout():
            items = []
            for kp in range(0, KD, 2):
                src = w_out[kp * 128:(kp + 2) * 128, :].rearrange("(k p) c -> p k c", p=128)
                dst = w_out_bf[:, kp:kp + 2, :]
                items.append((lambda E, dst=dst, src=src: E.dma_start(out=dst, in_=src), [], ["wout"]))
            S.dma_group(items, "wo", eng="pool")
            S.dma_group([(lambda E: E.dma_start(out=fgb[:], in_=d_fg[:, :]), [], ["fgb"])], "cst3")

        def load_ident():
            S.dma_group([(lambda E: E.dma_start(out=ident_bf[:], in_=d_ident[:, :]), [], ["ident"])], "wid", eng="pool")

        def load_pm():
            items = []
            for j in range(0, NPM, 5):
                n = min(5, NPM - j)
                dst = pm_bf[:, j:j + n, :].rearrange("p m t -> p (m t)")
                src = d_pm[:, j * PMW:(j + n) * PMW]
                items.append((lambda E, dst=dst, src=src: E.dma_start(out=dst, in_=src), [], ["pm"]))
            S.dma_group(items, "wpm", eng="pool")

        def load_small():
            items = [
                (lambda E: E.dma_start(out=wst_bf[:].rearrange("p h q -> p (h q)"), in_=d_wst[:, :]), [], ["wst"]),
                (lambda E: E.dma_start(out=lnb_bf[:], in_=d_lnb[:, :]), [], ["lnb"]),
                (lambda E: E.dma_start(out=poolw_bf[:].rearrange("p g d -> p (g d)"), in_=d_poolw[:, :]), [], ["poolw"]),
            ]
            S.dma_group(items, "wsm", eng="pool")
            S.dma_group([
                (lambda E: E.dma_start(out=t1[0][:], in_=d_bs[:, :]), [], ["t1_0"]),
            ], "cst2")

        def compute_cmat():
            fb = nxt("fm", NFM)

            def cm_mm(E):
                ins = None
                for h in range(4):
                    ins = E.matmul(fmb[fb][:, h * 128:(h + 1) * 128], lhsT=lnb_bf[:, h * 128:(h + 1) * 128],
                                   rhs=wst_bf[:, h, :], start=True, stop=True)
                return ins
            S.op("pe", cm_mm, reads=["lnb", "wst"], writes=[f"fmb{fb}"])
            S.op("dve", lambda E: E.tensor_tensor(out=cmat[:].rearrange("p h q -> p (h q)"), in0=fmb[fb][:],
                                                  in1=t1[0][:], op=ALU.add),
                 reads=[f"fmb{fb}", "t1_0"], writes=["cmat"])

        slot_occ = {}
        x_done = {}

        xslot = {}
        loaded = [0]

        def ensure_loaded(upto):
            while loaded[0] <= min(upto, NCH - 1):
                c = loaded[0]
                sl = ring["xs"]
                occ = slot_occ.get(sl)
                if occ is not None and not x_done.get(occ, False):
                    break
                nxt("xs", nx)
                xslot[c] = sl
                slot_occ[sl] = c
                thr = []
                if 4 <= c < 6:
                    thr = win_keys[0]
                elif 6 <= c < 9:
                    thr = win_keys[3]
                elif 9 <= c < 13:
                    thr = ["win2h1"]
                S.op("sp", lambda E: E.dma_start(out=xs[sl][:], in_=xin[c * 128:(c + 1) * 128, :]),
                     reads=thr, writes=[(f"xs{sl}", (c, "x"))], dsem=f"ld{sl}")
                loaded[0] += 1

        def scol(c, j):
            return stat[:, c * 16 + j:c * 16 + j + 1]

        def rstd_ops(c, jin, jtmp, jout, key_in, key_out):
            S.op("pool", lambda E: E.tensor_scalar(out=scol(c, jtmp), in0=scol(c, jin), scalar1=1.0, scalar2=EPS,
                                                   op0=ALU.mult, op1=ALU.add), reads=[key_in], writes=[f"tmp{c}_{jtmp}"])
            S.op("pool", lambda E: E.tensor_tensor(out=scol(c, jout), in0=scol(c, jtmp), in1=neghalf[:],
                                                   op=ALU.pow), reads=[f"tmp{c}_{jtmp}", "neghalf"], writes=[key_out])

        def hT_view(c, k=None):
            ch = chunks[c]
            if ch["kind"] == "real":
                t = hT[ch["st"] % 2]
                lo = ch["pos"] * 128
                key = f"hT{ch['st'] % 2}_{ch['pos']}"
            else:
                t = hTh
                lo = 0
                key = "hTh"
            if k is None:
                return t[:, :, lo:lo + 128], key
            return t[:, k, lo:lo + 128], key

        fe = dict(a1=0, a2=0, b=0)

        def fe_a1(c):
            ensure_loaded(c + LA)
            assert c < loaded[0], f"x ring too small: chunk {c} not loadable"
            sl = xslot[c]
            S.op("act", lambda E: E.activation(out=junk[:], in_=xs[sl][:], func=AF.Square, scale=1.0 / 32.0,
                                               accum_out=scol(c, 0)),
                 reads=[(f"xs{sl}", (c, "x"))], writes=[f"ms{c}", "junk"])
            rstd_ops(c, 0, 1, 2, f"ms{c}", f"rstd{c}")

        def fe_a2(c):
            sl = xslot[c]
            hs = c % 2
            S.op("dve", lambda E: E.scalar_tensor_tensor(out=htok[hs][:], in0=xs[sl][:], scalar=scol(c, 2),
                                                         in1=gbc[:], op0=ALU.mult, op1=ALU.mult),
                 reads=[(f"xs{sl}", (c, "x")), f"rstd{c}", "gbc"], writes=[(f"htok{hs}", c)])
            if chunks[c]["kind"] != "real":
                x_done[c] = True

        tp_alt = fmb[NFM - 1][:].bitcast(BF16)

        def fe_b(c):
            hs = c % 2
            alt = (c < 6 and c % 2 == 1)
            bank = tp_alt if alt else tp[:]
            bkey = f"fmb{NFM - 1}" if alt else "tp"

            def tr(E):
                ins = None
                for k in range(KD):
                    ins = E.transpose(bank[:, k * 128:(k + 1) * 128], htok[hs][:, k * 128:(k + 1) * 128], ident_bf[:])
                return ins
            S.op("pe", tr, reads=[(f"htok{hs}", c), "ident"], writes=[(bkey, ("T", c))])
            dst, hk = hT_view(c)
            S.op("act", lambda E: E.copy(out=dst, in_=bank.rearrange("p (k t) -> p k t", k=KD)),
                 reads=[(bkey, ("T", c))], writes=[(hk, c)])

        def fe_can(c):
            ensure_loaded(c + LA)
            return c < loaded[0]

        def fe_tick():
            cb = fe["b"]
            if cb >= NCH:
                return
            while fe["a1"] <= cb:
                fe_a1(fe["a1"]); fe["a1"] += 1
            while fe["a2"] <= cb:
                fe_a2(fe["a2"]); fe["a2"] += 1
            fe_b(cb)
            fe["b"] += 1
            if fe["a2"] == cb + 1 and cb + 1 < NCH and (fe["a1"] > cb + 1 or fe_can(cb + 1)):
                if fe["a1"] == cb + 1:
                    fe_a1(cb + 1); fe["a1"] += 1
                fe_a2(cb + 1); fe["a2"] += 1
            if fe["a1"] == cb + 2 and cb + 2 < NCH and fe_can(cb + 2):
                fe_a1(cb + 2); fe["a1"] += 1
            ensure_loaded(fe["a1"] + LA)

        def ensure_fe(upto):
            while fe["b"] <= min(upto, NCH - 1):
                fe_tick()

        def TM_a(c):
            ensure_fe(c)
            b = nxt("tok", NTOK)
            bk = f"tokb{b}"

            def mm(E):
                ins = None
                for k in range(KD):
                    lhsT, _ = hT_view(c, k)
                    ins = E.matmul(tokb[b][:], lhsT=lhsT, rhs=w_in_bf[:, k, 0:512],
                                   start=(k == 0), stop=(k == KD - 1))
                return ins
            S.op("pe", mm, reads=[(hT_view(c)[1], c)] + win_keys[0], writes=[bk])
            sa = c % NA
            S.op("act", lambda E: E.copy(out=a_tok[sa][:], in_=tokb[b][:]), reads=[bk], writes=[(f"a{sa}", c)])

        def TM_v(c):
            ensure_fe(c)
            b = nxt("tok", NTOK)
            bk = f"tokb{b}"

            def mm(E):
                ins = None
                for k in range(KD):
                    lhsT, _ = hT_view(c, k)
                    ins = E.matmul(tokb[b][:], lhsT=lhsT, rhs=w_in_bf[:, k, 3 * 512:4 * 512],
                                   start=(k == 0), stop=(k == KD - 1))
                return ins
            S.op("pe", mm, reads=[(hT_view(c)[1], c)] + win_keys[3], writes=[bk])
            S.op("dve", lambda E: E.bn_stats(out=stat[:, c * 16 + 4:c * 16 + 10], in_=tokb[b][:]),
                 reads=[bk], writes=[f"bst{c}"])
            S.op("dve", lambda E: E.bn_aggr(out=stat[:, c * 16 + 10:c * 16 + 12], in_=stat[:, c * 16 + 4:c * 16 + 10]),
                 reads=[f"bst{c}"], writes=[f"mv{c}"])
            rstd_ops(c, 11, 12, 13, f"mv{c}", f"rstdv{c}")

            def zfn():
                sz = chunks[c]["ridx"] % NZ
                S.op("dve", lambda E: E.tensor_scalar(out=z_tok[sz][:], in0=tokb[b][:], scalar1=scol(c, 10),
                                                      scalar2=scol(c, 13), op0=ALU.subtract, op1=ALU.mult),
                     reads=[bk, f"mv{c}", f"rstdv{c}"], writes=[(f"z{sz}", c)])
            return zfn

        def FM_block(g, kind, j):
            cg = dict(ga=1, u=2, gb=4)[kind]
            col0 = cg * 512 + j * 128
            slot = g % 2
            st = sts[g]
            b = nxt("fm", NFM)
            bk = f"fmb{b}"

            def mm(E):
                ins = None
                for k in range(KD):
                    ins = E.matmul(fmb[b][:], lhsT=w_in_bf[:, k, col0:col0 + 128], rhs=hT[slot][:, k, :],
                                   start=(k == 0), stop=(k == KD - 1))
                return ins
            wk = win_keys[cg] if cg == 1 else [f"win{cg}h{j // 2}"]
            S.op("pe", mm, reads=[(f"hT{slot}_{p}", st["c0"] + p) for p in range(4)] + wk, writes=[bk])
            if kind == "ga":
                S.op("act", lambda E: E.activation(out=sga[:, j, :], in_=fmb[b][:], func=AF.Silu),
                     reads=[bk], writes=[(f"sga{j}", g)])
            elif kind == "gb":
                s2 = 0
                S.op("act", lambda E: E.activation(out=sgb[s2][:], in_=fmb[b][:], func=AF.Silu),
                     reads=[bk], writes=[(f"sgb{s2}", (g, j))])
            else:
                s2 = 0
                S.op("dve", lambda E: E.tensor_tensor(out=usg[:, j, :], in0=fmb[b][:], in1=sgb[s2][:], op=ALU.mult),
                     reads=[bk, (f"sgb{s2}", (g, j))], writes=[(f"usg{j}", g)])

        def pm_idx(c, gq):
            ch = chunks[c]
            n = seg_chunks[ch["seg"]]
            if ch["e"] == 1:
                return 4 + ch["seg"] * 8 + gq
            if ch["e"] == n:
                return 4 + ch["seg"] * 8 + 4 + gq
            return gq

        def POOL(g, gq):
            st = sts[g]
            c0 = st["c0"]
            b = nxt("fm", NFM)
            bk = f"fmb{b}"

            def mm(E):
                E.matmul(fmb[b][:, 0:8], lhsT=a_tok[(c0 - 1) % NA][:, gq * 128:(gq + 1) * 128],
                         rhs=pm_bf[:, gq, 136:144], start=True, stop=False, skip_group_check=True)
                for j in range(4):
                    c = c0 + j
                    lo = 8 if j == 0 else 0
                    hi = 136 if j == 3 else 144
                    o0 = j * 128 - 8 + lo
                    E.matmul(fmb[b][:, o0:o0 + (hi - lo)], lhsT=a_tok[c % NA][:, gq * 128:(gq + 1) * 128],
                             rhs=pm_bf[:, pm_idx(c, gq), lo:hi], start=False, stop=False, skip_group_check=True)
                return E.matmul(fmb[b][:, 504:512], lhsT=a_tok[(c0 + 4) % NA][:, gq * 128:(gq + 1) * 128],
                                rhs=pm_bf[:, gq, 0:8], start=False, stop=True, skip_group_check=True)
            rd = [(f"a{(c0 + j) % NA}", c0 + j) for j in range(-1, 5)] + pm_keys
            S.op("pe", mm, reads=rd, writes=[bk])
            S.op("act", lambda E: E.copy(out=dT[:, gq, :], in_=fmb[b][:]), reads=[bk], writes=[(f"dT_{gq}", g)])

        def SGh(g, h):
            st = sts[g]
            ms = g % 2
            b = nxt("fm", NFM)
            bk = f"fmb{b}"

            def mm(E):
                ins = None
                for j in range(4):
                    c = st["c0"] + j
                    sz = chunks[c]["ridx"] % NZ
                    ins = E.matmul(fmb[b][:, j * 128:(j + 1) * 128], lhsT=z_tok[sz][:, h * 128:(h + 1) * 128],
                                   rhs=wst_bf[:, h, :], start=True, stop=True)
                return ins
            rd = [(f"z{chunks[st['c0'] + j]['ridx'] % NZ}", st["c0"] + j) for j in range(4)] + ["wst"]
            S.op("pe", mm, reads=rd, writes=[bk])
            ts = nxt("t1", 2)
            S.op("dve", lambda E: E.scalar_tensor_tensor(
                out=t1[ts][:].rearrange("p (j q) -> p j q", j=4),
                in0=fmb[b][:].rearrange("p (j q) -> p j q", j=4),
                scalar=cols[:, 12 + h:13 + h],
                in1=cmat[:, h, :].unsqueeze(1).to_broadcast([128, 4, 128]),
                op0=ALU.mult, op1=ALU.add), reads=[bk, "cols", "cmat"], writes=[(f"t1_{ts}", (g, h))])
            S.op("pool", lambda E: E.tensor_tensor(out=mixT[ms][:, 4 + h, :], in0=t1[ts][:], in1=usg[:, h, :],
                                                   op=ALU.mult),
                 reads=[(f"t1_{ts}", (g, h)), (f"usg{h}", g)], writes=[(f"mixT{ms}_{4 + h}", g)])

        def PWq(g, gq):
            ms = g % 2
            b = nxt("fm", NFM)
            bk = f"fmb{b}"
            S.op("pe", lambda E: E.matmul(fmb[b][:], lhsT=poolw_bf[:, gq, :], rhs=dT[:, gq, :],
                                          start=True, stop=True),
                 reads=[(f"dT_{gq}", g), "poolw"], writes=[bk])
            S.op("dve", lambda E: E.scalar_tensor_tensor(
                out=mixT[ms][:, gq, :], in0=fmb[b][:], scalar=cols[:, 8 + gq:9 + gq], in1=sga[:, gq, :],
                op0=ALU.mult, op1=ALU.mult), reads=[bk, "cols", (f"sga{gq}", g)], writes=[(f"mixT{ms}_{gq}", g)])

        def WO_main(g, j):
            st = sts[g]
            ms = g % 2
            c = st["c0"] + j
            sl = xslot[c]
            xk = f"xs{sl}"
            for hf in range(2):
                b = nxt("tok", NTOK)
                bk = f"tokb{b}"

                def mm(E):
                    ins = None
                    for k in range(KD):
                        ins = E.matmul(tokb[b][:], lhsT=mixT[ms][:, k, j * 128:(j + 1) * 128],
                                       rhs=w_out_bf[:, k, hf * 512:(hf + 1) * 512],
                                       start=(k == 0), stop=(k == KD - 1))
                    return ins
                S.op("pe", mm, reads=[(f"mixT{ms}_{k}", g) for k in range(KD)] + wout_keys, writes=[bk])
                S.op("dve", lambda E: E.tensor_tensor(out=xs[sl][:, hf * 512:(hf + 1) * 512], in0=tokb[b][:],
                                                      in1=xs[sl][:, hf * 512:(hf + 1) * 512], op=ALU.add),
                     reads=[bk, (xk, (c, "x" if hf == 0 else "xo0"))], writes=[(xk, (c, "xo0" if hf == 0 else "xo"))])
            S.op("act", lambda E: E.activation(out=junk[:], in_=xs[sl][:], func=AF.Square, scale=1.0 / 32.0,
                                               accum_out=scol(c, 3)),
                 reads=[(xk, (c, "xo"))], writes=[f"ms2_{c}", "junk"])
            rstd_ops(c, 3, 14, 15, f"ms2_{c}", f"rstd2_{c}")

            def tail():
                S.op("dve", lambda E: E.scalar_tensor_tensor(out=xs[sl][:], in0=xs[sl][:], scalar=scol(c, 15),
                                                             in1=fgb[:], op0=ALU.mult, op1=ALU.mult),
                     reads=[(xk, (c, "xo")), f"rstd2_{c}", "fgb"], writes=[(xk, (c, "y"))])
                r = chunks[c]["ridx"]
                S.op("sp", lambda E: E.dma_start(out=yout[r * 128:(r + 1) * 128, :], in_=xs[sl][:]),
                     reads=[(xk, (c, "y"))], dsem=f"st{sl}")
                x_done[c] = True
                ensure_loaded(fe["a1"] + LA)
            return tail

        load_ident()
        load_win(0)
        load_win(3)
        ensure_loaded(3)
        for c_ in range(3):
            fe_a1(c_)
        fe["a1"] = 3
        for c_ in range(2):
            fe_a2(c_)
        fe["a2"] = 2
        fe_tick()
        fe_tick()
        load_pm()
        load_win(1)
        late_loads = [lambda: (load_win_half(4, 0), load_win_half(2, 0)), load_small,
                      lambda: (load_win_half(4, 1), load_win_half(2, 1))]

        tma_done = [0]
        deferred = []
        for g, st in enumerate(sts):
            c0 = st["c0"]
            if g > 0:
                ensure_fe(c0 + 4)
            while tma_done[0] < c0 + 1:
                TM_a(tma_done[0])
                tma_done[0] += 1
            for j in range(4):
                if g == 0:
                    ensure_fe(c0 + j + 2)
                zfn = TM_v(c0 + j)
                TM_a(c0 + j + 1)
                tma_done[0] = c0 + j + 2
                zfn()
                if late_loads:
                    late_loads.pop(0)()
            if g > 0:
                tails = []
                for j in range(4):
                    tails.append(WO_main(g - 1, j))
                    if j >= 1:
                        tails[j - 1]()
                deferred.append(tails[3])
            nxt_c0 = sts[g + 1]["c0"] if g + 1 < NST else None
            tick_target = (nxt_c0 + 3) if nxt_c0 is not None else NCH - 1
            n_ticks = max(0, tick_target - fe["b"] + 1)
            if n_ticks <= 3:
                tick_pos = {("P", 1), ("P", 3), ("SG", 0)}
            else:
                tick_pos = {("P", 0), ("P", 1), ("P", 2), ("P", 3), ("PW", 0), ("SG", 0), ("SG", 1)}
            base = [("ga", 0), ("P", 0), ("ga", 1), ("P", 1), ("ga", 2), ("P", 2), ("ga", 3), ("P", 3),
                    ("gb", 0), ("PW", 0), ("u", 0), ("PW", 1), ("gb", 1), ("SG", 0), ("u", 1), ("PW", 2),
                    ("gb", 2), ("SG", 1), ("u", 2), ("PW", 3), ("gb", 3), ("SG", 2), ("u", 3), ("tick4",),
                    ("SG", 3)]
            if g == NST - 1:
                base = [("gb", 0), ("P", 0), ("u", 0), ("P", 1), ("gb", 1), ("SG", 0), ("u", 1), ("P", 2),
                        ("gb", 2), ("SG", 1), ("u", 2), ("P", 3), ("gb", 3), ("SG", 2), ("u", 3), ("SG", 3),
                        ("ga", 0), ("PW", 0), ("ga", 1), ("PW", 1), ("ga", 2), ("PW", 2), ("ga", 3), ("PW", 3)]
            seq = []
            for it in base:
                seq.append(it)
                if it in tick_pos:
                    seq.append(("tick",))
            bi = 0
            for it in seq:
                if it[0] in ("ga", "gb", "u"):
                    FM_block(g, it[0], it[1])
                    if deferred:
                        deferred.pop(0)()
                    if g == 0 and bi == 0:
                        load_wout()
                    bi += 1
                elif it[0] == "P":
                    POOL(g, it[1])
                elif it[0] == "SG":
                    if g == 0 and it[1] == 0:
                        compute_cmat()
                    SGh(g, it[1])
                elif it[0] == "PW":
                    PWq(g, it[1])
                elif it[0] == "tick":
                    if nxt_c0 is not None and fe["b"] <= nxt_c0 + 3:
                        fe_tick()
                    elif nxt_c0 is None and fe["b"] < NCH:
                        fe_tick()
                elif it[0] == "tick4":
                    if nxt_c0 is not None:
                        ensure_fe(nxt_c0 + 3)
                        ensure_fe(nxt_c0 + 4)
        tails = []
        for j in range(4):
            tails.append(WO_main(NST - 1, j))
            if j >= 1:
                tails[j - 1]()
        tails[3]()
        assert not deferred
        S.final_wait("sp")
        print(f"[build] nwaits={S.nwaits} counts={ {k: v for k, v in S.cnt.items() if k in ('pe', 'act', 'dve', 'pool')} }")
    return nc, dict(NCH=NCH, nreal=nreal, chunks=chunks)


def _pool_mats(first_is_start, last_is_end):
    out = {}
    for gq, w in enumerate(WINDOWS):
        h = w // 2
        s = np.arange(128)[:, None]
        t = np.arange(128)[None, :]
        inwin = ((s >= t - h) & (s < t + h)).astype(np.float64)
        eye = (s == t).astype(np.float64)
        out[("cur", gq)] = (inwin / w - eye).astype(np.float32)
        out[("prev", gq)] = ((s - 128 >= t - h).astype(np.float64) / w).astype(np.float32)
        out[("next", gq)] = ((s + 128 < t + h).astype(np.float64) / w).astype(np.float32)
        cnt_start = np.minimum(w, t + h).astype(np.float64)
        out[("cur_start", gq)] = (inwin / cnt_start - eye).astype(np.float32)
        cnt_end = np.minimum(w, 128 - t + h).astype(np.float64)
        out[("cur_end", gq)] = (inwin / cnt_end - eye).astype(np.float32)
    return out


def _pm_for_core(seg_start_end):
    m = _pool_mats(None, None)

    def wide(cur_key, gq):
        return np.concatenate([m[("next", gq)][:, 120:128], m[(cur_key, gq)], m[("prev", gq)][:, 0:8]], axis=1)
    mats = [wide("cur", gq) for gq in range(4)]
    for (fs, le) in seg_start_end:
        mats += [wide("cur_start" if fs else "cur", gq) for gq in range(4)]
        mats += [wide("cur_end" if le else "cur", gq) for gq in range(4)]
    arr = np.stack(mats, axis=0)
    assert arr.shape == (NPM, 128, PMW)
    return np.ascontiguousarray(arr.transpose(1, 0, 2).reshape(128, -1))


def _ext_segment(x_seq, lo, hi):
    S_ = x_seq.shape[0]
    out = np.zeros((hi - lo + 256, x_seq.shape[1]), np.float32)
    a = max(lo - 128, 0)
    b = min(hi + 128, S_)
    out[a - (lo - 128):b - (lo - 128)] = x_seq[a:b]
    return out


def make_in_maps(x_prompt, x_sample, norm_g, w_in, pool_w, pool_scale, sgu_ln_g, sgu_ln_b,
                 w_spatial, b_spatial, w_out, final_g, ncores=NCORES, seg_chunks=SEG_CHUNKS):
    f = np.float32
    w_in = np.ascontiguousarray(w_in, f)
    w_out = np.ascontiguousarray(w_out, f)
    cols = np.concatenate([np.asarray(norm_g, f).reshape(8, 128).T, np.asarray(pool_scale, f).reshape(4, 128).T,
                           np.asarray(sgu_ln_g, f).reshape(4, 128).T], axis=1)
    cols = np.ascontiguousarray(cols)
    fgb = np.ascontiguousarray(np.broadcast_to(np.asarray(final_g, f)[None, :], (128, D)))
    gbc = np.ascontiguousarray(np.broadcast_to(np.asarray(norm_g, f)[None, :], (128, D)))
    lnbb = np.ascontiguousarray(np.broadcast_to(np.asarray(sgu_ln_b, f)[None, :], (128, 512)))
    bsb = np.ascontiguousarray(np.broadcast_to(np.asarray(b_spatial, f).reshape(1, 512), (128, 512)))
    poolw = np.ascontiguousarray(np.asarray(pool_w, f).transpose(1, 0, 2).reshape(128, 512))
    wst = np.ascontiguousarray(np.asarray(w_spatial, f).transpose(2, 0, 1).reshape(128, 512))
    ident = np.eye(128, dtype=f)
    n0, n1 = seg_chunks
    L0, L1 = n0 * 128, n1 * 128
    halves = x_prompt.shape[1] // L0
    in_maps = []
    for c in range(ncores):
        b, hf = c // halves, c % halves
        seg0 = _ext_segment(np.asarray(x_prompt[b], f), hf * L0, (hf + 1) * L0)
        seg1 = _ext_segment(np.asarray(x_sample[c], f), 0, L1)
        xin = np.concatenate([seg0, seg1], axis=0)
        pm = _pm_for_core([(hf == 0, hf == halves - 1), (True, True)])
        in_maps.append(dict(xin=xin, w_in=w_in, w_out=w_out, pm=pm, poolw=poolw, wst=wst, ident=ident,
                            cols=cols, fgb=fgb, gbc=gbc, lnbb=lnbb, bsb=bsb))
    return in_maps


_NC_CACHE = {}


def kernel(x_prompt, x_sample, norm_g, w_in, pool_w, pool_scale, sgu_ln_g, sgu_ln_b,
           w_spatial, b_spatial, w_out, final_g):
    x_prompt = np.asarray(x_prompt)
    x_sample = np.asarray(x_sample)
    in_maps = make_in_maps(x_prompt, x_sample, norm_g, w_in, pool_w, pool_scale, sgu_ln_g, sgu_ln_b,
                           w_spatial, b_spatial, w_out, final_g)
    if "nc" not in _NC_CACHE:
        _NC_CACHE["nc"] = build_nc()[0]
    nc = _NC_CACHE["nc"]
    res = run_bass_kernel_spmd(nc, in_maps, core_ids=list(range(NCORES)))
    n0, n1 = SEG_CHUNKS
    L0, L1 = n0 * 128, n1 * 128
    y_prompt = np.empty(x_prompt.shape, np.float32)
    y_sample = np.empty(x_sample.shape, np.float32)
    halves = x_prompt.shape[1] // L0
    for c in range(NCORES):
        y = res.results[c]["yout"]
        b, hf = c // halves, c % halves
        y_prompt[b, hf * L0:(hf + 1) * L0] = y[:L0]
        y_sample[c] = y[L0:L0 + L1]
    return (y_prompt, y_sample)
```

```python
import numpy as np
from contextlib import ExitStack
import concourse.bass as bass
import concourse.mybir as mybir
from concourse.bass_utils import run_bass_kernel_spmd

F32 = mybir.dt.float32
BF16 = mybir.dt.bfloat16
AF = mybir.ActivationFunctionType
ALU = mybir.AluOpType

D = 1024
KD = 8
INW = 2560
EPS = 1e-6
WINDOWS = (2, 4, 8, 16)
NCORES = 8
SEG_CHUNKS = (32, 16)
LA = 2
NX = 13
NPM = 20
PMW = 144


class Sched:
    def __init__(self, nc, es):
        self.nc = nc
        self.es = es
        self.eng = dict(pe=nc.tensor, act=nc.scalar, dve=nc.vector, pool=nc.gpsimd, sp=nc.sync)
        self.sems = {}
        self.cnt = {}
        for e in ("pe", "act", "dve", "pool"):
            self.new_sem(e)
        self.waited = {e: {} for e in self.eng}
        self.lastw = {}
        self.readers = {}
        self.tags = {}
        self.nwaits = 0

    def new_sem(self, name):
        self.sems[name] = self.es.enter_context(self.nc.semaphore(name))
        self.cnt[name] = 0

    def _norm(self, reads, writes):
        rk, wk = [], []
        for it in reads:
            if isinstance(it, tuple):
                assert self.tags.get(it[0]) == it[1], f"stale read {it} has {self.tags.get(it[0])}"
                rk.append(it[0])
            else:
                rk.append(it)
        for it in writes:
            if isinstance(it, tuple):
                self.tags[it[0]] = it[1]
                wk.append(it[0])
            else:
                wk.append(it)
        return rk, wk

    def _waits(self, eng, reads, writes):
        need = {}

        def add(t):
            if t is None:
                return
            s, v = t
            if eng == "pe" and s == "pe":
                return
            if need.get(s, 0) < v:
                need[s] = v

        for k in reads:
            add(self.lastw.get(k))
        for k in writes:
            add(self.lastw.get(k))
            for s, v in self.readers.get(k, {}).items():
                add((s, v))
        E = self.eng[eng]
        w = self.waited[eng]
        for s, v in need.items():
            if w.get(s, 0) < v:
                E.wait_ge(self.sems[s], v)
                w[s] = v
                self.nwaits += 1
        return E

    def _record(self, t, reads, writes):
        for k in reads:
            r = self.readers.setdefault(k, {})
            if r.get(t[0], 0) < t[1]:
                r[t[0]] = t[1]
        for k in writes:
            self.lastw[k] = t
            self.readers[k] = {}

    def op(self, eng, fn, reads=(), writes=(), dsem=None):
        reads, writes = self._norm(reads, writes)
        E = self._waits(eng, reads, writes)
        ins = fn(E)
        if dsem is not None:
            if dsem not in self.sems:
                self.new_sem(dsem)
            sname, inc = dsem, 16
        else:
            sname, inc = eng, 1
        self.cnt[sname] += inc
        ins.then_inc(self.sems[sname], inc)
        t = (sname, self.cnt[sname])
        self._record(t, reads, writes)
        return t

    def dma_group(self, items, dsem, eng="sp"):
        if dsem not in self.sems:
            self.new_sem(dsem)
        for fn, reads, writes in items:
            E = self._waits(eng, reads, writes)
            ins = fn(E)
            self.cnt[dsem] += 16
            ins.then_inc(self.sems[dsem], 16)
        t = (dsem, self.cnt[dsem])
        for fn, reads, writes in items:
            self._record(t, reads, writes)
        return t

    def final_wait(self, eng):
        E = self.eng[eng]
        for s, v in self.cnt.items():
            if v > 0:
                E.wait_ge(self.sems[s], v)


def build_nc(seg_chunks=SEG_CHUNKS, nx=NX):
    nc = bass.Bass("TRN2", target_bir_lowering=False)
    chunks = []
    seg_first = []
    nreal = 0
    for s, n in enumerate(seg_chunks):
        assert n % 4 == 0
        seg_first.append(len(chunks))
        for e in range(n + 2):
            kind = "pre" if e == 0 else ("post" if e == n + 1 else "real")
            chunks.append(dict(seg=s, kind=kind, e=e, ridx=None))
            if kind == "real":
                chunks[-1]["ridx"] = nreal
                nreal += 1
    NCH = len(chunks)
    sts = []
    for s, n in enumerate(seg_chunks):
        for i in range(n // 4):
            sts.append(dict(seg=s, c0=seg_first[s] + 1 + 4 * i))
    NST = len(sts)
    for g, st in enumerate(sts):
        for j in range(4):
            chunks[st["c0"] + j]["st"] = g
            chunks[st["c0"] + j]["pos"] = j

    xin = nc.dram_tensor("xin", [NCH * 128, D], F32, kind="ExternalInput").ap()
    w_in = nc.dram_tensor("w_in", [D, INW], F32, kind="ExternalInput").ap()
    w_out = nc.dram_tensor("w_out", [D, D], F32, kind="ExternalInput").ap()
    d_pm = nc.dram_tensor("pm", [128, NPM * PMW], F32, kind="ExternalInput").ap()
    d_poolw = nc.dram_tensor("poolw", [128, 512], F32, kind="ExternalInput").ap()
    d_wst = nc.dram_tensor("wst", [128, 512], F32, kind="ExternalInput").ap()
    d_ident = nc.dram_tensor("ident", [128, 128], F32, kind="ExternalInput").ap()
    d_cols = nc.dram_tensor("cols", [128, 16], F32, kind="ExternalInput").ap()
    d_fg = nc.dram_tensor("fgb", [128, D], F32, kind="ExternalInput").ap()
    d_gb = nc.dram_tensor("gbc", [128, D], F32, kind="ExternalInput").ap()
    d_lnb = nc.dram_tensor("lnbb", [128, 512], F32, kind="ExternalInput").ap()
    d_bs = nc.dram_tensor("bsb", [128, 512], F32, kind="ExternalInput").ap()
    yout = nc.dram_tensor("yout", [nreal * 128, D], F32, kind="ExternalOutput").ap()

    es = ExitStack()
    with es:
        S = Sched(nc, es)

        def sb(name, shape, dt):
            return es.enter_context(nc.sbuf_tensor("s_" + name, shape, dt))

        xs = [sb(f"xs{i}", [128, D], F32) for i in range(nx)]
        w_in_bf = sb("w_in_bf", [128, KD, INW], BF16)
        w_out_bf = sb("w_out_bf", [128, KD, D], BF16)
        pm_bf = sb("pm_bf", [128, NPM, PMW], BF16)
        poolw_bf = sb("poolw_bf", [128, 4, 128], BF16)
        wst_bf = sb("wst_bf", [128, 4, 128], BF16)
        ident_bf = sb("ident_bf", [128, 128], BF16)
        lnb_bf = sb("lnb_bf", [128, 512], BF16)
        cols = sb("cols_sb", [128, 16], F32)
        fgb = sb("fgb_sb", [128, D], F32)
        gbc = sb("gbc_sb", [128, D], F32)
        cmat = sb("cmat", [128, 4, 128], F32)
        neghalf = sb("neghalf", [128, 1], F32)
        stat = sb("stat", [128, NCH * 16], F32)
        junk = sb("junk", [128, D], BF16)
        htok = [sb(f"htok{i}", [128, D], BF16) for i in range(2)]
        hT = [sb(f"hT{i}", [128, KD, 512], BF16) for i in range(2)]
        hTh = sb("hTh", [128, KD, 128], BF16)
        NA = 6
        a_tok = [sb(f"a_tok{i}", [128, 512], BF16) for i in range(NA)]
        NZ = 4
        z_tok = [sb(f"z_tok{i}", [128, 512], BF16) for i in range(NZ)]
        sga = sb("sga", [128, 4, 512], F32)
        sgb = [sb(f"sgb{i}", [128, 512], F32) for i in range(1)]
        usg = sb("usg", [128, 4, 512], F32)
        t1 = [sb(f"t1_{i}", [128, 512], F32) for i in range(2)]
        dT = sb("dT", [128, 4, 512], BF16)
        mixT = [sb(f"mixT{i}", [128, KD, 512], BF16) for i in range(2)]

        tp = es.enter_context(nc.psum_tensor("tp", [128, D], BF16))
        NTOK, NFM = 4, 3
        tokb = [es.enter_context(nc.psum_tensor(f"tokb{i}", [128, 512], F32)) for i in range(NTOK)]
        fmb = [es.enter_context(nc.psum_tensor(f"fmb{i}", [128, 512], F32)) for i in range(NFM)]
        ring = dict(tok=0, fm=0, xs=0, t1=0)

        def nxt(name, n):
            v = ring[name]
            ring[name] = (v + 1) % n
            return v

        S.op("pool", lambda E: E.memset(neghalf[:], -0.5), writes=["neghalf"])
        S.op("act", lambda E: E.activation(out=junk[:, 0:1], in_=neghalf[:], func=AF.Silu),
             reads=["neghalf"], writes=["junk"])
        S.dma_group([
            (lambda E: E.dma_start(out=cols[:], in_=d_cols[:, :]), [], ["cols"]),
            (lambda E: E.dma_start(out=gbc[:], in_=d_gb[:, :]), [], ["gbc"]),
        ], "cst")

        win_keys = {cg: [f"win{cg}"] for cg in range(5)}
        wout_keys = ["wout"]
        pm_keys = ["pm"]

        def load_win(cg):
            items = []
            for kp in range(0, KD, 2):
                src = w_in[kp * 128:(kp + 2) * 128, cg * 512:(cg + 1) * 512].rearrange("(k p) c -> p k c", p=128)
                dst = w_in_bf[:, kp:kp + 2, cg * 512:(cg + 1) * 512]
                items.append((lambda E, dst=dst, src=src: E.dma_start(out=dst, in_=src), [], [f"win{cg}"]))
            S.dma_group(items, f"w{cg}", eng="pool")

        def load_win_half(cg, hf):
            items = []
            c0_ = cg * 512 + hf * 256
            for kp in range(0, KD, 4):
                src = w_in[kp * 128:(kp + 4) * 128, c0_:c0_ + 256].rearrange("(k p) c -> p k c", p=128)
                dst = w_in_bf[:, kp:kp + 4, c0_:c0_ + 256]
                items.append((lambda E, dst=dst, src=src: E.dma_start(out=dst, in_=src), [], [f"win{cg}h{hf}"]))
            S.dma_group(items, f"w{cg}h{hf}", eng="pool")

        def load_wout():
            items = []
            for kp in range(0, KD, 2):
                src = w_out[kp * 128:(kp + 2) * 128, :].rearrange("(k p) c -> p k c", p=128)
                dst = w_out_bf[:, kp:kp + 2, :]
                items.append((lambda E, dst=dst, src=src: E.dma_start(out=dst, in_=src), [], ["wout"]))
            S.dma_group(items, "wo", eng="pool")
            S.dma_group([(lambda E: E.dma_start(out=fgb[:], in_=d_fg[:, :]), [], ["fgb"])], "cst3")

        def load_ident():
            S.dma_group([(lambda E: E.dma_start(out=ident_bf[:], in_=d_ident[:, :]), [], ["ident"])], "wid", eng="pool")

        def load_pm():
            items = []
            for j in range(0, NPM, 5):
                n = min(5, NPM - j)
                dst = pm_bf[:, j:j + n, :].rearrange("p m t -> p (m t)")
                src = d_pm[:, j * PMW:(j + n) * PMW]
                items.append((lambda E, dst=dst, src=src: E.dma_start(out=dst, in_=src), [], ["pm"]))
            S.dma_group(items, "wpm", eng="pool")

        def load_small():
            items = [
                (lambda E: E.dma_start(out=wst_bf[:].rearrange("p h q -> p (h q)"), in_=d_wst[:, :]), [], ["wst"]),
                (lambda E: E.dma_start(out=lnb_bf[:], in_=d_lnb[:, :]), [], ["lnb"]),
                (lambda E: E.dma_start(out=poolw_bf[:].rearrange("p g d -> p (g d)"), in_=d_poolw[:, :]), [], ["poolw"]),
            ]
            S.dma_group(items, "wsm", eng="pool")
            S.dma_group([
                (lambda E: E.dma_start(out=t1[0][:], in_=d_bs[:, :]), [], ["t1_0"]),
            ], "cst2")

        def compute_cmat():
            fb = nxt("fm", NFM)

            def cm_mm(E):
                ins = None
                for h in range(4):
                    ins = E.matmul(fmb[fb][:, h * 128:(h + 1) * 128], lhsT=lnb_bf[:, h * 128:(h + 1) * 128],
                                   rhs=wst_bf[:, h, :], start=True, stop=True)
                return ins
            S.op("pe", cm_mm, reads=["lnb", "wst"], writes=[f"fmb{fb}"])
            S.op("dve", lambda E: E.tensor_tensor(out=cmat[:].rearrange("p h q -> p (h q)"), in0=fmb[fb][:],
                                                  in1=t1[0][:], op=ALU.add),
                 reads=[f"fmb{fb}", "t1_0"], writes=["cmat"])

        slot_occ = {}
        x_done = {}

        xslot = {}
        loaded = [0]

        def ensure_loaded(upto):
            while loaded[0] <= min(upto, NCH - 1):
                c = loaded[0]
                sl = ring["xs"]
                occ = slot_occ.get(sl)
                if occ is not None and not x_done.get(occ, False):
                    break
                nxt("xs", nx)
                xslot[c] = sl
                slot_occ[sl] = c
                thr = []
                if 2 <= c < 4:
                    thr = win_keys[0]
                elif 4 <= c < 6:
                    thr = win_keys[3]
                elif 6 <= c < 9:
                    thr = win_keys[1]
                elif 9 <= c < 13:
                    thr = ["win2h1"]
                S.op("sp", lambda E: E.dma_start(out=xs[sl][:], in_=xin[c * 128:(c + 1) * 128, :]),
                     reads=thr, writes=[(f"xs{sl}", (c, "x"))], dsem=f"ld{sl}")
                loaded[0] += 1

        def scol(c, j):
            return stat[:, c * 16 + j:c * 16 + j + 1]

        def rstd_ops(c, jin, jtmp, jout, key_in, key_out):
            S.op("pool", lambda E: E.tensor_scalar(out=scol(c, jtmp), in0=scol(c, jin), scalar1=1.0, scalar2=EPS,
                                                   op0=ALU.mult, op1=ALU.add), reads=[key_in], writes=[f"tmp{c}_{jtmp}"])
            S.op("pool", lambda E: E.tensor_tensor(out=scol(c, jout), in0=scol(c, jtmp), in1=neghalf[:],
                                                   op=ALU.pow), reads=[f"tmp{c}_{jtmp}", "neghalf"], writes=[key_out])

        def hT_view(c, k=None):
            ch = chunks[c]
            if ch["kind"] == "real":
                t = hT[ch["st"] % 2]
                lo = ch["pos"] * 128
                key = f"hT{ch['st'] % 2}_{ch['pos']}"
            else:
                t = hTh
                lo = 0
                key = "hTh"
            if k is None:
                return t[:, :, lo:lo + 128], key
            return t[:, k, lo:lo + 128], key

        fe = dict(a1=0, a2=0, b=0)

        def fe_a1(c):
            ensure_loaded(c + LA)
            assert c < loaded[0], f"x ring too small: chunk {c} not loadable"
            sl = xslot[c]
            S.op("act", lambda E: E.activation(out=junk[:], in_=xs[sl][:], func=AF.Square, scale=1.0 / 32.0,
                                               accum_out=scol(c, 0)),
                 reads=[(f"xs{sl}", (c, "x"))], writes=[f"ms{c}", "junk"])
            rstd_ops(c, 0, 1, 2, f"ms{c}", f"rstd{c}")

        def fe_a2(c):
            sl = xslot[c]
            hs = c % 2
            S.op("dve", lambda E: E.scalar_tensor_tensor(out=htok[hs][:], in0=xs[sl][:], scalar=scol(c, 2),
                                                         in1=gbc[:], op0=ALU.mult, op1=ALU.mult),
                 reads=[(f"xs{sl}", (c, "x")), f"rstd{c}", "gbc"], writes=[(f"htok{hs}", c)])
            if chunks[c]["kind"] != "real":
                x_done[c] = True

        def fe_b(c):
            hs = c % 2

            def tr(E):
                ins = None
                for k in range(KD):
                    ins = E.transpose(tp[:, k * 128:(k + 1) * 128], htok[hs][:, k * 128:(k + 1) * 128], ident_bf[:])
                return ins
            S.op("pe", tr, reads=[(f"htok{hs}", c), "ident"], writes=[("tp", c)])
            dst, hk = hT_view(c)
            S.op("act", lambda E: E.copy(out=dst, in_=tp[:].rearrange("p (k t) -> p k t", k=KD)),
                 reads=[("tp", c)], writes=[(hk, c)])

        def fe_can(c):
            ensure_loaded(c + LA)
            return c < loaded[0]

        def fe_tick():
            cb = fe["b"]
            if cb >= NCH:
                return
            while fe["a1"] <= cb:
                fe_a1(fe["a1"]); fe["a1"] += 1
            while fe["a2"] <= cb:
                fe_a2(fe["a2"]); fe["a2"] += 1
            fe_b(cb)
            fe["b"] += 1
            if fe["a2"] == cb + 1 and cb + 1 < NCH and (fe["a1"] > cb + 1 or fe_can(cb + 1)):
                if fe["a1"] == cb + 1:
                    fe_a1(cb + 1); fe["a1"] += 1
                fe_a2(cb + 1); fe["a2"] += 1
            if fe["a1"] == cb + 2 and cb + 2 < NCH and fe_can(cb + 2):
                fe_a1(cb + 2); fe["a1"] += 1
            ensure_loaded(fe["a1"] + LA)

        def ensure_fe(upto):
            while fe["b"] <= min(upto, NCH - 1):
                fe_tick()

        def TM_a(c):
            ensure_fe(c)
            b = nxt("tok", NTOK)
            bk = f"tokb{b}"

            def mm(E):
                ins = None
                for k in range(KD):
                    lhsT, _ = hT_view(c, k)
                    ins = E.matmul(tokb[b][:], lhsT=lhsT, rhs=w_in_bf[:, k, 0:512],
                                   start=(k == 0), stop=(k == KD - 1))
                return ins
            S.op("pe", mm, reads=[(hT_view(c)[1], c)] + win_keys[0], writes=[bk])
            sa = c % NA
            S.op("act", lambda E: E.copy(out=a_tok[sa][:], in_=tokb[b][:]), reads=[bk], writes=[(f"a{sa}", c)])

        def TM_v(c):
            ensure_fe(c)
            b = nxt("tok", NTOK)
            bk = f"tokb{b}"

            def mm(E):
                ins = None
                for k in range(KD):
                    lhsT, _ = hT_view(c, k)
                    ins = E.matmul(tokb[b][:], lhsT=lhsT, rhs=w_in_bf[:, k, 3 * 512:4 * 512],
                                   start=(k == 0), stop=(k == KD - 1))
                return ins
            S.op("pe", mm, reads=[(hT_view(c)[1], c)] + win_keys[3], writes=[bk])
            S.op("dve", lambda E: E.bn_stats(out=stat[:, c * 16 + 4:c * 16 + 10], in_=tokb[b][:]),
                 reads=[bk], writes=[f"bst{c}"])
            S.op("dve", lambda E: E.bn_aggr(out=stat[:, c * 16 + 10:c * 16 + 12], in_=stat[:, c * 16 + 4:c * 16 + 10]),
                 reads=[f"bst{c}"], writes=[f"mv{c}"])
            rstd_ops(c, 11, 12, 13, f"mv{c}", f"rstdv{c}")

            def zfn():
                sz = chunks[c]["ridx"] % NZ
                S.op("dve", lambda E: E.tensor_scalar(out=z_tok[sz][:], in0=tokb[b][:], scalar1=scol(c, 10),
                                                      scalar2=scol(c, 13), op0=ALU.subtract, op1=ALU.mult),
                     reads=[bk, f"mv{c}", f"rstdv{c}"], writes=[(f"z{sz}", c)])
            return zfn

        def FM_block(g, kind, j):
            cg = dict(ga=1, u=2, gb=4)[kind]
            col0 = cg * 512 + j * 128
            slot = g % 2
            st = sts[g]
            b = nxt("fm", NFM)
            bk = f"fmb{b}"

            def mm(E):
                ins = None
                for k in range(KD):
                    ins = E.matmul(fmb[b][:], lhsT=w_in_bf[:, k, col0:col0 + 128], rhs=hT[slot][:, k, :],
                                   start=(k == 0), stop=(k == KD - 1))
                return ins
            wk = win_keys[cg] if cg == 1 else [f"win{cg}h{j // 2}"]
            S.op("pe", mm, reads=[(f"hT{slot}_{p}", st["c0"] + p) for p in range(4)] + wk, writes=[bk])
            if kind == "ga":
                S.op("act", lambda E: E.activation(out=sga[:, j, :], in_=fmb[b][:], func=AF.Silu),
                     reads=[bk], writes=[(f"sga{j}", g)])
            elif kind == "gb":
                s2 = 0
                S.op("act", lambda E: E.activation(out=sgb[s2][:], in_=fmb[b][:], func=AF.Silu),
                     reads=[bk], writes=[(f"sgb{s2}", (g, j))])
            else:
                s2 = 0
                S.op("dve", lambda E: E.tensor_tensor(out=usg[:, j, :], in0=fmb[b][:], in1=sgb[s2][:], op=ALU.mult),
                     reads=[bk, (f"sgb{s2}", (g, j))], writes=[(f"usg{j}", g)])

        def pm_idx(c, gq):
            ch = chunks[c]
            n = seg_chunks[ch["seg"]]
            if ch["e"] == 1:
                return 4 + ch["seg"] * 8 + gq
            if ch["e"] == n:
                return 4 + ch["seg"] * 8 + 4 + gq
            return gq

        def POOL(g, gq):
            st = sts[g]
            c0 = st["c0"]
            b = nxt("fm", NFM)
            bk = f"fmb{b}"

            def mm(E):
                E.matmul(fmb[b][:, 0:8], lhsT=a_tok[(c0 - 1) % NA][:, gq * 128:(gq + 1) * 128],
                         rhs=pm_bf[:, gq, 136:144], start=True, stop=False, skip_group_check=True)
                for j in range(4):
                    c = c0 + j
                    lo = 8 if j == 0 else 0
                    hi = 136 if j == 3 else 144
                    o0 = j * 128 - 8 + lo
                    E.matmul(fmb[b][:, o0:o0 + (hi - lo)], lhsT=a_tok[c % NA][:, gq * 128:(gq + 1) * 128],
                             rhs=pm_bf[:, pm_idx(c, gq), lo:hi], start=False, stop=False, skip_group_check=True)
                return E.matmul(fmb[b][:, 504:512], lhsT=a_tok[(c0 + 4) % NA][:, gq * 128:(gq + 1) * 128],
                                rhs=pm_bf[:, gq, 0:8], start=False, stop=True, skip_group_check=True)
            rd = [(f"a{(c0 + j) % NA}", c0 + j) for j in range(-1, 5)] + pm_keys
            S.op("pe", mm, reads=rd, writes=[bk])
            S.op("act", lambda E: E.copy(out=dT[:, gq, :], in_=fmb[b][:]), reads=[bk], writes=[(f"dT_{gq}", g)])

        def SGh(g, h):
            st = sts[g]
            ms = g % 2
            b = nxt("fm", NFM)
            bk = f"fmb{b}"

            def mm(E):
                ins = None
                for j in range(4):
                    c = st["c0"] + j
                    sz = chunks[c]["ridx"] % NZ
                    ins = E.matmul(fmb[b][:, j * 128:(j + 1) * 128], lhsT=z_tok[sz][:, h * 128:(h + 1) * 128],
                                   rhs=wst_bf[:, h, :], start=True, stop=True)
                return ins
            rd = [(f"z{chunks[st['c0'] + j]['ridx'] % NZ}", st["c0"] + j) for j in range(4)] + ["wst"]
            S.op("pe", mm, reads=rd, writes=[bk])
            ts = nxt("t1", 2)
            S.op("dve", lambda E: E.scalar_tensor_tensor(
                out=t1[ts][:].rearrange("p (j q) -> p j q", j=4),
                in0=fmb[b][:].rearrange("p (j q) -> p j q", j=4),
                scalar=cols[:, 12 + h:13 + h],
                in1=cmat[:, h, :].unsqueeze(1).to_broadcast([128, 4, 128]),
                op0=ALU.mult, op1=ALU.add), reads=[bk, "cols", "cmat"], writes=[(f"t1_{ts}", (g, h))])
            S.op("pool", lambda E: E.tensor_tensor(out=mixT[ms][:, 4 + h, :], in0=t1[ts][:], in1=usg[:, h, :],
                                                   op=ALU.mult),
                 reads=[(f"t1_{ts}", (g, h)), (f"usg{h}", g)], writes=[(f"mixT{ms}_{4 + h}", g)])

        def PWq(g, gq):
            ms = g % 2
            b = nxt("fm", NFM)
            bk = f"fmb{b}"
            S.op("pe", lambda E: E.matmul(fmb[b][:], lhsT=poolw_bf[:, gq, :], rhs=dT[:, gq, :],
                                          start=True, stop=True),
                 reads=[(f"dT_{gq}", g), "poolw"], writes=[bk])
            S.op("dve", lambda E: E.scalar_tensor_tensor(
                out=mixT[ms][:, gq, :], in0=fmb[b][:], scalar=cols[:, 8 + gq:9 + gq], in1=sga[:, gq, :],
                op0=ALU.mult, op1=ALU.mult), reads=[bk, "cols", (f"sga{gq}", g)], writes=[(f"mixT{ms}_{gq}", g)])

        def WO_main(g, j):
            st = sts[g]
            ms = g % 2
            c = st["c0"] + j
            sl = xslot[c]
            xk = f"xs{sl}"
            for hf in range(2):
                b = nxt("tok", NTOK)
                bk = f"tokb{b}"

                def mm(E):
                    ins = None
                    for k in range(KD):
                        ins = E.matmul(tokb[b][:], lhsT=mixT[ms][:, k, j * 128:(j + 1) * 128],
                                       rhs=w_out_bf[:, k, hf * 512:(hf + 1) * 512],
                                       start=(k == 0), stop=(k == KD - 1))
                    return ins
                S.op("pe", mm, reads=[(f"mixT{ms}_{k}", g) for k in range(KD)] + wout_keys, writes=[bk])
                S.op("dve", lambda E: E.tensor_tensor(out=xs[sl][:, hf * 512:(hf + 1) * 512], in0=tokb[b][:],
                                                      in1=xs[sl][:, hf * 512:(hf + 1) * 512], op=ALU.add),
                     reads=[bk, (xk, (c, "x" if hf == 0 else "xo0"))], writes=[(xk, (c, "xo0" if hf == 0 else "xo"))])
            S.op("act", lambda E: E.activation(out=junk[:], in_=xs[sl][:], func=AF.Square, scale=1.0 / 32.0,
                                               accum_out=scol(c, 3)),
                 reads=[(xk, (c, "xo"))], writes=[f"ms2_{c}", "junk"])
            rstd_ops(c, 3, 14, 15, f"ms2_{c}", f"rstd2_{c}")

            def tail():
                S.op("dve", lambda E: E.scalar_tensor_tensor(out=xs[sl][:], in0=xs[sl][:], scalar=scol(c, 15),
                                                             in1=fgb[:], op0=ALU.mult, op1=ALU.mult),
                     reads=[(xk, (c, "xo")), f"rstd2_{c}", "fgb"], writes=[(xk, (c, "y"))])
                r = chunks[c]["ridx"]
                S.op("sp", lambda E: E.dma_start(out=yout[r * 128:(r + 1) * 128, :], in_=xs[sl][:]),
                     reads=[(xk, (c, "y"))], dsem=f"st{sl}")
                x_done[c] = True
                ensure_loaded(fe["a1"] + LA)
            return tail

        load_ident()
        load_win(0)
        load_win(3)
        ensure_loaded(3)
        for c_ in range(3):
            fe_a1(c_)
        fe["a1"] = 3
        for c_ in range(2):
            fe_a2(c_)
        fe["a2"] = 2
        fe_tick()
        fe_tick()
        load_pm()
        load_win(1)
        late_loads = [lambda: (load_win_half(4, 0), load_win_half(2, 0)), load_small,
                      lambda: (load_win_half(4, 1), load_win_half(2, 1))]

        tma_done = [0]
        deferred = []
        for g, st in enumerate(sts):
            c0 = st["c0"]
            if g > 0:
                ensure_fe(c0 + 4)
            while tma_done[0] < c0 + 1:
                TM_a(tma_done[0])
                tma_done[0] += 1
            for j in range(4):
                if g == 0:
                    ensure_fe(c0 + j + 2)
                zfn = TM_v(c0 + j)
                TM_a(c0 + j + 1)
                tma_done[0] = c0 + j + 2
                zfn()
                if late_loads:
                    late_loads.pop(0)()
            if g > 0:
                tails = []
                for j in range(4):
                    tails.append(WO_main(g - 1, j))
                    if j >= 1:
                        tails[j - 1]()
                deferred.append(tails[3])
            nxt_c0 = sts[g + 1]["c0"] if g + 1 < NST else None
            tick_target = (nxt_c0 + 3) if nxt_c0 is not None else NCH - 1
            n_ticks = max(0, tick_target - fe["b"] + 1)
            if n_ticks <= 3:
                tick_pos = {("P", 1), ("P", 3), ("SG", 0)}
            else:
                tick_pos = {("P", 0), ("P", 1), ("P", 2), ("P", 3), ("PW", 0), ("SG", 0), ("SG", 1)}
            base = [("ga", 0), ("P", 0), ("ga", 1), ("P", 1), ("ga", 2), ("P", 2), ("ga", 3), ("P", 3),
                    ("gb", 0), ("PW", 0), ("u", 0), ("PW", 1), ("gb", 1), ("SG", 0), ("u", 1), ("PW", 2),
                    ("gb", 2), ("SG", 1), ("u", 2), ("PW", 3), ("gb", 3), ("SG", 2), ("u", 3), ("tick4",),
                    ("SG", 3)]
            if g == NST - 1:
                base = [("gb", 0), ("P", 0), ("u", 0), ("P", 1), ("gb", 1), ("SG", 0), ("u", 1), ("P", 2),
                        ("gb", 2), ("SG", 1), ("u", 2), ("P", 3), ("gb", 3), ("SG", 2), ("u", 3), ("SG", 3),
                        ("ga", 0), ("PW", 0), ("ga", 1), ("PW", 1), ("ga", 2), ("PW", 2), ("ga", 3), ("PW", 3)]
            seq = []
            for it in base:
                seq.append(it)
                if it in tick_pos:
                    seq.append(("tick",))
            bi = 0
            for it in seq:
                if it[0] in ("ga", "gb", "u"):
                    FM_block(g, it[0], it[1])
                    if deferred:
                        deferred.pop(0)()
                    if g == 0 and bi == 0:
                        load_wout()
                    bi += 1
                elif it[0] == "P":
                    POOL(g, it[1])
                elif it[0] == "SG":
                    if g == 0 and it[1] == 0:
                        compute_cmat()
                    SGh(g, it[1])
                elif it[0] == "PW":
                    PWq(g, it[1])
                elif it[0] == "tick":
                    if nxt_c0 is not None and fe["b"] <= nxt_c0 + 3:
                        fe_tick()
                    elif nxt_c0 is None and fe["b"] < NCH:
                        fe_tick()
                elif it[0] == "tick4":
                    if nxt_c0 is not None:
                        ensure_fe(nxt_c0 + 3)
                        ensure_fe(nxt_c0 + 4)
        tails = []
        for j in range(4):
            tails.append(WO_main(NST - 1, j))
            if j >= 1:
                tails[j - 1]()
        tails[3]()
        assert not deferred
        S.final_wait("sp")
        print(f"[build] nwaits={S.nwaits} counts={ {k: v for k, v in S.cnt.items() if k in ('pe', 'act', 'dve', 'pool')} }")
    return nc, dict(NCH=NCH, nreal=nreal, chunks=chunks)


def _pool_mats(first_is_start, last_is_end):
    out = {}
    for gq, w in enumerate(WINDOWS):
        h = w // 2
        s = np.arange(128)[:, None]
        t = np.arange(128)[None, :]
        inwin = ((s >= t - h) & (s < t + h)).astype(np.float64)
        eye = (s == t).astype(np.float64)
        out[("cur", gq)] = (inwin / w - eye).astype(np.float32)
        out[("prev", gq)] = ((s - 128 >= t - h).astype(np.float64) / w).astype(np.float32)
        out[("next", gq)] = ((s + 128 < t + h).astype(np.float64) / w).astype(np.float32)
        cnt_start = np.minimum(w, t + h).astype(np.float64)
        out[("cur_start", gq)] = (inwin / cnt_start - eye).astype(np.float32)
        cnt_end = np.minimum(w, 128 - t + h).astype(np.float64)
        out[("cur_end", gq)] = (inwin / cnt_end - eye).astype(np.float32)
    return out


def _pm_for_core(seg_start_end):
    m = _pool_mats(None, None)

    def wide(cur_key, gq):
        return np.concatenate([m[("next", gq)][:, 120:128], m[(cur_key, gq)], m[("prev", gq)][:, 0:8]], axis=1)
    mats = [wide("cur", gq) for gq in range(4)]
    for (fs, le) in seg_start_end:
        mats += [wide("cur_start" if fs else "cur", gq) for gq in range(4)]
        mats += [wide("cur_end" if le else "cur", gq) for gq in range(4)]
    arr = np.stack(mats, axis=0)
    assert arr.shape == (NPM, 128, PMW)
    return np.ascontiguousarray(arr.transpose(1, 0, 2).reshape(128, -1))


def _ext_segment(x_seq, lo, hi):
    S_ = x_seq.shape[0]
    out = np.zeros((hi - lo + 256, x_seq.shape[1]), np.float32)
    a = max(lo - 128, 0)
    b = min(hi + 128, S_)
    out[a - (lo - 128):b - (lo - 128)] = x_seq[a:b]
    return out


def make_in_maps(x_prompt, x_sample, norm_g, w_in, pool_w, pool_scale, sgu_ln_g, sgu_ln_b,
                 w_spatial, b_spatial, w_out, final_g, ncores=NCORES, seg_chunks=SEG_CHUNKS):
    f = np.float32
    w_in = np.ascontiguousarray(w_in, f)
    w_out = np.ascontiguousarray(w_out, f)
    cols = np.concatenate([np.asarray(norm_g, f).reshape(8, 128).T, np.asarray(pool_scale, f).reshape(4, 128).T,
                           np.asarray(sgu_ln_g, f).reshape(4, 128).T], axis=1)
    cols = np.ascontiguousarray(cols)
    fgb = np.ascontiguousarray(np.broadcast_to(np.asarray(final_g, f)[None, :], (128, D)))
    gbc = np.ascontiguousarray(np.broadcast_to(np.asarray(norm_g, f)[None, :], (128, D)))
    lnbb = np.ascontiguousarray(np.broadcast_to(np.asarray(sgu_ln_b, f)[None, :], (128, 512)))
    bsb = np.ascontiguousarray(np.broadcast_to(np.asarray(b_spatial, f).reshape(1, 512), (128, 512)))
    poolw = np.ascontiguousarray(np.asarray(pool_w, f).transpose(1, 0, 2).reshape(128, 512))
    wst = np.ascontiguousarray(np.asarray(w_spatial, f).transpose(2, 0, 1).reshape(128, 512))
    ident = np.eye(128, dtype=f)
    n0, n1 = seg_chunks
    L0, L1 = n0 * 128, n1 * 128
    halves = x_prompt.shape[1] // L0
    in_maps = []
    for c in range(ncores):
        b, hf = c // halves, c % halves
        seg0 = _ext_segment(np.asarray(x_prompt[b], f), hf * L0, (hf + 1) * L0)
        seg1 = _ext_segment(np.asarray(x_sample[c], f), 0, L1)
        xin = np.concatenate([seg0, seg1], axis=0)
        pm = _pm_for_core([(hf == 0, hf == halves - 1), (True, True)])
        in_maps.append(dict(xin=xin, w_in=w_in, w_out=w_out, pm=pm, poolw=poolw, wst=wst, ident=ident,
                            cols=cols, fgb=fgb, gbc=gbc, lnbb=lnbb, bsb=bsb))
    return in_maps


_NC_CACHE = {}


def kernel(x_prompt, x_sample, norm_g, w_in, pool_w, pool_scale, sgu_ln_g, sgu_ln_b,
           w_spatial, b_spatial, w_out, final_g):
    x_prompt = np.asarray(x_prompt)
    x_sample = np.asarray(x_sample)
    in_maps = make_in_maps(x_prompt, x_sample, norm_g, w_in, pool_w, pool_scale, sgu_ln_g, sgu_ln_b,
                           w_spatial, b_spatial, w_out, final_g)
    if "nc" not in _NC_CACHE:
        _NC_CACHE["nc"] = build_nc()[0]
    nc = _NC_CACHE["nc"]
    res = run_bass_kernel_spmd(nc, in_maps, core_ids=list(range(NCORES)))
    n0, n1 = SEG_CHUNKS
    L0, L1 = n0 * 128, n1 * 128
    y_prompt = np.empty(x_prompt.shape, np.float32)
    y_sample = np.empty(x_sample.shape, np.float32)
    halves = x_prompt.shape[1] // L0
    for c in range(NCORES):
        y = res.results[c]["yout"]
        b, hf = c // halves, c % halves
        y_prompt[b, hf * L0:(hf + 1) * L0] = y[:L0]
        y_sample[c] = y[L0:L0 + L1]
    return (y_prompt, y_sample)
```

```python
import numpy as np
from contextlib import ExitStack
import concourse.bass as bass
import concourse.mybir as mybir
from concourse.bass_utils import run_bass_kernel_spmd

F32 = mybir.dt.float32
BF16 = mybir.dt.bfloat16
AF = mybir.ActivationFunctionType
ALU = mybir.AluOpType

D = 1024
KD = 8
INW = 2560
EPS = 1e-6
WINDOWS = (2, 4, 8, 16)
NCORES = 8
SEG_CHUNKS = (32, 16)
LA = 2
NX = 13
NPM = 20
PMW = 144


class Sched:
    def __init__(self, nc, es):
        self.nc = nc
        self.es = es
        self.eng = dict(pe=nc.tensor, act=nc.scalar, dve=nc.vector, pool=nc.gpsimd, sp=nc.sync)
        self.sems = {}
        self.cnt = {}
        for e in ("pe", "act", "dve", "pool"):
            self.new_sem(e)
        self.waited = {e: {} for e in self.eng}
        self.lastw = {}
        self.readers = {}
        self.tags = {}
        self.nwaits = 0

    def new_sem(self, name):
        self.sems[name] = self.es.enter_context(self.nc.semaphore(name))
        self.cnt[name] = 0

    def _norm(self, reads, writes):
        rk, wk = [], []
        for it in reads:
            if isinstance(it, tuple):
                assert self.tags.get(it[0]) == it[1], f"stale read {it} has {self.tags.get(it[0])}"
                rk.append(it[0])
            else:
                rk.append(it)
        for it in writes:
            if isinstance(it, tuple):
                self.tags[it[0]] = it[1]
                wk.append(it[0])
            else:
                wk.append(it)
        return rk, wk

    def _waits(self, eng, reads, writes):
        need = {}

        def add(t):
            if t is None:
                return
            s, v = t
            if eng == "pe" and s == "pe":
                return
            if need.get(s, 0) < v:
                need[s] = v

        for k in reads:
            add(self.lastw.get(k))
        for k in writes:
            add(self.lastw.get(k))
            for s, v in self.readers.get(k, {}).items():
                add((s, v))
        E = self.eng[eng]
        w = self.waited[eng]
        for s, v in need.items():
            if w.get(s, 0) < v:
                E.wait_ge(self.sems[s], v)
                w[s] = v
                self.nwaits += 1
        return E

    def _record(self, t, reads, writes):
        for k in reads:
            r = self.readers.setdefault(k, {})
            if r.get(t[0], 0) < t[1]:
                r[t[0]] = t[1]
        for k in writes:
            self.lastw[k] = t
            self.readers[k] = {}

    def op(self, eng, fn, reads=(), writes=(), dsem=None):
        reads, writes = self._norm(reads, writes)
        E = self._waits(eng, reads, writes)
        ins = fn(E)
        if dsem is not None:
            if dsem not in self.sems:
                self.new_sem(dsem)
            sname, inc = dsem, 16
        else:
            sname, inc = eng, 1
        self.cnt[sname] += inc
        ins.then_inc(self.sems[sname], inc)
        t = (sname, self.cnt[sname])
        self._record(t, reads, writes)
        return t

    def dma_group(self, items, dsem, eng="sp"):
        if dsem not in self.sems:
            self.new_sem(dsem)
        for fn, reads, writes in items:
            E = self._waits(eng, reads, writes)
            ins = fn(E)
            self.cnt[dsem] += 16
            ins.then_inc(self.sems[dsem], 16)
        t = (dsem, self.cnt[dsem])
        for fn, reads, writes in items:
            self._record(t, reads, writes)
        return t

    def final_wait(self, eng):
        E = self.eng[eng]
        for s, v in self.cnt.items():
            if v > 0:
                E.wait_ge(self.sems[s], v)


def build_nc(seg_chunks=SEG_CHUNKS, nx=NX):
    nc = bass.Bass("TRN2", target_bir_lowering=False)
    chunks = []
    seg_first = []
    nreal = 0
    for s, n in enumerate(seg_chunks):
        assert n % 4 == 0
        seg_first.append(len(chunks))
        for e in range(n + 2):
            kind = "pre" if e == 0 else ("post" if e == n + 1 else "real")
            chunks.append(dict(seg=s, kind=kind, e=e, ridx=None))
            if kind == "real":
                chunks[-1]["ridx"] = nreal
                nreal += 1
    NCH = len(chunks)
    sts = []
    for s, n in enumerate(seg_chunks):
        for i in range(n // 4):
            sts.append(dict(seg=s, c0=seg_first[s] + 1 + 4 * i))
    NST = len(sts)
    for g, st in enumerate(sts):
        for j in range(4):
            chunks[st["c0"] + j]["st"] = g
            chunks[st["c0"] + j]["pos"] = j

    xin = nc.dram_tensor("xin", [NCH * 128, D], F32, kind="ExternalInput").ap()
    w_in = nc.dram_tensor("w_in", [D, INW], F32, kind="ExternalInput").ap()
    w_out = nc.dram_tensor("w_out", [D, D], F32, kind="ExternalInput").ap()
    d_pm = nc.dram_tensor("pm", [128, NPM * PMW], F32, kind="ExternalInput").ap()
    d_poolw = nc.dram_tensor("poolw", [128, 512], F32, kind="ExternalInput").ap()
    d_wst = nc.dram_tensor("wst", [128, 512], F32, kind="ExternalInput").ap()
    d_ident = nc.dram_tensor("ident", [128, 128], F32, kind="ExternalInput").ap()
    d_cols = nc.dram_tensor("cols", [128, 16], F32, kind="ExternalInput").ap()
    d_fg = nc.dram_tensor("fgb", [128, D], F32, kind="ExternalInput").ap()
    d_gb = nc.dram_tensor("gbc", [128, D], F32, kind="ExternalInput").ap()
    d_lnb = nc.dram_tensor("lnbb", [128, 512], F32, kind="ExternalInput").ap()
    d_bs = nc.dram_tensor("bsb", [128, 512], F32, kind="ExternalInput").ap()
    yout = nc.dram_tensor("yout", [nreal * 128, D], F32, kind="ExternalOutput").ap()

    es = ExitStack()
    with es:
        S = Sched(nc, es)

        def sb(name, shape, dt):
            return es.enter_context(nc.sbuf_tensor("s_" + name, shape, dt))

        xs = [sb(f"xs{i}", [128, D], F32) for i in range(nx)]
        w_in_bf = sb("w_in_bf", [128, KD, INW], BF16)
        w_out_bf = sb("w_out_bf", [128, KD, D], BF16)
        pm_bf = sb("pm_bf", [128, NPM, PMW], BF16)
        poolw_bf = sb("poolw_bf", [128, 4, 128], BF16)
        wst_bf = sb("wst_bf", [128, 4, 128], BF16)
        ident_bf = sb("ident_bf", [128, 128], BF16)
        lnb_bf = sb("lnb_bf", [128, 512], BF16)
        cols = sb("cols_sb", [128, 16], F32)
        fgb = sb("fgb_sb", [128, D], F32)
        gbc = sb("gbc_sb", [128, D], F32)
        cmat = sb("cmat", [128, 4, 128], F32)
        neghalf = sb("neghalf", [128, 1], F32)
        stat = sb("stat", [128, NCH * 16], F32)
        junk = sb("junk", [128, D], BF16)
        htok = [sb(f"htok{i}", [128, D], BF16) for i in range(2)]
        hT = [sb(f"hT{i}", [128, KD, 512], BF16) for i in range(2)]
        hTh = sb("hTh", [128, KD, 128], BF16)
        NA = 6
        a_tok = [sb(f"a_tok{i}", [128, 512], BF16) for i in range(NA)]
        NZ = 4
        z_tok = [sb(f"z_tok{i}", [128, 512], BF16) for i in range(NZ)]
        sga = sb("sga", [128, 4, 512], F32)
        sgb = [sb(f"sgb{i}", [128, 512], F32) for i in range(1)]
        usg = sb("usg", [128, 4, 512], F32)
        t1 = [sb(f"t1_{i}", [128, 512], F32) for i in range(2)]
        dT = sb("dT", [128, 4, 512], BF16)
        mixT = [sb(f"mixT{i}", [128, KD, 512], BF16) for i in range(2)]

        tp = es.enter_context(nc.psum_tensor("tp", [128, D], BF16))
        NTOK, NFM = 4, 3
        tokb = [es.enter_context(nc.psum_tensor(f"tokb{i}", [128, 512], F32)) for i in range(NTOK)]
        fmb = [es.enter_context(nc.psum_tensor(f"fmb{i}", [128, 512], F32)) for i in range(NFM)]
        ring = dict(tok=0, fm=0, xs=0, t1=0)

        def nxt(name, n):
            v = ring[name]
            ring[name] = (v + 1) % n
            return v

        S.op("pool", lambda E: E.memset(neghalf[:], -0.5), writes=["neghalf"])
        S.op("act", lambda E: E.activation(out=junk[:, 0:1], in_=neghalf[:], func=AF.Silu),
             reads=["neghalf"], writes=["junk"])
        S.dma_group([
            (lambda E: E.dma_start(out=cols[:], in_=d_cols[:, :]), [], ["cols"]),
            (lambda E: E.dma_start(out=gbc[:], in_=d_gb[:, :]), [], ["gbc"]),
        ], "cst")

        win_keys = {cg: [f"win{cg}"] for cg in range(5)}
        wout_keys = ["wout"]
        pm_keys = ["pm"]

        def load_win(cg):
            items = []
            for kp in range(0, KD, 2):
                src = w_in[kp * 128:(kp + 2) * 128, cg * 512:(cg + 1) * 512].rearrange("(k p) c -> p k c", p=128)
                dst = w_in_bf[:, kp:kp + 2, cg * 512:(cg + 1) * 512]
                items.append((lambda E, dst=dst, src=src: E.dma_start(out=dst, in_=src), [], [f"win{cg}"]))
            S.dma_group(items, f"w{cg}", eng="pool")

        def load_win_half(cg, hf):
            items = []
            c0_ = cg * 512 + hf * 256
            for kp in range(0, KD, 4):
                src = w_in[kp * 128:(kp + 4) * 128, c0_:c0_ + 256].rearrange("(k p) c -> p k c", p=128)
                dst = w_in_bf[:, kp:kp + 4, c0_:c0_ + 256]
                items.append((lambda E, dst=dst, src=src: E.dma_start(out=dst, in_=src), [], [f"win{cg}h{hf}"]))
            S.dma_group(items, f"w{cg}h{hf}", eng="pool")

        def load_wout():
            items = []
            for kp in range(0, KD, 2):
                src = w_out[kp * 128:(kp + 2) * 128, :].rearrange("(k p) c -> p k c", p=128)
                dst = w_out_bf[:, kp:kp + 2, :]
                items.append((lambda E, dst=dst, src=src: E.dma_start(out=dst, in_=src), [], ["wout"]))
            S.dma_group(items, "wo", eng="pool")
            S.dma_group([(lambda E: E.dma_start(out=fgb[:], in_=d_fg[:, :]), [], ["fgb"])], "cst3")

        def load_ident():
            S.dma_group([(lambda E: E.dma_start(out=ident_bf[:], in_=d_ident[:, :]), [], ["ident"])], "wid", eng="pool")

        def load_pm():
            items = []
            for j in range(0, NPM, 5):
                n = min(5, NPM - j)
                dst = pm_bf[:, j:j + n, :].rearrange("p m t -> p (m t)")
                src = d_pm[:, j * PMW:(j + n) * PMW]
                items.append((lambda E, dst=dst, src=src: E.dma_start(out=dst, in_=src), [], ["pm"]))
            S.dma_group(items, "wpm", eng="pool")

        def load_small():
            items = [
                (lambda E: E.dma_start(out=wst_bf[:].rearrange("p h q -> p (h q)"), in_=d_wst[:, :]), [], ["wst"]),
                (lambda E: E.dma_start(out=lnb_bf[:], in_=d_lnb[:, :]), [], ["lnb"]),
                (lambda E: E.dma_start(out=poolw_bf[:].rearrange("p g d -> p (g d)"), in_=d_poolw[:, :]), [], ["poolw"]),
            ]
            S.dma_group(items, "wsm", eng="pool")
            S.dma_group([
                (lambda E: E.dma_start(out=t1[0][:], in_=d_bs[:, :]), [], ["t1_0"]),
            ], "cst2")

        def compute_cmat():
            fb = nxt("fm", NFM)

            def cm_mm(E):
                ins = None
                for h in range(4):
                    ins = E.matmul(fmb[fb][:, h * 128:(h + 1) * 128], lhsT=lnb_bf[:, h * 128:(h + 1) * 128],
                                   rhs=wst_bf[:, h, :], start=True, stop=True)
                return ins
            S.op("pe", cm_mm, reads=["lnb", "wst"], writes=[f"fmb{fb}"])
            S.op("dve", lambda E: E.tensor_tensor(out=cmat[:].rearrange("p h q -> p (h q)"), in0=fmb[fb][:],
                                                  in1=t1[0][:], op=ALU.add),
                 reads=[f"fmb{fb}", "t1_0"], writes=["cmat"])

        slot_occ = {}
        x_done = {}

        xslot = {}
        loaded = [0]

        def ensure_loaded(upto):
            while loaded[0] <= min(upto, NCH - 1):
                c = loaded[0]
                sl = ring["xs"]
                occ = slot_occ.get(sl)
                if occ is not None and not x_done.get(occ, False):
                    break
                nxt("xs", nx)
                xslot[c] = sl
                slot_occ[sl] = c
                thr = []
                if 2 <= c < 6:
                    thr = win_keys[0]
                elif 6 <= c < 9:
                    thr = win_keys[3]
                elif 9 <= c < 13:
                    thr = ["win2h1"]
                S.op("sp", lambda E: E.dma_start(out=xs[sl][:], in_=xin[c * 128:(c + 1) * 128, :]),
                     reads=thr, writes=[(f"xs{sl}", (c, "x"))], dsem=f"ld{sl}")
                loaded[0] += 1

        def scol(c, j):
            return stat[:, c * 16 + j:c * 16 + j + 1]

        def rstd_ops(c, jin, jtmp, jout, key_in, key_out):
            S.op("pool", lambda E: E.tensor_scalar(out=scol(c, jtmp), in0=scol(c, jin), scalar1=1.0, scalar2=EPS,
                                                   op0=ALU.mult, op1=ALU.add), reads=[key_in], writes=[f"tmp{c}_{jtmp}"])
            S.op("pool", lambda E: E.tensor_tensor(out=scol(c, jout), in0=scol(c, jtmp), in1=neghalf[:],
                                                   op=ALU.pow), reads=[f"tmp{c}_{jtmp}", "neghalf"], writes=[key_out])

        def hT_view(c, k=None):
            ch = chunks[c]
            if ch["kind"] == "real":
                t = hT[ch["st"] % 2]
                lo = ch["pos"] * 128
                key = f"hT{ch['st'] % 2}_{ch['pos']}"
            else:
                t = hTh
                lo = 0
                key = "hTh"
            if k is None:
                return t[:, :, lo:lo + 128], key
            return t[:, k, lo:lo + 128], key

        fe = dict(a1=0, a2=0, b=0)

        def fe_a1(c):
            ensure_loaded(c + LA)
            assert c < loaded[0], f"x ring too small: chunk {c} not loadable"
            sl = xslot[c]
            S.op("act", lambda E: E.activation(out=junk[:], in_=xs[sl][:], func=AF.Square, scale=1.0 / 32.0,
                                               accum_out=scol(c, 0)),
                 reads=[(f"xs{sl}", (c, "x"))], writes=[f"ms{c}", "junk"])
            rstd_ops(c, 0, 1, 2, f"ms{c}", f"rstd{c}")

        def fe_a2(c):
            sl = xslot[c]
            hs = c % 2
            S.op("dve", lambda E: E.scalar_tensor_tensor(out=htok[hs][:], in0=xs[sl][:], scalar=scol(c, 2),
                                                         in1=gbc[:], op0=ALU.mult, op1=ALU.mult),
                 reads=[(f"xs{sl}", (c, "x")), f"rstd{c}", "gbc"], writes=[(f"htok{hs}", c)])
            if chunks[c]["kind"] != "real":
                x_done[c] = True

        def fe_b(c):
            hs = c % 2

            def tr(E):
                ins = None
                for k in range(KD):
                    ins = E.transpose(tp[:, k * 128:(k + 1) * 128], htok[hs][:, k * 128:(k + 1) * 128], ident_bf[:])
                return ins
            S.op("pe", tr, reads=[(f"htok{hs}", c), "ident"], writes=[("tp", c)])
            dst, hk = hT_view(c)
            S.op("act", lambda E: E.copy(out=dst, in_=tp[:].rearrange("p (k t) -> p k t", k=KD)),
                 reads=[("tp", c)], writes=[(hk, c)])

        def fe_can(c):
            ensure_loaded(c + LA)
            return c < loaded[0]

        def fe_tick():
            cb = fe["b"]
            if cb >= NCH:
                return
            while fe["a1"] <= cb:
                fe_a1(fe["a1"]); fe["a1"] += 1
            while fe["a2"] <= cb:
                fe_a2(fe["a2"]); fe["a2"] += 1
            fe_b(cb)
            fe["b"] += 1
            if fe["a2"] == cb + 1 and cb + 1 < NCH and (fe["a1"] > cb + 1 or fe_can(cb + 1)):
                if fe["a1"] == cb + 1:
                    fe_a1(cb + 1); fe["a1"] += 1
                fe_a2(cb + 1); fe["a2"] += 1
            if fe["a1"] == cb + 2 and cb + 2 < NCH and fe_can(cb + 2):
                fe_a1(cb + 2); fe["a1"] += 1
            ensure_loaded(fe["a1"] + LA)

        def ensure_fe(upto):
            while fe["b"] <= min(upto, NCH - 1):
                fe_tick()

        def TM_a(c):
            ensure_fe(c)
            b = nxt("tok", NTOK)
            bk = f"tokb{b}"

            def mm(E):
                ins = None
                for k in range(KD):
                    lhsT, _ = hT_view(c, k)
                    ins = E.matmul(tokb[b][:], lhsT=lhsT, rhs=w_in_bf[:, k, 0:512],
                                   start=(k == 0), stop=(k == KD - 1))
                return ins
            S.op("pe", mm, reads=[(hT_view(c)[1], c)] + win_keys[0], writes=[bk])
            sa = c % NA
            S.op("act", lambda E: E.copy(out=a_tok[sa][:], in_=tokb[b][:]), reads=[bk], writes=[(f"a{sa}", c)])

        def TM_v(c):
            ensure_fe(c)
            b = nxt("tok", NTOK)
            bk = f"tokb{b}"

            def mm(E):
                ins = None
                for k in range(KD):
                    lhsT, _ = hT_view(c, k)
                    ins = E.matmul(tokb[b][:], lhsT=lhsT, rhs=w_in_bf[:, k, 3 * 512:4 * 512],
                                   start=(k == 0), stop=(k == KD - 1))
                return ins
            S.op("pe", mm, reads=[(hT_view(c)[1], c)] + win_keys[3], writes=[bk])
            S.op("dve", lambda E: E.bn_stats(out=stat[:, c * 16 + 4:c * 16 + 10], in_=tokb[b][:]),
                 reads=[bk], writes=[f"bst{c}"])
            S.op("dve", lambda E: E.bn_aggr(out=stat[:, c * 16 + 10:c * 16 + 12], in_=stat[:, c * 16 + 4:c * 16 + 10]),
                 reads=[f"bst{c}"], writes=[f"mv{c}"])
            rstd_ops(c, 11, 12, 13, f"mv{c}", f"rstdv{c}")

            def zfn():
                sz = chunks[c]["ridx"] % NZ
                S.op("dve", lambda E: E.tensor_scalar(out=z_tok[sz][:], in0=tokb[b][:], scalar1=scol(c, 10),
                                                      scalar2=scol(c, 13), op0=ALU.subtract, op1=ALU.mult),
                     reads=[bk, f"mv{c}", f"rstdv{c}"], writes=[(f"z{sz}", c)])
            return zfn

        def FM_block(g, kind, j):
            cg = dict(ga=1, u=2, gb=4)[kind]
            col0 = cg * 512 + j * 128
            slot = g % 2
            st = sts[g]
            b = nxt("fm", NFM)
            bk = f"fmb{b}"

            def mm(E):
                ins = None
                for k in range(KD):
                    ins = E.matmul(fmb[b][:], lhsT=w_in_bf[:, k, col0:col0 + 128], rhs=hT[slot][:, k, :],
                                   start=(k == 0), stop=(k == KD - 1))
                return ins
            wk = win_keys[cg] if cg == 1 else [f"win{cg}h{j // 2}"]
            S.op("pe", mm, reads=[(f"hT{slot}_{p}", st["c0"] + p) for p in range(4)] + wk, writes=[bk])
            if kind == "ga":
                S.op("act", lambda E: E.activation(out=sga[:, j, :], in_=fmb[b][:], func=AF.Silu),
                     reads=[bk], writes=[(f"sga{j}", g)])
            elif kind == "gb":
                s2 = 0
                S.op("act", lambda E: E.activation(out=sgb[s2][:], in_=fmb[b][:], func=AF.Silu),
                     reads=[bk], writes=[(f"sgb{s2}", (g, j))])
            else:
                s2 = 0
                S.op("dve", lambda E: E.tensor_tensor(out=usg[:, j, :], in0=fmb[b][:], in1=sgb[s2][:], op=ALU.mult),
                     reads=[bk, (f"sgb{s2}", (g, j))], writes=[(f"usg{j}", g)])

        def pm_idx(c, gq):
            ch = chunks[c]
            n = seg_chunks[ch["seg"]]
            if ch["e"] == 1:
                return 4 + ch["seg"] * 8 + gq
            if ch["e"] == n:
                return 4 + ch["seg"] * 8 + 4 + gq
            return gq

        def POOL(g, gq):
            st = sts[g]
            c0 = st["c0"]
            b = nxt("fm", NFM)
            bk = f"fmb{b}"

            def mm(E):
                E.matmul(fmb[b][:, 0:8], lhsT=a_tok[(c0 - 1) % NA][:, gq * 128:(gq + 1) * 128],
                         rhs=pm_bf[:, gq, 136:144], start=True, stop=False, skip_group_check=True)
                for j in range(4):
                    c = c0 + j
                    lo = 8 if j == 0 else 0
                    hi = 136 if j == 3 else 144
                    o0 = j * 128 - 8 + lo
                    E.matmul(fmb[b][:, o0:o0 + (hi - lo)], lhsT=a_tok[c % NA][:, gq * 128:(gq + 1) * 128],
                             rhs=pm_bf[:, pm_idx(c, gq), lo:hi], start=False, stop=False, skip_group_check=True)
                return E.matmul(fmb[b][:, 504:512], lhsT=a_tok[(c0 + 4) % NA][:, gq * 128:(gq + 1) * 128],
                                rhs=pm_bf[:, gq, 0:8], start=False, stop=True, skip_group_check=True)
            rd = [(f"a{(c0 + j) % NA}", c0 + j) for j in range(-1, 5)] + pm_keys
            S.op("pe", mm, reads=rd, writes=[bk])
            S.op("act", lambda E: E.copy(out=dT[:, gq, :], in_=fmb[b][:]), reads=[bk], writes=[(f"dT_{gq}", g)])

        def SGh(g, h):
            st = sts[g]
            ms = g % 2
            b = nxt("fm", NFM)
            bk = f"fmb{b}"

            def mm(E):
                ins = None
                for j in range(4):
                    c = st["c0"] + j
                    sz = chunks[c]["ridx"] % NZ
                    ins = E.matmul(fmb[b][:, j * 128:(j + 1) * 128], lhsT=z_tok[sz][:, h * 128:(h + 1) * 128],
                                   rhs=wst_bf[:, h, :], start=True, stop=True)
                return ins
            rd = [(f"z{chunks[st['c0'] + j]['ridx'] % NZ}", st["c0"] + j) for j in range(4)] + ["wst"]
            S.op("pe", mm, reads=rd, writes=[bk])
            ts = nxt("t1", 2)
            S.op("dve", lambda E: E.scalar_tensor_tensor(
                out=t1[ts][:].rearrange("p (j q) -> p j q", j=4),
                in0=fmb[b][:].rearrange("p (j q) -> p j q", j=4),
                scalar=cols[:, 12 + h:13 + h],
                in1=cmat[:, h, :].unsqueeze(1).to_broadcast([128, 4, 128]),
                op0=ALU.mult, op1=ALU.add), reads=[bk, "cols", "cmat"], writes=[(f"t1_{ts}", (g, h))])
            S.op("pool", lambda E: E.tensor_tensor(out=mixT[ms][:, 4 + h, :], in0=t1[ts][:], in1=usg[:, h, :],
                                                   op=ALU.mult),
                 reads=[(f"t1_{ts}", (g, h)), (f"usg{h}", g)], writes=[(f"mixT{ms}_{4 + h}", g)])

        def PWq(g, gq):
            ms = g % 2
            b = nxt("fm", NFM)
            bk = f"fmb{b}"
            S.op("pe", lambda E: E.matmul(fmb[b][:], lhsT=poolw_bf[:, gq, :], rhs=dT[:, gq, :],
                                          start=True, stop=True),
                 reads=[(f"dT_{gq}", g), "poolw"], writes=[bk])
            S.op("dve", lambda E: E.scalar_tensor_tensor(
                out=mixT[ms][:, gq, :], in0=fmb[b][:], scalar=cols[:, 8 + gq:9 + gq], in1=sga[:, gq, :],
                op0=ALU.mult, op1=ALU.mult), reads=[bk, "cols", (f"sga{gq}", g)], writes=[(f"mixT{ms}_{gq}", g)])

        def WO_main(g, j):
            st = sts[g]
            ms = g % 2
            c = st["c0"] + j
            sl = xslot[c]
            xk = f"xs{sl}"
            for hf in range(2):
                b = nxt("tok", NTOK)
                bk = f"tokb{b}"

                def mm(E):
                    ins = None
                    for k in range(KD):
                        ins = E.matmul(tokb[b][:], lhsT=mixT[ms][:, k, j * 128:(j + 1) * 128],
                                       rhs=w_out_bf[:, k, hf * 512:(hf + 1) * 512],
                                       start=(k == 0), stop=(k == KD - 1))
                    return ins
                S.op("pe", mm, reads=[(f"mixT{ms}_{k}", g) for k in range(KD)] + wout_keys, writes=[bk])
                S.op("dve", lambda E: E.tensor_tensor(out=xs[sl][:, hf * 512:(hf + 1) * 512], in0=tokb[b][:],
                                                      in1=xs[sl][:, hf * 512:(hf + 1) * 512], op=ALU.add),
                     reads=[bk, (xk, (c, "x" if hf == 0 else "xo0"))], writes=[(xk, (c, "xo0" if hf == 0 else "xo"))])
            S.op("act", lambda E: E.activation(out=junk[:], in_=xs[sl][:], func=AF.Square, scale=1.0 / 32.0,
                                               accum_out=scol(c, 3)),
                 reads=[(xk, (c, "xo"))], writes=[f"ms2_{c}", "junk"])
            rstd_ops(c, 3, 14, 15, f"ms2_{c}", f"rstd2_{c}")

            def tail():
                S.op("dve", lambda E: E.scalar_tensor_tensor(out=xs[sl][:], in0=xs[sl][:], scalar=scol(c, 15),
                                                             in1=fgb[:], op0=ALU.mult, op1=ALU.mult),
                     reads=[(xk, (c, "xo")), f"rstd2_{c}", "fgb"], writes=[(xk, (c, "y"))])
                r = chunks[c]["ridx"]
                S.op("sp", lambda E: E.dma_start(out=yout[r * 128:(r + 1) * 128, :], in_=xs[sl][:]),
                     reads=[(xk, (c, "y"))], dsem=f"st{sl}")
                x_done[c] = True
                ensure_loaded(fe["a1"] + LA)
            return tail

        load_ident()
        load_win(0)
        load_win(3)
        ensure_loaded(3)
        for c_ in range(2):
            fe_a1(c_)
        fe["a1"] = 2
        for c_ in range(2):
            fe_a2(c_)
        fe["a2"] = 2
        fe_tick()
        fe_tick()
        load_pm()
        load_win(1)
        late_loads = [lambda: (load_win_half(4, 0), load_win_half(2, 0)), load_small,
                      lambda: (load_win_half(4, 1), load_win_half(2, 1))]

        tma_done = [0]
        deferred = []
        for g, st in enumerate(sts):
            c0 = st["c0"]
            if g > 0:
                ensure_fe(c0 + 4)
            while tma_done[0] < c0 + 1:
                TM_a(tma_done[0])
                tma_done[0] += 1
            for j in range(4):
                if g == 0:
                    ensure_fe(c0 + j + 2)
                zfn = TM_v(c0 + j)
                TM_a(c0 + j + 1)
                tma_done[0] = c0 + j + 2
                zfn()
                if late_loads:
                    late_loads.pop(0)()
            if g > 0:
                tails = []
                for j in range(4):
                    tails.append(WO_main(g - 1, j))
                    if j >= 1:
                        tails[j - 1]()
                deferred.append(tails[3])
            nxt_c0 = sts[g + 1]["c0"] if g + 1 < NST else None
            tick_target = (nxt_c0 + 3) if nxt_c0 is not None else NCH - 1
            n_ticks = max(0, tick_target - fe["b"] + 1)
            if n_ticks <= 3:
                tick_pos = {("P", 1), ("P", 3), ("SG", 0)}
            else:
                tick_pos = {("P", 0), ("P", 1), ("P", 2), ("P", 3), ("PW", 0), ("SG", 0), ("SG", 1)}
            base = [("ga", 0), ("P", 0), ("ga", 1), ("P", 1), ("ga", 2), ("P", 2), ("ga", 3), ("P", 3),
                    ("gb", 0), ("PW", 0), ("u", 0), ("PW", 1), ("gb", 1), ("SG", 0), ("u", 1), ("PW", 2),
                    ("gb", 2), ("SG", 1), ("u", 2), ("PW", 3), ("gb", 3), ("SG", 2), ("u", 3), ("tick4",),
                    ("SG", 3)]
            if g == NST - 1:
                base = [("gb", 0), ("P", 0), ("u", 0), ("P", 1), ("gb", 1), ("SG", 0), ("u", 1), ("P", 2),
                        ("gb", 2), ("SG", 1), ("u", 2), ("P", 3), ("gb", 3), ("SG", 2), ("u", 3), ("SG", 3),
                        ("ga", 0), ("PW", 0), ("ga", 1), ("PW", 1), ("ga", 2), ("PW", 2), ("ga", 3), ("PW", 3)]
            seq = []
            for it in base:
                seq.append(it)
                if it in tick_pos:
                    seq.append(("tick",))
            bi = 0
            for it in seq:
                if it[0] in ("ga", "gb", "u"):
                    FM_block(g, it[0], it[1])
                    if deferred:
                        deferred.pop(0)()
                    if g == 0 and bi == 0:
                        load_wout()
                    bi += 1
                elif it[0] == "P":
                    POOL(g, it[1])
                elif it[0] == "SG":
                    if g == 0 and it[1] == 0:
                        compute_cmat()
                    SGh(g, it[1])
                elif it[0] == "PW":
                    PWq(g, it[1])
                elif it[0] == "tick":
                    if nxt_c0 is not None and fe["b"] <= nxt_c0 + 3:
                        fe_tick()
                    elif nxt_c0 is None and fe["b"] < NCH:
                        fe_tick()
                elif it[0] == "tick4":
                    if nxt_c0 is not None:
                        ensure_fe(nxt_c0 + 3)
                        ensure_fe(nxt_c0 + 4)
        tails = []
        for j in range(4):
            tails.append(WO_main(NST - 1, j))
            if j >= 1:
                tails[j - 1]()
        tails[3]()
        assert not deferred
        S.final_wait("sp")
        print(f"[build] nwaits={S.nwaits} counts={ {k: v for k, v in S.cnt.items() if k in ('pe', 'act', 'dve', 'pool')} }")
    return nc, dict(NCH=NCH, nreal=nreal, chunks=chunks)


def _pool_mats(first_is_start, last_is_end):
    out = {}
    for gq, w in enumerate(WINDOWS):
        h = w // 2
        s = np.arange(128)[:, None]
        t = np.arange(128)[None, :]
        inwin = ((s >= t - h) & (s < t + h)).astype(np.float64)
        eye = (s == t).astype(np.float64)
        out[("cur", gq)] = (inwin / w - eye).astype(np.float32)
        out[("prev", gq)] = ((s - 128 >= t - h).astype(np.float64) / w).astype(np.float32)
        out[("next", gq)] = ((s + 128 < t + h).astype(np.float64) / w).astype(np.float32)
        cnt_start = np.minimum(w, t + h).astype(np.float64)
        out[("cur_start", gq)] = (inwin / cnt_start - eye).astype(np.float32)
        cnt_end = np.minimum(w, 128 - t + h).astype(np.float64)
        out[("cur_end", gq)] = (inwin / cnt_end - eye).astype(np.float32)
    return out


def _pm_for_core(seg_start_end):
    m = _pool_mats(None, None)

    def wide(cur_key, gq):
        return np.concatenate([m[("next", gq)][:, 120:128], m[(cur_key, gq)], m[("prev", gq)][:, 0:8]], axis=1)
    mats = [wide("cur", gq) for gq in range(4)]
    for (fs, le) in seg_start_end:
        mats += [wide("cur_start" if fs else "cur", gq) for gq in range(4)]
        mats += [wide("cur_end" if le else "cur", gq) for gq in range(4)]
    arr = np.stack(mats, axis=0)
    assert arr.shape == (NPM, 128, PMW)
    return np.ascontiguousarray(arr.transpose(1, 0, 2).reshape(128, -1))


def _ext_segment(x_seq, lo, hi):
    S_ = x_seq.shape[0]
    out = np.zeros((hi - lo + 256, x_seq.shape[1]), np.float32)
    a = max(lo - 128, 0)
    b = min(hi + 128, S_)
    out[a - (lo - 128):b - (lo - 128)] = x_seq[a:b]
    return out


def make_in_maps(x_prompt, x_sample, norm_g, w_in, pool_w, pool_scale, sgu_ln_g, sgu_ln_b,
                 w_spatial, b_spatial, w_out, final_g, ncores=NCORES, seg_chunks=SEG_CHUNKS):
    f = np.float32
    w_in = np.ascontiguousarray(w_in, f)
    w_out = np.ascontiguousarray(w_out, f)
    cols = np.concatenate([np.asarray(norm_g, f).reshape(8, 128).T, np.asarray(pool_scale, f).reshape(4, 128).T,
                           np.asarray(sgu_ln_g, f).reshape(4, 128).T], axis=1)
    cols = np.ascontiguousarray(cols)
    fgb = np.ascontiguousarray(np.broadcast_to(np.asarray(final_g, f)[None, :], (128, D)))
    gbc = np.ascontiguousarray(np.broadcast_to(np.asarray(norm_g, f)[None, :], (128, D)))
    lnbb = np.ascontiguousarray(np.broadcast_to(np.asarray(sgu_ln_b, f)[None, :], (128, 512)))
    bsb = np.ascontiguousarray(np.broadcast_to(np.asarray(b_spatial, f).reshape(1, 512), (128, 512)))
    poolw = np.ascontiguousarray(np.asarray(pool_w, f).transpose(1, 0, 2).reshape(128, 512))
    wst = np.ascontiguousarray(np.asarray(w_spatial, f).transpose(2, 0, 1).reshape(128, 512))
    ident = np.eye(128, dtype=f)
    n0, n1 = seg_chunks
    L0, L1 = n0 * 128, n1 * 128
    halves = x_prompt.shape[1] // L0
    in_maps = []
    for c in range(ncores):
        b, hf = c // halves, c % halves
        seg0 = _ext_segment(np.asarray(x_prompt[b], f), hf * L0, (hf + 1) * L0)
        seg1 = _ext_segment(np.asarray(x_sample[c], f), 0, L1)
        xin = np.concatenate([seg0, seg1], axis=0)
        pm = _pm_for_core([(hf == 0, hf == halves - 1), (True, True)])
        in_maps.append(dict(xin=xin, w_in=w_in, w_out=w_out, pm=pm, poolw=poolw, wst=wst, ident=ident,
                            cols=cols, fgb=fgb, gbc=gbc, lnbb=lnbb, bsb=bsb))
    return in_maps


_NC_CACHE = {}


def kernel(x_prompt, x_sample, norm_g, w_in, pool_w, pool_scale, sgu_ln_g, sgu_ln_b,
           w_spatial, b_spatial, w_out, final_g):
    x_prompt = np.asarray(x_prompt)
    x_sample = np.asarray(x_sample)
    in_maps = make_in_maps(x_prompt, x_sample, norm_g, w_in, pool_w, pool_scale, sgu_ln_g, sgu_ln_b,
                           w_spatial, b_spatial, w_out, final_g)
    if "nc" not in _NC_CACHE:
        _NC_CACHE["nc"] = build_nc()[0]
    nc = _NC_CACHE["nc"]
    res = run_bass_kernel_spmd(nc, in_maps, core_ids=list(range(NCORES)))
    n0, n1 = SEG_CHUNKS
    L0, L1 = n0 * 128, n1 * 128
    y_prompt = np.empty(x_prompt.shape, np.float32)
    y_sample = np.empty(x_sample.shape, np.float32)
    halves = x_prompt.shape[1] // L0
    for c in range(NCORES):
        y = res.results[c]["yout"]
        b, hf = c // halves, c % halves
        y_prompt[b, hf * L0:(hf + 1) * L0] = y[:L0]
        y_sample[c] = y[L0:L0 + L1]
    return (y_prompt, y_sample)
```

```python
import numpy as np
from contextlib import ExitStack
import concourse.bass as bass
import concourse.mybir as mybir
from concourse.bass_utils import run_bass_kernel_spmd

F32 = mybir.dt.float32
BF16 = mybir.dt.bfloat16
AF = mybir.ActivationFunctionType
ALU = mybir.AluOpType

D = 1024
KD = 8
INW = 2560
EPS = 1e-6
WINDOWS = (2, 4, 8, 16)
NCORES = 8
SEG_CHUNKS = (32, 16)
LA = 2
NX = 13
HALO_FREE_SEGS = (1,)
NPM = 20
PMW = 144


class Sched:
    def __init__(self, nc, es):
        self.nc = nc
        self.es = es
        self.eng = dict(pe=nc.tensor, act=nc.scalar, dve=nc.vector, pool=nc.gpsimd, sp=nc.sync)
        self.sems = {}
        self.cnt = {}
        for e in ("pe", "act", "dve", "pool"):
            self.new_sem(e)
        self.waited = {e: {} for e in self.eng}
        self.lastw = {}
        self.readers = {}
        self.tags = {}
        self.nwaits = 0

    def new_sem(self, name):
        self.sems[name] = self.es.enter_context(self.nc.semaphore(name))
        self.cnt[name] = 0

    def _norm(self, reads, writes):
        rk, wk = [], []
        for it in reads:
            if isinstance(it, tuple):
                assert self.tags.get(it[0]) == it[1], f"stale read {it} has {self.tags.get(it[0])}"
                rk.append(it[0])
            else:
                rk.append(it)
        for it in writes:
            if isinstance(it, tuple):
                self.tags[it[0]] = it[1]
                wk.append(it[0])
            else:
                wk.append(it)
        return rk, wk

    def _waits(self, eng, reads, writes):
        need = {}

        def add(t):
            if t is None:
                return
            s, v = t
            if eng == "pe" and s == "pe":
                return
            if need.get(s, 0) < v:
                need[s] = v

        for k in reads:
            add(self.lastw.get(k))
        for k in writes:
            add(self.lastw.get(k))
            for s, v in self.readers.get(k, {}).items():
                add((s, v))
        E = self.eng[eng]
        w = self.waited[eng]
        for s, v in need.items():
            if w.get(s, 0) < v:
                E.wait_ge(self.sems[s], v)
                w[s] = v
                self.nwaits += 1
        return E

    def _record(self, t, reads, writes):
        for k in reads:
            r = self.readers.setdefault(k, {})
            if r.get(t[0], 0) < t[1]:
                r[t[0]] = t[1]
        for k in writes:
            self.lastw[k] = t
            self.readers[k] = {}

    def op(self, eng, fn, reads=(), writes=(), dsem=None):
        reads, writes = self._norm(reads, writes)
        E = self._waits(eng, reads, writes)
        ins = fn(E)
        if dsem is not None:
            if dsem not in self.sems:
                self.new_sem(dsem)
            sname, inc = dsem, 16
        else:
            sname, inc = eng, 1
        self.cnt[sname] += inc
        ins.then_inc(self.sems[sname], inc)
        t = (sname, self.cnt[sname])
        self._record(t, reads, writes)
        return t

    def dma_group(self, items, dsem, eng="sp"):
        if dsem not in self.sems:
            self.new_sem(dsem)
        for fn, reads, writes in items:
            E = self._waits(eng, reads, writes)
            ins = fn(E)
            self.cnt[dsem] += 16
            ins.then_inc(self.sems[dsem], 16)
        t = (dsem, self.cnt[dsem])
        for fn, reads, writes in items:
            self._record(t, reads, writes)
        return t

    def final_wait(self, eng):
        E = self.eng[eng]
        for s, v in self.cnt.items():
            if v > 0:
                E.wait_ge(self.sems[s], v)


def build_nc(seg_chunks=SEG_CHUNKS, nx=NX):
    nc = bass.Bass("TRN2", target_bir_lowering=False)
    chunks = []
    seg_first = []
    nreal = 0
    for s, n in enumerate(seg_chunks):
        assert n % 4 == 0
        seg_first.append(len(chunks))
        for e in range(n + 2):
            kind = "pre" if e == 0 else ("post" if e == n + 1 else "real")
            if kind != "real" and s in HALO_FREE_SEGS:
                continue
            chunks.append(dict(seg=s, kind=kind, e=e, ridx=None))
            if kind == "real":
                chunks[-1]["ridx"] = nreal
                nreal += 1
    NCH = len(chunks)
    sts = []
    for s, n in enumerate(seg_chunks):
        for i in range(n // 4):
            sts.append(dict(seg=s, c0=seg_first[s] + (0 if s in HALO_FREE_SEGS else 1) + 4 * i,
                            has_prev=not (s in HALO_FREE_SEGS and i == 0),
                            has_next=not (s in HALO_FREE_SEGS and i == n // 4 - 1)))
    NST = len(sts)
    for g, st in enumerate(sts):
        for j in range(4):
            chunks[st["c0"] + j]["st"] = g
            chunks[st["c0"] + j]["pos"] = j

    xin = nc.dram_tensor("xin", [NCH * 128, D], F32, kind="ExternalInput").ap()
    w_in = nc.dram_tensor("w_in", [D, INW], F32, kind="ExternalInput").ap()
    w_out = nc.dram_tensor("w_out", [D, D], F32, kind="ExternalInput").ap()
    d_pm = nc.dram_tensor("pm", [128, NPM * PMW], F32, kind="ExternalInput").ap()
    d_poolw = nc.dram_tensor("poolw", [128, 512], F32, kind="ExternalInput").ap()
    d_wst = nc.dram_tensor("wst", [128, 512], F32, kind="ExternalInput").ap()
    d_ident = nc.dram_tensor("ident", [128, 128], F32, kind="ExternalInput").ap()
    d_cols = nc.dram_tensor("cols", [128, 16], F32, kind="ExternalInput").ap()
    d_fg = nc.dram_tensor("fgb", [128, D], F32, kind="ExternalInput").ap()
    d_gb = nc.dram_tensor("gbc", [128, D], F32, kind="ExternalInput").ap()
    d_lnb = nc.dram_tensor("lnbb", [128, 512], F32, kind="ExternalInput").ap()
    d_bs = nc.dram_tensor("bsb", [128, 512], F32, kind="ExternalInput").ap()
    yout = nc.dram_tensor("yout", [nreal * 128, D], F32, kind="ExternalOutput").ap()

    es = ExitStack()
    with es:
        S = Sched(nc, es)

        def sb(name, shape, dt):
            return es.enter_context(nc.sbuf_tensor("s_" + name, shape, dt))

        xs = [sb(f"xs{i}", [128, D], F32) for i in range(nx)]
        w_in_bf = sb("w_in_bf", [128, KD, INW], BF16)
        w_out_bf = sb("w_out_bf", [128, KD, D], BF16)
        pm_bf = sb("pm_bf", [128, NPM, PMW], BF16)
        poolw_bf = sb("poolw_bf", [128, 4, 128], BF16)
        wst_bf = sb("wst_bf", [128, 4, 128], BF16)
        ident_bf = sb("ident_bf", [128, 128], BF16)
        lnb_bf = sb("lnb_bf", [128, 512], BF16)
        cols = sb("cols_sb", [128, 16], F32)
        fgb = sb("fgb_sb", [128, D], F32)
        gbc = sb("gbc_sb", [128, D], F32)
        cmat = sb("cmat", [128, 4, 128], F32)
        neghalf = sb("neghalf", [128, 1], F32)
        stat = sb("stat", [128, NCH * 16], F32)
        junk = sb("junk", [128, D], BF16)
        htok = [sb(f"htok{i}", [128, D], BF16) for i in range(2)]
        hT = [sb(f"hT{i}", [128, KD, 512], BF16) for i in range(2)]
        hTh = sb("hTh", [128, KD, 128], BF16)
        NA = 6
        a_tok = [sb(f"a_tok{i}", [128, 512], BF16) for i in range(NA)]
        NZ = 4
        z_tok = [sb(f"z_tok{i}", [128, 512], BF16) for i in range(NZ)]
        sga = sb("sga", [128, 4, 512], F32)
        sgb = [sb(f"sgb{i}", [128, 512], F32) for i in range(1)]
        usg = sb("usg", [128, 4, 512], F32)
        t1 = [sb(f"t1_{i}", [128, 512], F32) for i in range(2)]
        dT = sb("dT", [128, 4, 512], BF16)
        mixT = [sb(f"mixT{i}", [128, KD, 512], BF16) for i in range(2)]

        tp = es.enter_context(nc.psum_tensor("tp", [128, D], BF16))
        NTOK, NFM = 4, 3
        tokb = [es.enter_context(nc.psum_tensor(f"tokb{i}", [128, 512], F32)) for i in range(NTOK)]
        fmb = [es.enter_context(nc.psum_tensor(f"fmb{i}", [128, 512], F32)) for i in range(NFM)]
        ring = dict(tok=0, fm=0, xs=0, t1=0)

        def nxt(name, n):
            v = ring[name]
            ring[name] = (v + 1) % n
            return v

        S.op("pool", lambda E: E.memset(neghalf[:], -0.5), writes=["neghalf"])
        S.op("act", lambda E: E.activation(out=junk[:, 0:1], in_=neghalf[:], func=AF.Silu),
             reads=["neghalf"], writes=["junk"])
        S.dma_group([
            (lambda E: E.dma_start(out=cols[:], in_=d_cols[:, :]), [], ["cols"]),
            (lambda E: E.dma_start(out=gbc[:], in_=d_gb[:, :]), [], ["gbc"]),
        ], "cst")

        win_keys = {cg: [f"win{cg}"] for cg in range(5)}
        wout_keys = ["wout"]
        pm_keys = ["pm"]

        def load_win(cg):
            items = []
            for kp in range(0, KD, 2):
                src = w_in[kp * 128:(kp + 2) * 128, cg * 512:(cg + 1) * 512].rearrange("(k p) c -> p k c", p=128)
                dst = w_in_bf[:, kp:kp + 2, cg * 512:(cg + 1) * 512]
                items.append((lambda E, dst=dst, src=src: E.dma_start(out=dst, in_=src), [], [f"win{cg}"]))
            S.dma_group(items, f"w{cg}", eng="pool")

        def load_win_half(cg, hf):
            items = []
            c0_ = cg * 512 + hf * 256
            for kp in range(0, KD, 4):
                src = w_in[kp * 128:(kp + 4) * 128, c0_:c0_ + 256].rearrange("(k p) c -> p k c", p=128)
                dst = w_in_bf[:, kp:kp + 4, c0_:c0_ + 256]
                items.append((lambda E, dst=dst, src=src: E.dma_start(out=dst, in_=src), [], [f"win{cg}h{hf}"]))
            S.dma_group(items, f"w{cg}h{hf}", eng="pool")

        def load_wout():
            items = []
            for kp in range(0, KD, 2):
                src = w_out[kp * 128:(kp + 2) * 128, :].rearrange("(k p) c -> p k c", p=128)
                dst = w_out_bf[:, kp:kp + 2, :]
                items.append((lambda E, dst=dst, src=src: E.dma_start(out=dst, in_=src), [], ["wout"]))
            S.dma_group(items, "wo", eng="pool")
            S.dma_group([(lambda E: E.dma_start(out=fgb[:], in_=d_fg[:, :]), [], ["fgb"])], "cst3")

        def load_ident():
            S.dma_group([(lambda E: E.dma_start(out=ident_bf[:], in_=d_ident[:, :]), [], ["ident"])], "wid", eng="pool")

        def load_pm():
            items = []
            for j in range(0, NPM, 5):
                n = min(5, NPM - j)
                dst = pm_bf[:, j:j + n, :].rearrange("p m t -> p (m t)")
                src = d_pm[:, j * PMW:(j + n) * PMW]
                items.append((lambda E, dst=dst, src=src: E.dma_start(out=dst, in_=src), [], ["pm"]))
            S.dma_group(items, "wpm", eng="pool")

        def load_small():
            items = [
                (lambda E: E.dma_start(out=wst_bf[:].rearrange("p h q -> p (h q)"), in_=d_wst[:, :]), [], ["wst"]),
                (lambda E: E.dma_start(out=lnb_bf[:], in_=d_lnb[:, :]), [], ["lnb"]),
                (lambda E: E.dma_start(out=poolw_bf[:].rearrange("p g d -> p (g d)"), in_=d_poolw[:, :]), [], ["poolw"]),
            ]
            S.dma_group(items, "wsm", eng="pool")
            S.dma_group([
                (lambda E: E.dma_start(out=t1[0][:], in_=d_bs[:, :]), [], ["t1_0"]),
            ], "cst2")

        def compute_cmat():
            fb = nxt("fm", NFM)

            def cm_mm(E):
                ins = None
                for h in range(4):
                    ins = E.matmul(fmb[fb][:, h * 128:(h + 1) * 128], lhsT=lnb_bf[:, h * 128:(h + 1) * 128],
                                   rhs=wst_bf[:, h, :], start=True, stop=True)
                return ins
            S.op("pe", cm_mm, reads=["lnb", "wst"], writes=[f"fmb{fb}"])
            S.op("dve", lambda E: E.tensor_tensor(out=cmat[:].rearrange("p h q -> p (h q)"), in0=fmb[fb][:],
                                                  in1=t1[0][:], op=ALU.add),
                 reads=[f"fmb{fb}", "t1_0"], writes=["cmat"])

        slot_occ = {}
        x_done = {}

        xslot = {}
        loaded = [0]

        def ensure_loaded(upto):
            while loaded[0] <= min(upto, NCH - 1):
                c = loaded[0]
                sl = ring["xs"]
                occ = slot_occ.get(sl)
                if occ is not None and not x_done.get(occ, False):
                    break
                nxt("xs", nx)
                xslot[c] = sl
                slot_occ[sl] = c
                thr = []
                if 4 <= c < 6:
                    thr = win_keys[0]
                elif 6 <= c < 9:
                    thr = win_keys[3]
                elif 9 <= c < 13:
                    thr = ["win2h1"]
                S.op("sp", lambda E: E.dma_start(out=xs[sl][:], in_=xin[c * 128:(c + 1) * 128, :]),
                     reads=thr, writes=[(f"xs{sl}", (c, "x"))], dsem=f"ld{sl}")
                loaded[0] += 1

        def scol(c, j):
            return stat[:, c * 16 + j:c * 16 + j + 1]

        def rstd_ops(c, jin, jtmp, jout, key_in, key_out):
            S.op("pool", lambda E: E.tensor_scalar(out=scol(c, jtmp), in0=scol(c, jin), scalar1=1.0, scalar2=EPS,
                                                   op0=ALU.mult, op1=ALU.add), reads=[key_in], writes=[f"tmp{c}_{jtmp}"])
            S.op("pool", lambda E: E.tensor_tensor(out=scol(c, jout), in0=scol(c, jtmp), in1=neghalf[:],
                                                   op=ALU.pow), reads=[f"tmp{c}_{jtmp}", "neghalf"], writes=[key_out])

        def hT_view(c, k=None):
            ch = chunks[c]
            if ch["kind"] == "real":
                t = hT[ch["st"] % 2]
                lo = ch["pos"] * 128
                key = f"hT{ch['st'] % 2}_{ch['pos']}"
            else:
                t = hTh
                lo = 0
                key = "hTh"
            if k is None:
                return t[:, :, lo:lo + 128], key
            return t[:, k, lo:lo + 128], key

        fe = dict(a1=0, a2=0, b=0)

        def fe_a1(c):
            ensure_loaded(c + LA)
            assert c < loaded[0], f"x ring too small: chunk {c} not loadable"
            sl = xslot[c]
            S.op("act", lambda E: E.activation(out=junk[:], in_=xs[sl][:], func=AF.Square, scale=1.0 / 32.0,
                                               accum_out=scol(c, 0)),
                 reads=[(f"xs{sl}", (c, "x"))], writes=[f"ms{c}", "junk"])
            rstd_ops(c, 0, 1, 2, f"ms{c}", f"rstd{c}")

        def fe_a2(c):
            sl = xslot[c]
            hs = c % 2
            S.op("dve", lambda E: E.scalar_tensor_tensor(out=htok[hs][:], in0=xs[sl][:], scalar=scol(c, 2),
                                                         in1=gbc[:], op0=ALU.mult, op1=ALU.mult),
                 reads=[(f"xs{sl}", (c, "x")), f"rstd{c}", "gbc"], writes=[(f"htok{hs}", c)])
            if chunks[c]["kind"] != "real":
                x_done[c] = True

        def fe_b(c):
            hs = c % 2

            def tr(E):
                ins = None
                for k in range(KD):
                    ins = E.transpose(tp[:, k * 128:(k + 1) * 128], htok[hs][:, k * 128:(k + 1) * 128], ident_bf[:])
                return ins
            S.op("pe", tr, reads=[(f"htok{hs}", c), "ident"], writes=[("tp", c)])
            dst, hk = hT_view(c)
            S.op("act", lambda E: E.copy(out=dst, in_=tp[:].rearrange("p (k t) -> p k t", k=KD)),
                 reads=[("tp", c)], writes=[(hk, c)])

        def fe_can(c):
            ensure_loaded(c + LA)
            return c < loaded[0]

        def fe_tick():
            cb = fe["b"]
            if cb >= NCH:
                return
            while fe["a1"] <= cb:
                fe_a1(fe["a1"]); fe["a1"] += 1
            while fe["a2"] <= cb:
                fe_a2(fe["a2"]); fe["a2"] += 1
            fe_b(cb)
            fe["b"] += 1
            if fe["a2"] == cb + 1 and cb + 1 < NCH and (fe["a1"] > cb + 1 or fe_can(cb + 1)):
                if fe["a1"] == cb + 1:
                    fe_a1(cb + 1); fe["a1"] += 1
                fe_a2(cb + 1); fe["a2"] += 1
            if fe["a1"] == cb + 2 and cb + 2 < NCH and fe_can(cb + 2):
                fe_a1(cb + 2); fe["a1"] += 1
            ensure_loaded(fe["a1"] + LA)

        def ensure_fe(upto):
            while fe["b"] <= min(upto, NCH - 1):
                fe_tick()

        def TM_a(c):
            ensure_fe(c)
            b = nxt("tok", NTOK)
            bk = f"tokb{b}"

            def mm(E):
                ins = None
                for k in range(KD):
                    lhsT, _ = hT_view(c, k)
                    ins = E.matmul(tokb[b][:], lhsT=lhsT, rhs=w_in_bf[:, k, 0:512],
                                   start=(k == 0), stop=(k == KD - 1))
                return ins
            S.op("pe", mm, reads=[(hT_view(c)[1], c)] + win_keys[0], writes=[bk])
            sa = c % NA
            S.op("act", lambda E: E.copy(out=a_tok[sa][:], in_=tokb[b][:]), reads=[bk], writes=[(f"a{sa}", c)])

        def TM_v(c):
            ensure_fe(c)
            b = nxt("tok", NTOK)
            bk = f"tokb{b}"

            def mm(E):
                ins = None
                for k in range(KD):
                    lhsT, _ = hT_view(c, k)
                    ins = E.matmul(tokb[b][:], lhsT=lhsT, rhs=w_in_bf[:, k, 3 * 512:4 * 512],
                                   start=(k == 0), stop=(k == KD - 1))
                return ins
            S.op("pe", mm, reads=[(hT_view(c)[1], c)] + win_keys[3], writes=[bk])
            S.op("dve", lambda E: E.bn_stats(out=stat[:, c * 16 + 4:c * 16 + 10], in_=tokb[b][:]),
                 reads=[bk], writes=[f"bst{c}"])
            S.op("dve", lambda E: E.bn_aggr(out=stat[:, c * 16 + 10:c * 16 + 12], in_=stat[:, c * 16 + 4:c * 16 + 10]),
                 reads=[f"bst{c}"], writes=[f"mv{c}"])
            rstd_ops(c, 11, 12, 13, f"mv{c}", f"rstdv{c}")

            def zfn():
                sz = chunks[c]["ridx"] % NZ
                S.op("dve", lambda E: E.tensor_scalar(out=z_tok[sz][:], in0=tokb[b][:], scalar1=scol(c, 10),
                                                      scalar2=scol(c, 13), op0=ALU.subtract, op1=ALU.mult),
                     reads=[bk, f"mv{c}", f"rstdv{c}"], writes=[(f"z{sz}", c)])
            return zfn

        def FM_block(g, kind, j):
            cg = dict(ga=1, u=2, gb=4)[kind]
            col0 = cg * 512 + j * 128
            slot = g % 2
            st = sts[g]
            b = nxt("fm", NFM)
            bk = f"fmb{b}"

            def mm(E):
                ins = None
                for k in range(KD):
                    ins = E.matmul(fmb[b][:], lhsT=w_in_bf[:, k, col0:col0 + 128], rhs=hT[slot][:, k, :],
                                   start=(k == 0), stop=(k == KD - 1))
                return ins
            wk = win_keys[cg] if cg == 1 else [f"win{cg}h{j // 2}"]
            S.op("pe", mm, reads=[(f"hT{slot}_{p}", st["c0"] + p) for p in range(4)] + wk, writes=[bk])
            if kind == "ga":
                S.op("act", lambda E: E.activation(out=sga[:, j, :], in_=fmb[b][:], func=AF.Silu),
                     reads=[bk], writes=[(f"sga{j}", g)])
            elif kind == "gb":
                s2 = 0
                S.op("act", lambda E: E.activation(out=sgb[s2][:], in_=fmb[b][:], func=AF.Silu),
                     reads=[bk], writes=[(f"sgb{s2}", (g, j))])
            else:
                s2 = 0
                S.op("dve", lambda E: E.tensor_tensor(out=usg[:, j, :], in0=fmb[b][:], in1=sgb[s2][:], op=ALU.mult),
                     reads=[bk, (f"sgb{s2}", (g, j))], writes=[(f"usg{j}", g)])

        def pm_idx(c, gq):
            ch = chunks[c]
            n = seg_chunks[ch["seg"]]
            if ch["e"] == 1:
                return 4 + ch["seg"] * 8 + gq
            if ch["e"] == n:
                return 4 + ch["seg"] * 8 + 4 + gq
            return gq

        def POOL(g, gq):
            st = sts[g]
            c0 = st["c0"]
            b = nxt("fm", NFM)
            bk = f"fmb{b}"

            def mm(E):
                first = True
                ins = None
                if st["has_prev"]:
                    ins = E.matmul(fmb[b][:, 0:8], lhsT=a_tok[(c0 - 1) % NA][:, gq * 128:(gq + 1) * 128],
                                   rhs=pm_bf[:, gq, 136:144], start=True, stop=False, skip_group_check=True)
                    first = False
                for j in range(4):
                    c = c0 + j
                    lo = 8 if j == 0 else 0
                    hi = 136 if j == 3 else 144
                    o0 = j * 128 - 8 + lo
                    last = (j == 3 and not st["has_next"])
                    ins = E.matmul(fmb[b][:, o0:o0 + (hi - lo)], lhsT=a_tok[c % NA][:, gq * 128:(gq + 1) * 128],
                                   rhs=pm_bf[:, pm_idx(c, gq), lo:hi], start=first, stop=last,
                                   skip_group_check=True)
                    first = False
                if st["has_next"]:
                    ins = E.matmul(fmb[b][:, 504:512], lhsT=a_tok[(c0 + 4) % NA][:, gq * 128:(gq + 1) * 128],
                                   rhs=pm_bf[:, gq, 0:8], start=False, stop=True, skip_group_check=True)
                return ins
            jr = range(-1 if st["has_prev"] else 0, 5 if st["has_next"] else 4)
            rd = [(f"a{(c0 + j) % NA}", c0 + j) for j in jr] + pm_keys
            S.op("pe", mm, reads=rd, writes=[bk])
            S.op("act", lambda E: E.copy(out=dT[:, gq, :], in_=fmb[b][:]), reads=[bk], writes=[(f"dT_{gq}", g)])

        def SGh(g, h):
            st = sts[g]
            ms = g % 2
            b = nxt("fm", NFM)
            bk = f"fmb{b}"

            def mm(E):
                ins = None
                for j in range(4):
                    c = st["c0"] + j
                    sz = chunks[c]["ridx"] % NZ
                    ins = E.matmul(fmb[b][:, j * 128:(j + 1) * 128], lhsT=z_tok[sz][:, h * 128:(h + 1) * 128],
                                   rhs=wst_bf[:, h, :], start=True, stop=True)
                return ins
            rd = [(f"z{chunks[st['c0'] + j]['ridx'] % NZ}", st["c0"] + j) for j in range(4)] + ["wst"]
            S.op("pe", mm, reads=rd, writes=[bk])
            ts = nxt("t1", 2)
            S.op("dve", lambda E: E.scalar_tensor_tensor(
                out=t1[ts][:].rearrange("p (j q) -> p j q", j=4),
                in0=fmb[b][:].rearrange("p (j q) -> p j q", j=4),
                scalar=cols[:, 12 + h:13 + h],
                in1=cmat[:, h, :].unsqueeze(1).to_broadcast([128, 4, 128]),
                op0=ALU.mult, op1=ALU.add), reads=[bk, "cols", "cmat"], writes=[(f"t1_{ts}", (g, h))])
            S.op("pool", lambda E: E.tensor_tensor(out=mixT[ms][:, 4 + h, :], in0=t1[ts][:], in1=usg[:, h, :],
                                                   op=ALU.mult),
                 reads=[(f"t1_{ts}", (g, h)), (f"usg{h}", g)], writes=[(f"mixT{ms}_{4 + h}", g)])

        def PWq(g, gq):
            ms = g % 2
            b = nxt("fm", NFM)
            bk = f"fmb{b}"
            S.op("pe", lambda E: E.matmul(fmb[b][:], lhsT=poolw_bf[:, gq, :], rhs=dT[:, gq, :],
                                          start=True, stop=True),
                 reads=[(f"dT_{gq}", g), "poolw"], writes=[bk])
            S.op("dve", lambda E: E.scalar_tensor_tensor(
                out=mixT[ms][:, gq, :], in0=fmb[b][:], scalar=cols[:, 8 + gq:9 + gq], in1=sga[:, gq, :],
                op0=ALU.mult, op1=ALU.mult), reads=[bk, "cols", (f"sga{gq}", g)], writes=[(f"mixT{ms}_{gq}", g)])

        def WO_main(g, j):
            st = sts[g]
            ms = g % 2
            c = st["c0"] + j
            sl = xslot[c]
            xk = f"xs{sl}"
            for hf in range(2):
                b = nxt("tok", NTOK)
                bk = f"tokb{b}"

                def mm(E):
                    ins = None
                    for k in range(KD):
                        ins = E.matmul(tokb[b][:], lhsT=mixT[ms][:, k, j * 128:(j + 1) * 128],
                                       rhs=w_out_bf[:, k, hf * 512:(hf + 1) * 512],
                                       start=(k == 0), stop=(k == KD - 1))
                    return ins
                S.op("pe", mm, reads=[(f"mixT{ms}_{k}", g) for k in range(KD)] + wout_keys, writes=[bk])
                S.op("dve", lambda E: E.tensor_tensor(out=xs[sl][:, hf * 512:(hf + 1) * 512], in0=tokb[b][:],
                                                      in1=xs[sl][:, hf * 512:(hf + 1) * 512], op=ALU.add),
                     reads=[bk, (xk, (c, "x" if hf == 0 else "xo0"))], writes=[(xk, (c, "xo0" if hf == 0 else "xo"))])
            S.op("act", lambda E: E.activation(out=junk[:], in_=xs[sl][:], func=AF.Square, scale=1.0 / 32.0,
                                               accum_out=scol(c, 3)),
                 reads=[(xk, (c, "xo"))], writes=[f"ms2_{c}", "junk"])
            rstd_ops(c, 3, 14, 15, f"ms2_{c}", f"rstd2_{c}")

            def tail():
                S.op("dve", lambda E: E.scalar_tensor_tensor(out=xs[sl][:], in0=xs[sl][:], scalar=scol(c, 15),
                                                             in1=fgb[:], op0=ALU.mult, op1=ALU.mult),
                     reads=[(xk, (c, "xo")), f"rstd2_{c}", "fgb"], writes=[(xk, (c, "y"))])
                r = chunks[c]["ridx"]
                S.op("sp", lambda E: E.dma_start(out=yout[r * 128:(r + 1) * 128, :], in_=xs[sl][:]),
                     reads=[(xk, (c, "y"))], dsem=f"st{sl}")
                x_done[c] = True
                ensure_loaded(fe["a1"] + LA)
            return tail

        load_ident()
        load_win(0)
        load_win(3)
        ensure_loaded(3)
        for c_ in range(3):
            fe_a1(c_)
        fe["a1"] = 3
        for c_ in range(2):
            fe_a2(c_)
        fe["a2"] = 2
        fe_tick()
        fe_tick()
        load_pm()
        load_win(1)
        late_loads = [lambda: (load_win_half(4, 0), load_win_half(2, 0)), load_small,
                      lambda: (load_win_half(4, 1), load_win_half(2, 1))]

        tma_done = [0]
        deferred = []
        for g, st in enumerate(sts):
            c0 = st["c0"]
            if g > 0:
                ensure_fe(c0 + 4)
            while tma_done[0] < c0 + 1:
                TM_a(tma_done[0])
                tma_done[0] += 1
            for j in range(4):
                if g == 0:
                    ensure_fe(c0 + j + 2)
                zfn = TM_v(c0 + j)
                if j < 3 or st["has_next"]:
                    TM_a(c0 + j + 1)
                    tma_done[0] = c0 + j + 2
                zfn()
                if late_loads:
                    late_loads.pop(0)()
            if g > 0:
                tails = []
                for j in range(4):
                    tails.append(WO_main(g - 1, j))
                    if j >= 1:
                        tails[j - 1]()
                deferred.append(tails[3])
            nxt_c0 = sts[g + 1]["c0"] if g + 1 < NST else None
            tick_target = (nxt_c0 + 3) if nxt_c0 is not None else NCH - 1
            n_ticks = max(0, tick_target - fe["b"] + 1)
            if n_ticks <= 3:
                tick_pos = {("P", 1), ("P", 3), ("SG", 0)}
            else:
                tick_pos = {("P", 0), ("P", 1), ("P", 2), ("P", 3), ("PW", 0), ("SG", 0), ("SG", 1)}
            base = [("ga", 0), ("P", 0), ("ga", 1), ("P", 1), ("ga", 2), ("P", 2), ("ga", 3), ("P", 3),
                    ("gb", 0), ("PW", 0), ("u", 0), ("PW", 1), ("gb", 1), ("SG", 0), ("u", 1), ("PW", 2),
                    ("gb", 2), ("SG", 1), ("u", 2), ("PW", 3), ("gb", 3), ("SG", 2), ("u", 3), ("tick4",),
                    ("SG", 3)]
            if g == NST - 1:
                base = [("gb", 0), ("P", 0), ("u", 0), ("P", 1), ("gb", 1), ("SG", 0), ("u", 1), ("P", 2),
                        ("gb", 2), ("SG", 1), ("u", 2), ("P", 3), ("gb", 3), ("SG", 2), ("u", 3), ("SG", 3),
                        ("ga", 0), ("PW", 0), ("ga", 1), ("PW", 1), ("ga", 2), ("PW", 2), ("ga", 3), ("PW", 3)]
            seq = []
            for it in base:
                seq.append(it)
                if it in tick_pos:
                    seq.append(("tick",))
            bi = 0
            for it in seq:
                if it[0] in ("ga", "gb", "u"):
                    FM_block(g, it[0], it[1])
                    if deferred:
                        deferred.pop(0)()
                    if g == 0 and bi == 0:
                        load_wout()
                    bi += 1
                elif it[0] == "P":
                    POOL(g, it[1])
                elif it[0] == "SG":
                    if g == 0 and it[1] == 0:
                        compute_cmat()
                    SGh(g, it[1])
                elif it[0] == "PW":
                    PWq(g, it[1])
                elif it[0] == "tick":
                    if nxt_c0 is not None and fe["b"] <= nxt_c0 + 3:
                        fe_tick()
                    elif nxt_c0 is None and fe["b"] < NCH:
                        fe_tick()
                elif it[0] == "tick4":
                    if nxt_c0 is not None:
                        ensure_fe(nxt_c0 + 3)
                        ensure_fe(nxt_c0 + 4)
        tails = []
        for j in range(4):
            tails.append(WO_main(NST - 1, j))
            if j >= 1:
                tails[j - 1]()
        tails[3]()
        assert not deferred
        S.final_wait("sp")
        print(f"[build] nwaits={S.nwaits} counts={ {k: v for k, v in S.cnt.items() if k in ('pe', 'act', 'dve', 'pool')} }")
    return nc, dict(NCH=NCH, nreal=nreal, chunks=chunks)


def _pool_mats(first_is_start, last_is_end):
    out = {}
    for gq, w in enumerate(WINDOWS):
        h = w // 2
        s = np.arange(128)[:, None]
        t = np.arange(128)[None, :]
        inwin = ((s >= t - h) & (s < t + h)).astype(np.float64)
        eye = (s == t).astype(np.float64)
        out[("cur", gq)] = (inwin / w - eye).astype(np.float32)
        out[("prev", gq)] = ((s - 128 >= t - h).astype(np.float64) / w).astype(np.float32)
        out[("next", gq)] = ((s + 128 < t + h).astype(np.float64) / w).astype(np.float32)
        cnt_start = np.minimum(w, t + h).astype(np.float64)
        out[("cur_start", gq)] = (inwin / cnt_start - eye).astype(np.float32)
        cnt_end = np.minimum(w, 128 - t + h).astype(np.float64)
        out[("cur_end", gq)] = (inwin / cnt_end - eye).astype(np.float32)
    return out


def _pm_for_core(seg_start_end):
    m = _pool_mats(None, None)

    def wide(cur_key, gq):
        return np.concatenate([m[("next", gq)][:, 120:128], m[(cur_key, gq)], m[("prev", gq)][:, 0:8]], axis=1)
    mats = [wide("cur", gq) for gq in range(4)]
    for (fs, le) in seg_start_end:
        mats += [wide("cur_start" if fs else "cur", gq) for gq in range(4)]
        mats += [wide("cur_end" if le else "cur", gq) for gq in range(4)]
    arr = np.stack(mats, axis=0)
    assert arr.shape == (NPM, 128, PMW)
    return np.ascontiguousarray(arr.transpose(1, 0, 2).reshape(128, -1))


def _ext_segment(x_seq, lo, hi):
    S_ = x_seq.shape[0]
    out = np.zeros((hi - lo + 256, x_seq.shape[1]), np.float32)
    a = max(lo - 128, 0)
    b = min(hi + 128, S_)
    out[a - (lo - 128):b - (lo - 128)] = x_seq[a:b]
    return out


def make_in_maps(x_prompt, x_sample, norm_g, w_in, pool_w, pool_scale, sgu_ln_g, sgu_ln_b,
                 w_spatial, b_spatial, w_out, final_g, ncores=NCORES, seg_chunks=SEG_CHUNKS):
    f = np.float32
    w_in = np.ascontiguousarray(w_in, f)
    w_out = np.ascontiguousarray(w_out, f)
    cols = np.concatenate([np.asarray(norm_g, f).reshape(8, 128).T, np.asarray(pool_scale, f).reshape(4, 128).T,
                           np.asarray(sgu_ln_g, f).reshape(4, 128).T], axis=1)
    cols = np.ascontiguousarray(cols)
    fgb = np.ascontiguousarray(np.broadcast_to(np.asarray(final_g, f)[None, :], (128, D)))
    gbc = np.ascontiguousarray(np.broadcast_to(np.asarray(norm_g, f)[None, :], (128, D)))
    lnbb = np.ascontiguousarray(np.broadcast_to(np.asarray(sgu_ln_b, f)[None, :], (128, 512)))
    bsb = np.ascontiguousarray(np.broadcast_to(np.asarray(b_spatial, f).reshape(1, 512), (128, 512)))
    poolw = np.ascontiguousarray(np.asarray(pool_w, f).transpose(1, 0, 2).reshape(128, 512))
    wst = np.ascontiguousarray(np.asarray(w_spatial, f).transpose(2, 0, 1).reshape(128, 512))
    ident = np.eye(128, dtype=f)
    n0, n1 = seg_chunks
    L0, L1 = n0 * 128, n1 * 128
    halves = x_prompt.shape[1] // L0
    in_maps = []
    for c in range(ncores):
        b, hf = c // halves, c % halves
        seg0 = _ext_segment(np.asarray(x_prompt[b], f), hf * L0, (hf + 1) * L0)
        seg1 = np.asarray(x_sample[c], f)
        xin = np.concatenate([seg0, seg1], axis=0)
        pm = _pm_for_core([(hf == 0, hf == halves - 1), (True, True)])
        in_maps.append(dict(xin=xin, w_in=w_in, w_out=w_out, pm=pm, poolw=poolw, wst=wst, ident=ident,
                            cols=cols, fgb=fgb, gbc=gbc, lnbb=lnbb, bsb=bsb))
    return in_maps


_NC_CACHE = {}


def kernel(x_prompt, x_sample, norm_g, w_in, pool_w, pool_scale, sgu_ln_g, sgu_ln_b,
           w_spatial, b_spatial, w_out, final_g):
    x_prompt = np.asarray(x_prompt)
    x_sample = np.asarray(x_sample)
    in_maps = make_in_maps(x_prompt, x_sample, norm_g, w_in, pool_w, pool_scale, sgu_ln_g, sgu_ln_b,
                           w_spatial, b_spatial, w_out, final_g)
    if "nc" not in _NC_CACHE:
        _NC_CACHE["nc"] = build_nc()[0]
    nc = _NC_CACHE["nc"]
    res = run_bass_kernel_spmd(nc, in_maps, core_ids=list(range(NCORES)))
    n0, n1 = SEG_CHUNKS
    L0, L1 = n0 * 128, n1 * 128
    y_prompt = np.empty(x_prompt.shape, np.float32)
    y_sample = np.empty(x_sample.shape, np.float32)
    halves = x_prompt.shape[1] // L0
    for c in range(NCORES):
        y = res.results[c]["yout"]
        b, hf = c // halves, c % halves
        y_prompt[b, hf * L0:(hf + 1) * L0] = y[:L0]
        y_sample[c] = y[L0:L0 + L1]
    return (y_prompt, y_sample)
```

```python
import numpy as np
from contextlib import ExitStack
import concourse.bass as bass
import concourse.mybir as mybir
from concourse.bass_utils import run_bass_kernel_spmd

F32 = mybir.dt.float32
BF16 = mybir.dt.bfloat16
AF = mybir.ActivationFunctionType
ALU = mybir.AluOpType

D = 1024
KD = 8
INW = 2560
EPS = 1e-6
WINDOWS = (2, 4, 8, 16)
NCORES = 8
SEG_CHUNKS = (16, 32)
LA = 2
NX = 13
HALO_FREE_SEGS = (0,)
NPM = 20
PMW = 144


class Sched:
    def __init__(self, nc, es):
        self.nc = nc
        self.es = es
        self.eng = dict(pe=nc.tensor, act=nc.scalar, dve=nc.vector, pool=nc.gpsimd, sp=nc.sync)
        self.sems = {}
        self.cnt = {}
        for e in ("pe", "act", "dve", "pool"):
            self.new_sem(e)
        self.waited = {e: {} for e in self.eng}
        self.lastw = {}
        self.readers = {}
        self.tags = {}
        self.nwaits = 0

    def new_sem(self, name):
        self.sems[name] = self.es.enter_context(self.nc.semaphore(name))
        self.cnt[name] = 0

    def _norm(self, reads, writes):
        rk, wk = [], []
        for it in reads:
            if isinstance(it, tuple):
                assert self.tags.get(it[0]) == it[1], f"stale read {it} has {self.tags.get(it[0])}"
                rk.append(it[0])
            else:
                rk.append(it)
        for it in writes:
            if isinstance(it, tuple):
                self.tags[it[0]] = it[1]
                wk.append(it[0])
            else:
                wk.append(it)
        return rk, wk

    def _waits(self, eng, reads, writes):
        need = {}

        def add(t):
            if t is None:
                return
            s, v = t
            if eng == "pe" and s == "pe":
                return
            if need.get(s, 0) < v:
                need[s] = v

        for k in reads:
            add(self.lastw.get(k))
        for k in writes:
            add(self.lastw.get(k))
            for s, v in self.readers.get(k, {}).items():
                add((s, v))
        E = self.eng[eng]
        w = self.waited[eng]
        for s, v in need.items():
            if w.get(s, 0) < v:
                E.wait_ge(self.sems[s], v)
                w[s] = v
                self.nwaits += 1
        return E

    def _record(self, t, reads, writes):
        for k in reads:
            r = self.readers.setdefault(k, {})
            if r.get(t[0], 0) < t[1]:
                r[t[0]] = t[1]
        for k in writes:
            self.lastw[k] = t
            self.readers[k] = {}

    def op(self, eng, fn, reads=(), writes=(), dsem=None):
        reads, writes = self._norm(reads, writes)
        E = self._waits(eng, reads, writes)
        ins = fn(E)
        if dsem is not None:
            if dsem not in self.sems:
                self.new_sem(dsem)
            sname, inc = dsem, 16
        else:
            sname, inc = eng, 1
        self.cnt[sname] += inc
        ins.then_inc(self.sems[sname], inc)
        t = (sname, self.cnt[sname])
        self._record(t, reads, writes)
        return t

    def dma_group(self, items, dsem, eng="sp"):
        if dsem not in self.sems:
            self.new_sem(dsem)
        for fn, reads, writes in items:
            E = self._waits(eng, reads, writes)
            ins = fn(E)
            self.cnt[dsem] += 16
            ins.then_inc(self.sems[dsem], 16)
        t = (dsem, self.cnt[dsem])
        for fn, reads, writes in items:
            self._record(t, reads, writes)
        return t

    def final_wait(self, eng):
        E = self.eng[eng]
        for s, v in self.cnt.items():
            if v > 0:
                E.wait_ge(self.sems[s], v)


def build_nc(seg_chunks=SEG_CHUNKS, nx=NX):
    nc = bass.Bass("TRN2", target_bir_lowering=False)
    chunks = []
    seg_first = []
    nreal = 0
    for s, n in enumerate(seg_chunks):
        assert n % 4 == 0
        seg_first.append(len(chunks))
        for e in range(n + 2):
            kind = "pre" if e == 0 else ("post" if e == n + 1 else "real")
            if kind != "real" and s in HALO_FREE_SEGS:
                continue
            chunks.append(dict(seg=s, kind=kind, e=e, ridx=None))
            if kind == "real":
                chunks[-1]["ridx"] = nreal
                nreal += 1
    NCH = len(chunks)
    sts = []
    for s, n in enumerate(seg_chunks):
        for i in range(n // 4):
            sts.append(dict(seg=s, c0=seg_first[s] + (0 if s in HALO_FREE_SEGS else 1) + 4 * i,
                            has_prev=not (s in HALO_FREE_SEGS and i == 0),
                            has_next=not (s in HALO_FREE_SEGS and i == n // 4 - 1)))
    NST = len(sts)
    for g, st in enumerate(sts):
        for j in range(4):
            chunks[st["c0"] + j]["st"] = g
            chunks[st["c0"] + j]["pos"] = j

    xin = nc.dram_tensor("xin", [NCH * 128, D], F32, kind="ExternalInput").ap()
    w_in = nc.dram_tensor("w_in", [D, INW], F32, kind="ExternalInput").ap()
    w_out = nc.dram_tensor("w_out", [D, D], F32, kind="ExternalInput").ap()
    d_pm = nc.dram_tensor("pm", [128, NPM * PMW], F32, kind="ExternalInput").ap()
    d_poolw = nc.dram_tensor("poolw", [128, 512], F32, kind="ExternalInput").ap()
    d_wst = nc.dram_tensor("wst", [128, 512], F32, kind="ExternalInput").ap()
    d_ident = nc.dram_tensor("ident", [128, 128], F32, kind="ExternalInput").ap()
    d_cols = nc.dram_tensor("cols", [128, 16], F32, kind="ExternalInput").ap()
    d_fg = nc.dram_tensor("fgb", [128, D], F32, kind="ExternalInput").ap()
    d_gb = nc.dram_tensor("gbc", [128, D], F32, kind="ExternalInput").ap()
    d_lnb = nc.dram_tensor("lnbb", [128, 512], F32, kind="ExternalInput").ap()
    d_bs = nc.dram_tensor("bsb", [128, 512], F32, kind="ExternalInput").ap()
    yout = nc.dram_tensor("yout", [nreal * 128, D], F32, kind="ExternalOutput").ap()

    es = ExitStack()
    with es:
        S = Sched(nc, es)

        def sb(name, shape, dt):
            return es.enter_context(nc.sbuf_tensor("s_" + name, shape, dt))

        xs = [sb(f"xs{i}", [128, D], F32) for i in range(nx)]
        w_in_bf = sb("w_in_bf", [128, KD, INW], BF16)
        w_out_bf = sb("w_out_bf", [128, KD, D], BF16)
        pm_bf = sb("pm_bf", [128, NPM, PMW], BF16)
        poolw_bf = sb("poolw_bf", [128, 4, 128], BF16)
        wst_bf = sb("wst_bf", [128, 4, 128], BF16)
        ident_bf = sb("ident_bf", [128, 128], BF16)
        lnb_bf = sb("lnb_bf", [128, 512], BF16)
        cols = sb("cols_sb", [128, 16], F32)
        fgb = sb("fgb_sb", [128, D], F32)
        gbc = sb("gbc_sb", [128, D], F32)
        cmat = sb("cmat", [128, 4, 128], F32)
        neghalf = sb("neghalf", [128, 1], F32)
        stat = sb("stat", [128, NCH * 16], F32)
        junk = sb("junk", [128, D], BF16)
        htok = [sb(f"htok{i}", [128, D], BF16) for i in range(2)]
        hT = [sb(f"hT{i}", [128, KD, 512], BF16) for i in range(2)]
        hTh = sb("hTh", [128, KD, 128], BF16)
        NA = 6
        a_tok = [sb(f"a_tok{i}", [128, 512], BF16) for i in range(NA)]
        NZ = 4
        z_tok = [sb(f"z_tok{i}", [128, 512], BF16) for i in range(NZ)]
        sga = sb("sga", [128, 4, 512], F32)
        sgb = [sb(f"sgb{i}", [128, 512], F32) for i in range(1)]
        usg = sb("usg", [128, 4, 512], F32)
        t1 = [sb(f"t1_{i}", [128, 512], F32) for i in range(2)]
        dT = sb("dT", [128, 4, 512], BF16)
        mixT = [sb(f"mixT{i}", [128, KD, 512], BF16) for i in range(2)]

        tp = es.enter_context(nc.psum_tensor("tp", [128, D], BF16))
        NTOK, NFM = 4, 3
        tokb = [es.enter_context(nc.psum_tensor(f"tokb{i}", [128, 512], F32)) for i in range(NTOK)]
        fmb = [es.enter_context(nc.psum_tensor(f"fmb{i}", [128, 512], F32)) for i in range(NFM)]
        ring = dict(tok=0, fm=0, xs=0, t1=0)

        def nxt(name, n):
            v = ring[name]
            ring[name] = (v + 1) % n
            return v

        S.op("pool", lambda E: E.memset(neghalf[:], -0.5), writes=["neghalf"])
        S.op("act", lambda E: E.activation(out=junk[:, 0:1], in_=neghalf[:], func=AF.Silu),
             reads=["neghalf"], writes=["junk"])
        S.dma_group([
            (lambda E: E.dma_start(out=cols[:], in_=d_cols[:, :]), [], ["cols"]),
            (lambda E: E.dma_start(out=gbc[:], in_=d_gb[:, :]), [], ["gbc"]),
        ], "cst")

        win_keys = {cg: [f"win{cg}"] for cg in range(5)}
        wout_keys = ["wout"]
        pm_keys = ["pm"]

        def load_win(cg):
            items = []
            for kp in range(0, KD, 2):
                src = w_in[kp * 128:(kp + 2) * 128, cg * 512:(cg + 1) * 512].rearrange("(k p) c -> p k c", p=128)
                dst = w_in_bf[:, kp:kp + 2, cg * 512:(cg + 1) * 512]
                items.append((lambda E, dst=dst, src=src: E.dma_start(out=dst, in_=src), [], [f"win{cg}"]))
            S.dma_group(items, f"w{cg}", eng="pool")

        def load_win_half(cg, hf):
            items = []
            c0_ = cg * 512 + hf * 256
            for kp in range(0, KD, 4):
                src = w_in[kp * 128:(kp + 4) * 128, c0_:c0_ + 256].rearrange("(k p) c -> p k c", p=128)
                dst = w_in_bf[:, kp:kp + 4, c0_:c0_ + 256]
                items.append((lambda E, dst=dst, src=src: E.dma_start(out=dst, in_=src), [], [f"win{cg}h{hf}"]))
            S.dma_group(items, f"w{cg}h{hf}", eng="pool")

        def load_wout():
            items = []
            for kp in range(0, KD, 2):
                src = w_out[kp * 128:(kp + 2) * 128, :].rearrange("(k p) c -> p k c", p=128)
                dst = w_out_bf[:, kp:kp + 2, :]
                items.append((lambda E, dst=dst, src=src: E.dma_start(out=dst, in_=src), [], ["wout"]))
            S.dma_group(items, "wo", eng="pool")
            S.dma_group([(lambda E: E.dma_start(out=fgb[:], in_=d_fg[:, :]), [], ["fgb"])], "cst3")

        def load_ident():
            S.dma_group([(lambda E: E.dma_start(out=ident_bf[:], in_=d_ident[:, :]), [], ["ident"])], "wid", eng="pool")

        def load_pm():
            items = []
            for j in range(0, NPM, 5):
                n = min(5, NPM - j)
                dst = pm_bf[:, j:j + n, :].rearrange("p m t -> p (m t)")
                src = d_pm[:, j * PMW:(j + n) * PMW]
                items.append((lambda E, dst=dst, src=src: E.dma_start(out=dst, in_=src), [], ["pm"]))
            S.dma_group(items, "wpm", eng="pool")

        def load_small():
            items = [
                (lambda E: E.dma_start(out=wst_bf[:].rearrange("p h q -> p (h q)"), in_=d_wst[:, :]), [], ["wst"]),
                (lambda E: E.dma_start(out=lnb_bf[:], in_=d_lnb[:, :]), [], ["lnb"]),
                (lambda E: E.dma_start(out=poolw_bf[:].rearrange("p g d -> p (g d)"), in_=d_poolw[:, :]), [], ["poolw"]),
            ]
            S.dma_group(items, "wsm", eng="pool")
            S.dma_group([
                (lambda E: E.dma_start(out=t1[0][:], in_=d_bs[:, :]), [], ["t1_0"]),
            ], "cst2")

        def compute_cmat():
            fb = nxt("fm", NFM)

            def cm_mm(E):
                ins = None
                for h in range(4):
                    ins = E.matmul(fmb[fb][:, h * 128:(h + 1) * 128], lhsT=lnb_bf[:, h * 128:(h + 1) * 128],
                                   rhs=wst_bf[:, h, :], start=True, stop=True)
                return ins
            S.op("pe", cm_mm, reads=["lnb", "wst"], writes=[f"fmb{fb}"])
            S.op("dve", lambda E: E.tensor_tensor(out=cmat[:].rearrange("p h q -> p (h q)"), in0=fmb[fb][:],
                                                  in1=t1[0][:], op=ALU.add),
                 reads=[f"fmb{fb}", "t1_0"], writes=["cmat"])

        slot_occ = {}
        x_done = {}

        xslot = {}
        loaded = [0]

        def ensure_loaded(upto):
            while loaded[0] <= min(upto, NCH - 1):
                c = loaded[0]
                sl = ring["xs"]
                occ = slot_occ.get(sl)
                if occ is not None and not x_done.get(occ, False):
                    break
                nxt("xs", nx)
                xslot[c] = sl
                slot_occ[sl] = c
                thr = []
                if 4 <= c < 6:
                    thr = win_keys[0]
                elif 6 <= c < 9:
                    thr = win_keys[3]
                elif 9 <= c < 13:
                    thr = ["win2h1"]
                S.op("sp", lambda E: E.dma_start(out=xs[sl][:], in_=xin[c * 128:(c + 1) * 128, :]),
                     reads=thr, writes=[(f"xs{sl}", (c, "x"))], dsem=f"ld{sl}")
                loaded[0] += 1

        def scol(c, j):
            return stat[:, c * 16 + j:c * 16 + j + 1]

        def rstd_ops(c, jin, jtmp, jout, key_in, key_out):
            S.op("pool", lambda E: E.tensor_scalar(out=scol(c, jtmp), in0=scol(c, jin), scalar1=1.0, scalar2=EPS,
                                                   op0=ALU.mult, op1=ALU.add), reads=[key_in], writes=[f"tmp{c}_{jtmp}"])
            S.op("pool", lambda E: E.tensor_tensor(out=scol(c, jout), in0=scol(c, jtmp), in1=neghalf[:],
                                                   op=ALU.pow), reads=[f"tmp{c}_{jtmp}", "neghalf"], writes=[key_out])

        def hT_view(c, k=None):
            ch = chunks[c]
            if ch["kind"] == "real":
                t = hT[ch["st"] % 2]
                lo = ch["pos"] * 128
                key = f"hT{ch['st'] % 2}_{ch['pos']}"
            else:
                t = hTh
                lo = 0
                key = "hTh"
            if k is None:
                return t[:, :, lo:lo + 128], key
            return t[:, k, lo:lo + 128], key

        fe = dict(a1=0, a2=0, b=0)

        def fe_a1(c):
            ensure_loaded(c + LA)
            assert c < loaded[0], f"x ring too small: chunk {c} not loadable"
            sl = xslot[c]
            S.op("act", lambda E: E.activation(out=junk[:], in_=xs[sl][:], func=AF.Square, scale=1.0 / 32.0,
                                               accum_out=scol(c, 0)),
                 reads=[(f"xs{sl}", (c, "x"))], writes=[f"ms{c}", "junk"])
            rstd_ops(c, 0, 1, 2, f"ms{c}", f"rstd{c}")

        def fe_a2(c):
            sl = xslot[c]
            hs = c % 2
            S.op("dve", lambda E: E.scalar_tensor_tensor(out=htok[hs][:], in0=xs[sl][:], scalar=scol(c, 2),
                                                         in1=gbc[:], op0=ALU.mult, op1=ALU.mult),
                 reads=[(f"xs{sl}", (c, "x")), f"rstd{c}", "gbc"], writes=[(f"htok{hs}", c)])
            if chunks[c]["kind"] != "real":
                x_done[c] = True

        def fe_b(c):
            hs = c % 2

            def tr(E):
                ins = None
                for k in range(KD):
                    ins = E.transpose(tp[:, k * 128:(k + 1) * 128], htok[hs][:, k * 128:(k + 1) * 128], ident_bf[:])
                return ins
            S.op("pe", tr, reads=[(f"htok{hs}", c), "ident"], writes=[("tp", c)])
            dst, hk = hT_view(c)
            S.op("act", lambda E: E.copy(out=dst, in_=tp[:].rearrange("p (k t) -> p k t", k=KD)),
                 reads=[("tp", c)], writes=[(hk, c)])

        def fe_can(c):
            ensure_loaded(c + LA)
            return c < loaded[0]

        def fe_tick():
            cb = fe["b"]
            if cb >= NCH:
                return
            while fe["a1"] <= cb:
                fe_a1(fe["a1"]); fe["a1"] += 1
            while fe["a2"] <= cb:
                fe_a2(fe["a2"]); fe["a2"] += 1
            fe_b(cb)
            fe["b"] += 1
            if fe["a2"] == cb + 1 and cb + 1 < NCH and (fe["a1"] > cb + 1 or fe_can(cb + 1)):
                if fe["a1"] == cb + 1:
                    fe_a1(cb + 1); fe["a1"] += 1
                fe_a2(cb + 1); fe["a2"] += 1
            if fe["a1"] == cb + 2 and cb + 2 < NCH and fe_can(cb + 2):
                fe_a1(cb + 2); fe["a1"] += 1
            ensure_loaded(fe["a1"] + LA)

        def ensure_fe(upto):
            while fe["b"] <= min(upto, NCH - 1):
                fe_tick()

        def TM_a(c):
            ensure_fe(c)
            b = nxt("tok", NTOK)
            bk = f"tokb{b}"

            def mm(E):
                ins = None
                for k in range(KD):
                    lhsT, _ = hT_view(c, k)
                    ins = E.matmul(tokb[b][:], lhsT=lhsT, rhs=w_in_bf[:, k, 0:512],
                                   start=(k == 0), stop=(k == KD - 1))
                return ins
            S.op("pe", mm, reads=[(hT_view(c)[1], c)] + win_keys[0], writes=[bk])
            sa = c % NA
            S.op("act", lambda E: E.copy(out=a_tok[sa][:], in_=tokb[b][:]), reads=[bk], writes=[(f"a{sa}", c)])

        def TM_v(c):
            ensure_fe(c)
            b = nxt("tok", NTOK)
            bk = f"tokb{b}"

            def mm(E):
                ins = None
                for k in range(KD):
                    lhsT, _ = hT_view(c, k)
                    ins = E.matmul(tokb[b][:], lhsT=lhsT, rhs=w_in_bf[:, k, 3 * 512:4 * 512],
                                   start=(k == 0), stop=(k == KD - 1))
                return ins
            S.op("pe", mm, reads=[(hT_view(c)[1], c)] + win_keys[3], writes=[bk])
            S.op("dve", lambda E: E.bn_stats(out=stat[:, c * 16 + 4:c * 16 + 10], in_=tokb[b][:]),
                 reads=[bk], writes=[f"bst{c}"])
            S.op("dve", lambda E: E.bn_aggr(out=stat[:, c * 16 + 10:c * 16 + 12], in_=stat[:, c * 16 + 4:c * 16 + 10]),
                 reads=[f"bst{c}"], writes=[f"mv{c}"])
            rstd_ops(c, 11, 12, 13, f"mv{c}", f"rstdv{c}")

            def zfn():
                sz = chunks[c]["ridx"] % NZ
                S.op("dve", lambda E: E.tensor_scalar(out=z_tok[sz][:], in0=tokb[b][:], scalar1=scol(c, 10),
                                                      scalar2=scol(c, 13), op0=ALU.subtract, op1=ALU.mult),
                     reads=[bk, f"mv{c}", f"rstdv{c}"], writes=[(f"z{sz}", c)])
            return zfn

        def FM_block(g, kind, j):
            cg = dict(ga=1, u=2, gb=4)[kind]
            col0 = cg * 512 + j * 128
            slot = g % 2
            st = sts[g]
            b = nxt("fm", NFM)
            bk = f"fmb{b}"

            def mm(E):
                ins = None
                for k in range(KD):
                    ins = E.matmul(fmb[b][:], lhsT=w_in_bf[:, k, col0:col0 + 128], rhs=hT[slot][:, k, :],
                                   start=(k == 0), stop=(k == KD - 1))
                return ins
            wk = win_keys[cg] if cg == 1 else [f"win{cg}h{j // 2}"]
            S.op("pe", mm, reads=[(f"hT{slot}_{p}", st["c0"] + p) for p in range(4)] + wk, writes=[bk])
            if kind == "ga":
                S.op("act", lambda E: E.activation(out=sga[:, j, :], in_=fmb[b][:], func=AF.Silu),
                     reads=[bk], writes=[(f"sga{j}", g)])
            elif kind == "gb":
                s2 = 0
                S.op("act", lambda E: E.activation(out=sgb[s2][:], in_=fmb[b][:], func=AF.Silu),
                     reads=[bk], writes=[(f"sgb{s2}", (g, j))])
            else:
                s2 = 0
                S.op("dve", lambda E: E.tensor_tensor(out=usg[:, j, :], in0=fmb[b][:], in1=sgb[s2][:], op=ALU.mult),
                     reads=[bk, (f"sgb{s2}", (g, j))], writes=[(f"usg{j}", g)])

        def pm_idx(c, gq):
            ch = chunks[c]
            n = seg_chunks[ch["seg"]]
            if ch["e"] == 1:
                return 4 + ch["seg"] * 8 + gq
            if ch["e"] == n:
                return 4 + ch["seg"] * 8 + 4 + gq
            return gq

        def POOL(g, gq):
            st = sts[g]
            c0 = st["c0"]
            b = nxt("fm", NFM)
            bk = f"fmb{b}"

            def mm(E):
                first = True
                ins = None
                if st["has_prev"]:
                    ins = E.matmul(fmb[b][:, 0:8], lhsT=a_tok[(c0 - 1) % NA][:, gq * 128:(gq + 1) * 128],
                                   rhs=pm_bf[:, gq, 136:144], start=True, stop=False, skip_group_check=True)
                    first = False
                for j in range(4):
                    c = c0 + j
                    lo = 8 if j == 0 else 0
                    hi = 136 if j == 3 else 144
                    o0 = j * 128 - 8 + lo
                    last = (j == 3 and not st["has_next"])
                    ins = E.matmul(fmb[b][:, o0:o0 + (hi - lo)], lhsT=a_tok[c % NA][:, gq * 128:(gq + 1) * 128],
                                   rhs=pm_bf[:, pm_idx(c, gq), lo:hi], start=first, stop=last,
                                   skip_group_check=True)
                    first = False
                if st["has_next"]:
                    ins = E.matmul(fmb[b][:, 504:512], lhsT=a_tok[(c0 + 4) % NA][:, gq * 128:(gq + 1) * 128],
                                   rhs=pm_bf[:, gq, 0:8], start=False, stop=True, skip_group_check=True)
                return ins
            jr = range(-1 if st["has_prev"] else 0, 5 if st["has_next"] else 4)
            rd = [(f"a{(c0 + j) % NA}", c0 + j) for j in jr] + pm_keys
            S.op("pe", mm, reads=rd, writes=[bk])
            S.op("act", lambda E: E.copy(out=dT[:, gq, :], in_=fmb[b][:]), reads=[bk], writes=[(f"dT_{gq}", g)])

        def SGh(g, h):
            st = sts[g]
            ms = g % 2
            b = nxt("fm", NFM)
            bk = f"fmb{b}"

            def mm(E):
                ins = None
                for j in range(4):
                    c = st["c0"] + j
                    sz = chunks[c]["ridx"] % NZ
                    ins = E.matmul(fmb[b][:, j * 128:(j + 1) * 128], lhsT=z_tok[sz][:, h * 128:(h + 1) * 128],
                                   rhs=wst_bf[:, h, :], start=True, stop=True)
                return ins
            rd = [(f"z{chunks[st['c0'] + j]['ridx'] % NZ}", st["c0"] + j) for j in range(4)] + ["wst"]
            S.op("pe", mm, reads=rd, writes=[bk])
            ts = nxt("t1", 2)
            S.op("dve", lambda E: E.scalar_tensor_tensor(
                out=t1[ts][:].rearrange("p (j q) -> p j q", j=4),
                in0=fmb[b][:].rearrange("p (j q) -> p j q", j=4),
                scalar=cols[:, 12 + h:13 + h],
                in1=cmat[:, h, :].unsqueeze(1).to_broadcast([128, 4, 128]),
                op0=ALU.mult, op1=ALU.add), reads=[bk, "cols", "cmat"], writes=[(f"t1_{ts}", (g, h))])
            S.op("pool", lambda E: E.tensor_tensor(out=mixT[ms][:, 4 + h, :], in0=t1[ts][:], in1=usg[:, h, :],
                                                   op=ALU.mult),
                 reads=[(f"t1_{ts}", (g, h)), (f"usg{h}", g)], writes=[(f"mixT{ms}_{4 + h}", g)])

        def PWq(g, gq):
            ms = g % 2
            b = nxt("fm", NFM)
            bk = f"fmb{b}"
            S.op("pe", lambda E: E.matmul(fmb[b][:], lhsT=poolw_bf[:, gq, :], rhs=dT[:, gq, :],
                                          start=True, stop=True),
                 reads=[(f"dT_{gq}", g), "poolw"], writes=[bk])
            S.op("dve", lambda E: E.scalar_tensor_tensor(
                out=mixT[ms][:, gq, :], in0=fmb[b][:], scalar=cols[:, 8 + gq:9 + gq], in1=sga[:, gq, :],
                op0=ALU.mult, op1=ALU.mult), reads=[bk, "cols", (f"sga{gq}", g)], writes=[(f"mixT{ms}_{gq}", g)])

        def WO_main(g, j):
            st = sts[g]
            ms = g % 2
            c = st["c0"] + j
            sl = xslot[c]
            xk = f"xs{sl}"
            for hf in range(2):
                b = nxt("tok", NTOK)
                bk = f"tokb{b}"

                def mm(E):
                    ins = None
                    for k in range(KD):
                        ins = E.matmul(tokb[b][:], lhsT=mixT[ms][:, k, j * 128:(j + 1) * 128],
                                       rhs=w_out_bf[:, k, hf * 512:(hf + 1) * 512],
                                       start=(k == 0), stop=(k == KD - 1))
                    return ins
                S.op("pe", mm, reads=[(f"mixT{ms}_{k}", g) for k in range(KD)] + wout_keys, writes=[bk])
                S.op("dve", lambda E: E.tensor_tensor(out=xs[sl][:, hf * 512:(hf + 1) * 512], in0=tokb[b][:],
                                                      in1=xs[sl][:, hf * 512:(hf + 1) * 512], op=ALU.add),
                     reads=[bk, (xk, (c, "x" if hf == 0 else "xo0"))], writes=[(xk, (c, "xo0" if hf == 0 else "xo"))])
            S.op("act", lambda E: E.activation(out=junk[:], in_=xs[sl][:], func=AF.Square, scale=1.0 / 32.0,
                                               accum_out=scol(c, 3)),
                 reads=[(xk, (c, "xo"))], writes=[f"ms2_{c}", "junk"])
            rstd_ops(c, 3, 14, 15, f"ms2_{c}", f"rstd2_{c}")

            def tail():
                S.op("dve", lambda E: E.scalar_tensor_tensor(out=xs[sl][:], in0=xs[sl][:], scalar=scol(c, 15),
                                                             in1=fgb[:], op0=ALU.mult, op1=ALU.mult),
                     reads=[(xk, (c, "xo")), f"rstd2_{c}", "fgb"], writes=[(xk, (c, "y"))])
                r = chunks[c]["ridx"]
                S.op("sp", lambda E: E.dma_start(out=yout[r * 128:(r + 1) * 128, :], in_=xs[sl][:]),
                     reads=[(xk, (c, "y"))], dsem=f"st{sl}")
                x_done[c] = True
                ensure_loaded(fe["a1"] + LA)
            return tail

        load_ident()
        load_win(0)
        load_win(3)
        ensure_loaded(3)
        for c_ in range(3):
            fe_a1(c_)
        fe["a1"] = 3
        for c_ in range(2):
            fe_a2(c_)
        fe["a2"] = 2
        fe_tick()
        fe_tick()
        load_pm()
        load_win(1)
        late_loads = [lambda: (load_win_half(4, 0), load_win_half(2, 0)), load_small,
                      lambda: (load_win_half(4, 1), load_win_half(2, 1))]

        tma_done = [0]
        deferred = []
        for g, st in enumerate(sts):
            c0 = st["c0"]
            if g > 0:
                ensure_fe(c0 + 4)
            while tma_done[0] < c0 + 1:
                TM_a(tma_done[0])
                tma_done[0] += 1
            for j in range(4):
                if g == 0:
                    ensure_fe(c0 + j + 2)
                zfn = TM_v(c0 + j)
                if j < 3 or st["has_next"]:
                    TM_a(c0 + j + 1)
                    tma_done[0] = c0 + j + 2
                zfn()
                if late_loads:
                    late_loads.pop(0)()
            if g > 0:
                tails = []
                for j in range(4):
                    tails.append(WO_main(g - 1, j))
                    if j >= 1:
                        tails[j - 1]()
                deferred.append(tails[3])
            nxt_c0 = sts[g + 1]["c0"] if g + 1 < NST else None
            tick_target = (nxt_c0 + 3) if nxt_c0 is not None else NCH - 1
            n_ticks = max(0, tick_target - fe["b"] + 1)
            if n_ticks <= 3:
                tick_pos = {("P", 1), ("P", 3), ("SG", 0)}
            else:
                tick_pos = {("P", 0), ("P", 1), ("P", 2), ("P", 3), ("PW", 0), ("SG", 0), ("SG", 1)}
            base = [("ga", 0), ("P", 0), ("ga", 1), ("P", 1), ("ga", 2), ("P", 2), ("ga", 3), ("P", 3),
                    ("gb", 0), ("PW", 0), ("u", 0), ("PW", 1), ("gb", 1), ("SG", 0), ("u", 1), ("PW", 2),
                    ("gb", 2), ("SG", 1), ("u", 2), ("PW", 3), ("gb", 3), ("SG", 2), ("u", 3), ("tick4",),
                    ("SG", 3)]
            if g == NST - 1:
                base = [("gb", 0), ("P", 0), ("u", 0), ("P", 1), ("gb", 1), ("SG", 0), ("u", 1), ("P", 2),
                        ("gb", 2), ("SG", 1), ("u", 2), ("P", 3), ("gb", 3), ("SG", 2), ("u", 3), ("SG", 3),
                        ("ga", 0), ("PW", 0), ("ga", 1), ("PW", 1), ("ga", 2), ("PW", 2), ("ga", 3), ("PW", 3)]
            seq = []
            for it in base:
                seq.append(it)
                if it in tick_pos:
                    seq.append(("tick",))
            bi = 0
            for it in seq:
                if it[0] in ("ga", "gb", "u"):
                    FM_block(g, it[0], it[1])
                    if deferred:
                        deferred.pop(0)()
                    if g == 0 and bi == 0:
                        load_wout()
                    bi += 1
                elif it[0] == "P":
                    POOL(g, it[1])
                elif it[0] == "SG":
                    if g == 0 and it[1] == 0:
                        compute_cmat()
                    SGh(g, it[1])
                elif it[0] == "PW":
                    PWq(g, it[1])
                elif it[0] == "tick":
                    if nxt_c0 is not None and fe["b"] <= nxt_c0 + 3:
                        fe_tick()
                    elif nxt_c0 is None and fe["b"] < NCH:
                        fe_tick()
                elif it[0] == "tick4":
                    if nxt_c0 is not None:
                        ensure_fe(nxt_c0 + 3)
                        ensure_fe(nxt_c0 + 4)
        tails = []
        for j in range(4):
            tails.append(WO_main(NST - 1, j))
            if j >= 1:
                tails[j - 1]()
        tails[3]()
        assert not deferred
        S.final_wait("sp")
        print(f"[build] nwaits={S.nwaits} counts={ {k: v for k, v in S.cnt.items() if k in ('pe', 'act', 'dve', 'pool')} }")
    return nc, dict(NCH=NCH, nreal=nreal, chunks=chunks)


def _pool_mats(first_is_start, last_is_end):
    out = {}
    for gq, w in enumerate(WINDOWS):
        h = w // 2
        s = np.arange(128)[:, None]
        t = np.arange(128)[None, :]
        inwin = ((s >= t - h) & (s < t + h)).astype(np.float64)
        eye = (s == t).astype(np.float64)
        out[("cur", gq)] = (inwin / w - eye).astype(np.float32)
        out[("prev", gq)] = ((s - 128 >= t - h).astype(np.float64) / w).astype(np.float32)
        out[("next", gq)] = ((s + 128 < t + h).astype(np.float64) / w).astype(np.float32)
        cnt_start = np.minimum(w, t + h).astype(np.float64)
        out[("cur_start", gq)] = (inwin / cnt_start - eye).astype(np.float32)
        cnt_end = np.minimum(w, 128 - t + h).astype(np.float64)
        out[("cur_end", gq)] = (inwin / cnt_end - eye).astype(np.float32)
    return out


def _pm_for_core(seg_start_end):
    m = _pool_mats(None, None)

    def wide(cur_key, gq):
        return np.concatenate([m[("next", gq)][:, 120:128], m[(cur_key, gq)], m[("prev", gq)][:, 0:8]], axis=1)
    mats = [wide("cur", gq) for gq in range(4)]
    for (fs, le) in seg_start_end:
        mats += [wide("cur_start" if fs else "cur", gq) for gq in range(4)]
        mats += [wide("cur_end" if le else "cur", gq) for gq in range(4)]
    arr = np.stack(mats, axis=0)
    assert arr.shape == (NPM, 128, PMW)
    return np.ascontiguousarray(arr.transpose(1, 0, 2).reshape(128, -1))


def _ext_segment(x_seq, lo, hi):
    S_ = x_seq.shape[0]
    out = np.zeros((hi - lo + 256, x_seq.shape[1]), np.float32)
    a = max(lo - 128, 0)
    b = min(hi + 128, S_)
    out[a - (lo - 128):b - (lo - 128)] = x_seq[a:b]
    return out


def make_in_maps(x_prompt, x_sample, norm_g, w_in, pool_w, pool_scale, sgu_ln_g, sgu_ln_b,
                 w_spatial, b_spatial, w_out, final_g, ncores=NCORES, seg_chunks=SEG_CHUNKS):
    f = np.float32
    w_in = np.ascontiguousarray(w_in, f)
    w_out = np.ascontiguousarray(w_out, f)
    cols = np.concatenate([np.asarray(norm_g, f).reshape(8, 128).T, np.asarray(pool_scale, f).reshape(4, 128).T,
                           np.asarray(sgu_ln_g, f).reshape(4, 128).T], axis=1)
    cols = np.ascontiguousarray(cols)
    fgb = np.ascontiguousarray(np.broadcast_to(np.asarray(final_g, f)[None, :], (128, D)))
    gbc = np.ascontiguousarray(np.broadcast_to(np.asarray(norm_g, f)[None, :], (128, D)))
    lnbb = np.ascontiguousarray(np.broadcast_to(np.asarray(sgu_ln_b, f)[None, :], (128, 512)))
    bsb = np.ascontiguousarray(np.broadcast_to(np.asarray(b_spatial, f).reshape(1, 512), (128, 512)))
    poolw = np.ascontiguousarray(np.asarray(pool_w, f).transpose(1, 0, 2).reshape(128, 512))
    wst = np.ascontiguousarray(np.asarray(w_spatial, f).transpose(2, 0, 1).reshape(128, 512))
    ident = np.eye(128, dtype=f)
    ns, npr = seg_chunks
    Lp = npr * 128
    halves = x_prompt.shape[1] // Lp
    in_maps = []
    for c in range(ncores):
        b, hf = c // halves, c % halves
        seg_s = np.asarray(x_sample[c], f)
        seg_p = _ext_segment(np.asarray(x_prompt[b], f), hf * Lp, (hf + 1) * Lp)
        xin = np.concatenate([seg_s, seg_p], axis=0)
        pm = _pm_for_core([(True, True), (hf == 0, hf == halves - 1)])
        in_maps.append(dict(xin=xin, w_in=w_in, w_out=w_out, pm=pm, poolw=poolw, wst=wst, ident=ident,
                            cols=cols, fgb=fgb, gbc=gbc, lnbb=lnbb, bsb=bsb))
    return in_maps


_NC_CACHE = {}


def kernel(x_prompt, x_sample, norm_g, w_in, pool_w, pool_scale, sgu_ln_g, sgu_ln_b,
           w_spatial, b_spatial, w_out, final_g):
    x_prompt = np.asarray(x_prompt)
    x_sample = np.asarray(x_sample)
    in_maps = make_in_maps(x_prompt, x_sample, norm_g, w_in, pool_w, pool_scale, sgu_ln_g, sgu_ln_b,
                           w_spatial, b_spatial, w_out, final_g)
    if "nc" not in _NC_CACHE:
        _NC_CACHE["nc"] = build_nc()[0]
    nc = _NC_CACHE["nc"]
    res = run_bass_kernel_spmd(nc, in_maps, core_ids=list(range(NCORES)))
    ns, npr = SEG_CHUNKS
    Ls, Lp = ns * 128, npr * 128
    y_prompt = np.empty(x_prompt.shape, np.float32)
    y_sample = np.empty(x_sample.shape, np.float32)
    halves = x_prompt.shape[1] // Lp
    for c in range(NCORES):
        y = res.results[c]["yout"]
        b, hf = c // halves, c % halves
        y_sample[c] = y[:Ls]
        y_prompt[b, hf * Lp:(hf + 1) * Lp] = y[Ls:Ls + Lp]
    return (y_prompt, y_sample)
```

```python
import numpy as np
from contextlib import ExitStack
import concourse.bass as bass
import concourse.mybir as mybir
from concourse.bass_utils import run_bass_kernel_spmd

F32 = mybir.dt.float32
BF16 = mybir.dt.bfloat16
AF = mybir.ActivationFunctionType
ALU = mybir.AluOpType

D = 1024
KD = 8
INW = 2560
EPS = 1e-6
WINDOWS = (2, 4, 8, 16)
NCORES = 8
SEG_CHUNKS = (32, 16)
LA = 2
NX = 13
HALO_FREE_SEGS = (1,)
NPM = 20
PMW = 144


class Sched:
    def __init__(self, nc, es):
        self.nc = nc
        self.es = es
        self.eng = dict(pe=nc.tensor, act=nc.scalar, dve=nc.vector, pool=nc.gpsimd, sp=nc.sync)
        self.sems = {}
        self.cnt = {}
        for e in ("pe", "act", "dve", "pool"):
            self.new_sem(e)
        self.waited = {e: {} for e in self.eng}
        self.lastw = {}
        self.readers = {}
        self.tags = {}
        self.nwaits = 0

    def new_sem(self, name):
        self.sems[name] = self.es.enter_context(self.nc.semaphore(name))
        self.cnt[name] = 0

    def _norm(self, reads, writes):
        rk, wk = [], []
        for it in reads:
            if isinstance(it, tuple):
                assert self.tags.get(it[0]) == it[1], f"stale read {it} has {self.tags.get(it[0])}"
                rk.append(it[0])
            else:
                rk.append(it)
        for it in writes:
            if isinstance(it, tuple):
                self.tags[it[0]] = it[1]
                wk.append(it[0])
            else:
                wk.append(it)
        return rk, wk

    def _waits(self, eng, reads, writes):
        need = {}

        def add(t):
            if t is None:
                return
            s, v = t
            if eng == "pe" and s == "pe":
                return
            if need.get(s, 0) < v:
                need[s] = v

        for k in reads:
            add(self.lastw.get(k))
        for k in writes:
            add(self.lastw.get(k))
            for s, v in self.readers.get(k, {}).items():
                add((s, v))
        E = self.eng[eng]
        w = self.waited[eng]
        for s, v in need.items():
            if w.get(s, 0) < v:
                E.wait_ge(self.sems[s], v)
                w[s] = v
                self.nwaits += 1
        return E

    def _record(self, t, reads, writes):
        for k in reads:
            r = self.readers.setdefault(k, {})
            if r.get(t[0], 0) < t[1]:
                r[t[0]] = t[1]
        for k in writes:
            self.lastw[k] = t
            self.readers[k] = {}

    def op(self, eng, fn, reads=(), writes=(), dsem=None):
        reads, writes = self._norm(reads, writes)
        E = self._waits(eng, reads, writes)
        ins = fn(E)
        if dsem is not None:
            if dsem not in self.sems:
                self.new_sem(dsem)
            sname, inc = dsem, 16
        else:
            sname, inc = eng, 1
        self.cnt[sname] += inc
        ins.then_inc(self.sems[sname], inc)
        t = (sname, self.cnt[sname])
        self._record(t, reads, writes)
        return t

    def dma_group(self, items, dsem, eng="sp"):
        if dsem not in self.sems:
            self.new_sem(dsem)
        for fn, reads, writes in items:
            E = self._waits(eng, reads, writes)
            ins = fn(E)
            self.cnt[dsem] += 16
            ins.then_inc(self.sems[dsem], 16)
        t = (dsem, self.cnt[dsem])
        for fn, reads, writes in items:
            self._record(t, reads, writes)
        return t

    def final_wait(self, eng):
        E = self.eng[eng]
        for s, v in self.cnt.items():
            if v > 0:
                E.wait_ge(self.sems[s], v)


def build_nc(seg_chunks=SEG_CHUNKS, nx=NX):
    nc = bass.Bass("TRN2", target_bir_lowering=False)
    chunks = []
    seg_first = []
    nreal = 0
    for s, n in enumerate(seg_chunks):
        assert n % 4 == 0
        seg_first.append(len(chunks))
        for e in range(n + 2):
            kind = "pre" if e == 0 else ("post" if e == n + 1 else "real")
            if kind != "real" and s in HALO_FREE_SEGS:
                continue
            chunks.append(dict(seg=s, kind=kind, e=e, ridx=None))
            if kind == "real":
                chunks[-1]["ridx"] = nreal
                nreal += 1
    NCH = len(chunks)
    sts = []
    for s, n in enumerate(seg_chunks):
        for i in range(n // 4):
            sts.append(dict(seg=s, c0=seg_first[s] + (0 if s in HALO_FREE_SEGS else 1) + 4 * i,
                            has_prev=not (s in HALO_FREE_SEGS and i == 0),
                            has_next=not (s in HALO_FREE_SEGS and i == n // 4 - 1)))
    NST = len(sts)
    for g, st in enumerate(sts):
        for j in range(4):
            chunks[st["c0"] + j]["st"] = g
            chunks[st["c0"] + j]["pos"] = j

    xin = nc.dram_tensor("xin", [NCH * 128, D], F32, kind="ExternalInput").ap()
    w_in = nc.dram_tensor("w_in", [D, INW], F32, kind="ExternalInput").ap()
    w_out = nc.dram_tensor("w_out", [D, D], F32, kind="ExternalInput").ap()
    d_pm = nc.dram_tensor("pm", [128, NPM * PMW], F32, kind="ExternalInput").ap()
    d_poolw = nc.dram_tensor("poolw", [128, 512], F32, kind="ExternalInput").ap()
    d_wst = nc.dram_tensor("wst", [128, 512], F32, kind="ExternalInput").ap()
    d_ident = nc.dram_tensor("ident", [128, 128], F32, kind="ExternalInput").ap()
    d_cols = nc.dram_tensor("cols", [128, 16], F32, kind="ExternalInput").ap()
    d_fg = nc.dram_tensor("fgb", [128, D], F32, kind="ExternalInput").ap()
    d_gb = nc.dram_tensor("gbc", [128, D], F32, kind="ExternalInput").ap()
    d_lnb = nc.dram_tensor("lnbb", [128, 512], F32, kind="ExternalInput").ap()
    d_bs = nc.dram_tensor("bsb", [128, 512], F32, kind="ExternalInput").ap()
    yout = nc.dram_tensor("yout", [nreal * 128, D], F32, kind="ExternalOutput").ap()

    es = ExitStack()
    with es:
        S = Sched(nc, es)

        def sb(name, shape, dt):
            return es.enter_context(nc.sbuf_tensor("s_" + name, shape, dt))

        xs = [sb(f"xs{i}", [128, D], F32) for i in range(nx)]
        w_in_bf = sb("w_in_bf", [128, KD, INW], BF16)
        w_out_bf = sb("w_out_bf", [128, KD, D], BF16)
        pm_bf = sb("pm_bf", [128, NPM, PMW], BF16)
        poolw_bf = sb("poolw_bf", [128, 4, 128], BF16)
        wst_bf = sb("wst_bf", [128, 4, 128], BF16)
        ident_bf = sb("ident_bf", [128, 128], BF16)
        lnb_bf = sb("lnb_bf", [128, 512], BF16)
        cols = sb("cols_sb", [128, 16], F32)
        fgb = sb("fgb_sb", [128, D], F32)
        gbc = sb("gbc_sb", [128, D], F32)
        cmat = sb("cmat", [128, 4, 128], F32)
        neghalf = sb("neghalf", [128, 1], F32)
        stat = sb("stat", [128, NCH * 16], F32)
        junk = sb("junk", [128, D], BF16)
        htok = [sb(f"htok{i}", [128, D], BF16) for i in range(2)]
        hT = [sb(f"hT{i}", [128, KD, 512], BF16) for i in range(2)]
        hTh = sb("hTh", [128, KD, 128], BF16)
        NA = 6
        a_tok = [sb(f"a_tok{i}", [128, 512], BF16) for i in range(NA)]
        NZ = 4
        z_tok = [sb(f"z_tok{i}", [128, 512], BF16) for i in range(NZ)]
        sga = sb("sga", [128, 4, 512], F32)
        sgb = [sb(f"sgb{i}", [128, 512], F32) for i in range(1)]
        usg = sb("usg", [128, 4, 512], F32)
        t1 = [sb(f"t1_{i}", [128, 512], F32) for i in range(2)]
        dT = sb("dT", [128, 4, 512], BF16)
        mixT = [sb(f"mixT{i}", [128, KD, 512], BF16) for i in range(2)]

        tp = es.enter_context(nc.psum_tensor("tp", [128, D], BF16))
        NTOK, NFM = 4, 3
        tokb = [es.enter_context(nc.psum_tensor(f"tokb{i}", [128, 512], F32)) for i in range(NTOK)]
        fmb = [es.enter_context(nc.psum_tensor(f"fmb{i}", [128, 512], F32)) for i in range(NFM)]
        ring = dict(tok=0, fm=0, xs=0, t1=0)

        def nxt(name, n):
            v = ring[name]
            ring[name] = (v + 1) % n
            return v

        S.op("pool", lambda E: E.memset(neghalf[:], -0.5), writes=["neghalf"])
        S.op("act", lambda E: E.activation(out=junk[:, 0:1], in_=neghalf[:], func=AF.Silu),
             reads=["neghalf"], writes=["junk"])
        S.dma_group([
            (lambda E: E.dma_start(out=cols[:], in_=d_cols[:, :]), [], ["cols"]),
            (lambda E: E.dma_start(out=gbc[:], in_=d_gb[:, :]), [], ["gbc"]),
        ], "cst")

        win_keys = {cg: [f"win{cg}"] for cg in range(5)}
        wout_keys = ["wout"]
        pm_keys = ["pm"]

        def load_win(cg):
            items = []
            for kp in range(0, KD, 2):
                src = w_in[kp * 128:(kp + 2) * 128, cg * 512:(cg + 1) * 512].rearrange("(k p) c -> p k c", p=128)
                dst = w_in_bf[:, kp:kp + 2, cg * 512:(cg + 1) * 512]
                items.append((lambda E, dst=dst, src=src: E.dma_start(out=dst, in_=src), [], [f"win{cg}"]))
            S.dma_group(items, f"w{cg}", eng="pool")

        def load_win_half(cg, hf):
            items = []
            c0_ = cg * 512 + hf * 256
            for kp in range(0, KD, 4):
                src = w_in[kp * 128:(kp + 4) * 128, c0_:c0_ + 256].rearrange("(k p) c -> p k c", p=128)
                dst = w_in_bf[:, kp:kp + 4, c0_:c0_ + 256]
                items.append((lambda E, dst=dst, src=src: E.dma_start(out=dst, in_=src), [], [f"win{cg}h{hf}"]))
            S.dma_group(items, f"w{cg}h{hf}", eng="pool")

        def load_wout():
            items = []
            for kp in range(0, KD, 2):
                src = w_out[kp * 128:(kp + 2) * 128, :].rearrange("(k p) c -> p k c", p=128)
                dst = w_out_bf[:, kp:kp + 2, :]
                items.append((lambda E, dst=dst, src=src: E.dma_start(out=dst, in_=src), [], ["wout"]))
            S.dma_group(items, "wo", eng="pool")
            S.dma_group([(lambda E: E.dma_start(out=fgb[:], in_=d_fg[:, :]), [], ["fgb"])], "cst3")

        def load_ident():
            S.dma_group([(lambda E: E.dma_start(out=ident_bf[:], in_=d_ident[:, :]), [], ["ident"])], "wid", eng="pool")

        def load_pm():
            items = []
            for j in range(0, NPM, 5):
                n = min(5, NPM - j)
                dst = pm_bf[:, j:j + n, :].rearrange("p m t -> p (m t)")
                src = d_pm[:, j * PMW:(j + n) * PMW]
                items.append((lambda E, dst=dst, src=src: E.dma_start(out=dst, in_=src), [], ["pm"]))
            S.dma_group(items, "wpm", eng="pool")

        def load_small():
            items = [
                (lambda E: E.dma_start(out=wst_bf[:].rearrange("p h q -> p (h q)"), in_=d_wst[:, :]), [], ["wst"]),
                (lambda E: E.dma_start(out=lnb_bf[:], in_=d_lnb[:, :]), [], ["lnb"]),
                (lambda E: E.dma_start(out=poolw_bf[:].rearrange("p g d -> p (g d)"), in_=d_poolw[:, :]), [], ["poolw"]),
            ]
            S.dma_group(items, "wsm", eng="pool")
            S.dma_group([
                (lambda E: E.dma_start(out=t1[0][:], in_=d_bs[:, :]), [], ["t1_0"]),
            ], "cst2")

        def compute_cmat():
            fb = nxt("fm", NFM)

            def cm_mm(E):
                ins = None
                for h in range(4):
                    ins = E.matmul(fmb[fb][:, h * 128:(h + 1) * 128], lhsT=lnb_bf[:, h * 128:(h + 1) * 128],
                                   rhs=wst_bf[:, h, :], start=True, stop=True)
                return ins
            S.op("pe", cm_mm, reads=["lnb", "wst"], writes=[f"fmb{fb}"])
            S.op("dve", lambda E: E.tensor_tensor(out=cmat[:].rearrange("p h q -> p (h q)"), in0=fmb[fb][:],
                                                  in1=t1[0][:], op=ALU.add),
                 reads=[f"fmb{fb}", "t1_0"], writes=["cmat"])

        slot_occ = {}
        x_done = {}

        xslot = {}
        loaded = [0]

        def ensure_loaded(upto):
            while loaded[0] <= min(upto, NCH - 1):
                c = loaded[0]
                sl = ring["xs"]
                occ = slot_occ.get(sl)
                if occ is not None and not x_done.get(occ, False):
                    break
                nxt("xs", nx)
                xslot[c] = sl
                slot_occ[sl] = c
                thr = []
                if 4 <= c < 6:
                    thr = win_keys[0]
                elif 6 <= c < 9:
                    thr = win_keys[3]
                elif 9 <= c < 13:
                    thr = ["win2h1"]
                S.op("sp", lambda E: E.dma_start(out=xs[sl][:], in_=xin[c * 128:(c + 1) * 128, :]),
                     reads=thr, writes=[(f"xs{sl}", (c, "x"))], dsem=f"ld{sl}")
                loaded[0] += 1

        def scol(c, j):
            return stat[:, c * 16 + j:c * 16 + j + 1]

        def rstd_ops(c, jin, jtmp, jout, key_in, key_out):
            S.op("pool", lambda E: E.tensor_scalar(out=scol(c, jtmp), in0=scol(c, jin), scalar1=1.0, scalar2=EPS,
                                                   op0=ALU.mult, op1=ALU.add), reads=[key_in], writes=[f"tmp{c}_{jtmp}"])
            S.op("pool", lambda E: E.tensor_tensor(out=scol(c, jout), in0=scol(c, jtmp), in1=neghalf[:],
                                                   op=ALU.pow), reads=[f"tmp{c}_{jtmp}", "neghalf"], writes=[key_out])

        def hT_view(c, k=None):
            ch = chunks[c]
            if ch["kind"] == "real":
                t = hT[ch["st"] % 2]
                lo = ch["pos"] * 128
                key = f"hT{ch['st'] % 2}_{ch['pos']}"
            else:
                t = hTh
                lo = 0
                key = "hTh"
            if k is None:
                return t[:, :, lo:lo + 128], key
            return t[:, k, lo:lo + 128], key

        fe = dict(a1=0, a2=0, b=0)

        def fe_a1(c):
            ensure_loaded(c + LA)
            assert c < loaded[0], f"x ring too small: chunk {c} not loadable"
            sl = xslot[c]
            S.op("act", lambda E: E.activation(out=junk[:], in_=xs[sl][:], func=AF.Square, scale=1.0 / 32.0,
                                               accum_out=scol(c, 0)),
                 reads=[(f"xs{sl}", (c, "x"))], writes=[f"ms{c}", "junk"])
            rstd_ops(c, 0, 1, 2, f"ms{c}", f"rstd{c}")

        def fe_a2(c):
            sl = xslot[c]
            hs = c % 2
            S.op("dve", lambda E: E.scalar_tensor_tensor(out=htok[hs][:], in0=xs[sl][:], scalar=scol(c, 2),
                                                         in1=gbc[:], op0=ALU.mult, op1=ALU.mult),
                 reads=[(f"xs{sl}", (c, "x")), f"rstd{c}", "gbc"], writes=[(f"htok{hs}", c)])
            if chunks[c]["kind"] != "real":
                x_done[c] = True

        def fe_b(c):
            hs = c % 2

            def tr(E):
                ins = None
                for k in range(KD):
                    ins = E.transpose(tp[:, k * 128:(k + 1) * 128], htok[hs][:, k * 128:(k + 1) * 128], ident_bf[:])
                return ins
            S.op("pe", tr, reads=[(f"htok{hs}", c), "ident"], writes=[("tp", c)])
            dst, hk = hT_view(c)
            S.op("act", lambda E: E.copy(out=dst, in_=tp[:].rearrange("p (k t) -> p k t", k=KD)),
                 reads=[("tp", c)], writes=[(hk, c)])

        def fe_can(c):
            ensure_loaded(c + LA)
            return c < loaded[0]

        def fe_tick():
            cb = fe["b"]
            if cb >= NCH:
                return
            while fe["a1"] <= cb:
                fe_a1(fe["a1"]); fe["a1"] += 1
            while fe["a2"] <= cb:
                fe_a2(fe["a2"]); fe["a2"] += 1
            fe_b(cb)
            fe["b"] += 1
            if fe["a2"] == cb + 1 and cb + 1 < NCH and (fe["a1"] > cb + 1 or fe_can(cb + 1)):
                if fe["a1"] == cb + 1:
                    fe_a1(cb + 1); fe["a1"] += 1
                fe_a2(cb + 1); fe["a2"] += 1
            if fe["a1"] == cb + 2 and cb + 2 < NCH and fe_can(cb + 2):
                fe_a1(cb + 2); fe["a1"] += 1
            ensure_loaded(fe["a1"] + LA)

        def ensure_fe(upto):
            while fe["b"] <= min(upto, NCH - 1):
                fe_tick()

        def TM_a(c):
            ensure_fe(c)
            b = nxt("tok", NTOK)
            bk = f"tokb{b}"

            def mm(E):
                ins = None
                for k in range(KD):
                    lhsT, _ = hT_view(c, k)
                    ins = E.matmul(tokb[b][:], lhsT=lhsT, rhs=w_in_bf[:, k, 0:512],
                                   start=(k == 0), stop=(k == KD - 1))
                return ins
            S.op("pe", mm, reads=[(hT_view(c)[1], c)] + win_keys[0], writes=[bk])
            sa = c % NA
            S.op("act", lambda E: E.copy(out=a_tok[sa][:], in_=tokb[b][:]), reads=[bk], writes=[(f"a{sa}", c)])

        def TM_v(c):
            ensure_fe(c)
            b = nxt("tok", NTOK)
            bk = f"tokb{b}"

            def mm(E):
                ins = None
                for k in range(KD):
                    lhsT, _ = hT_view(c, k)
                    ins = E.matmul(tokb[b][:], lhsT=lhsT, rhs=w_in_bf[:, k, 3 * 512:4 * 512],
                                   start=(k == 0), stop=(k == KD - 1))
                return ins
            S.op("pe", mm, reads=[(hT_view(c)[1], c)] + win_keys[3], writes=[bk])
            S.op("dve", lambda E: E.bn_stats(out=stat[:, c * 16 + 4:c * 16 + 10], in_=tokb[b][:]),
                 reads=[bk], writes=[f"bst{c}"])
            S.op("dve", lambda E: E.bn_aggr(out=stat[:, c * 16 + 10:c * 16 + 12], in_=stat[:, c * 16 + 4:c * 16 + 10]),
                 reads=[f"bst{c}"], writes=[f"mv{c}"])
            rstd_ops(c, 11, 12, 13, f"mv{c}", f"rstdv{c}")

            def zfn():
                sz = chunks[c]["ridx"] % NZ
                S.op("dve", lambda E: E.tensor_scalar(out=z_tok[sz][:], in0=tokb[b][:], scalar1=scol(c, 10),
                                                      scalar2=scol(c, 13), op0=ALU.subtract, op1=ALU.mult),
                     reads=[bk, f"mv{c}", f"rstdv{c}"], writes=[(f"z{sz}", c)])
            return zfn

        def FM_block(g, kind, j):
            cg = dict(ga=1, u=2, gb=4)[kind]
            col0 = cg * 512 + j * 128
            slot = g % 2
            st = sts[g]
            b = nxt("fm", NFM)
            bk = f"fmb{b}"

            def mm(E):
                ins = None
                for k in range(KD):
                    ins = E.matmul(fmb[b][:], lhsT=w_in_bf[:, k, col0:col0 + 128], rhs=hT[slot][:, k, :],
                                   start=(k == 0), stop=(k == KD - 1))
                return ins
            wk = win_keys[cg] if cg == 1 else [f"win{cg}h{j // 2}"]
            S.op("pe", mm, reads=[(f"hT{slot}_{p}", st["c0"] + p) for p in range(4)] + wk, writes=[bk])
            if kind == "ga":
                S.op("act", lambda E: E.activation(out=sga[:, j, :], in_=fmb[b][:], func=AF.Silu),
                     reads=[bk], writes=[(f"sga{j}", g)])
            elif kind == "gb":
                s2 = 0
                S.op("act", lambda E: E.activation(out=sgb[s2][:], in_=fmb[b][:], func=AF.Silu),
                     reads=[bk], writes=[(f"sgb{s2}", (g, j))])
            else:
                s2 = 0
                S.op("dve", lambda E: E.tensor_tensor(out=usg[:, j, :], in0=fmb[b][:], in1=sgb[s2][:], op=ALU.mult),
                     reads=[bk, (f"sgb{s2}", (g, j))], writes=[(f"usg{j}", g)])

        def pm_idx(c, gq):
            ch = chunks[c]
            n = seg_chunks[ch["seg"]]
            if ch["e"] == 1:
                return 4 + ch["seg"] * 8 + gq
            if ch["e"] == n:
                return 4 + ch["seg"] * 8 + 4 + gq
            return gq

        def POOL(g, gq):
            st = sts[g]
            c0 = st["c0"]
            b = nxt("fm", NFM)
            bk = f"fmb{b}"

            def mm(E):
                first = True
                ins = None
                if st["has_prev"]:
                    ins = E.matmul(fmb[b][:, 0:8], lhsT=a_tok[(c0 - 1) % NA][:, gq * 128:(gq + 1) * 128],
                                   rhs=pm_bf[:, gq, 136:144], start=True, stop=False, skip_group_check=True)
                    first = False
                for j in range(4):
                    c = c0 + j
                    lo = 8 if j == 0 else 0
                    hi = 136 if j == 3 else 144
                    o0 = j * 128 - 8 + lo
                    last = (j == 3 and not st["has_next"])
                    ins = E.matmul(fmb[b][:, o0:o0 + (hi - lo)], lhsT=a_tok[c % NA][:, gq * 128:(gq + 1) * 128],
                                   rhs=pm_bf[:, pm_idx(c, gq), lo:hi], start=first, stop=last,
                                   skip_group_check=True)
                    first = False
                if st["has_next"]:
                    ins = E.matmul(fmb[b][:, 504:512], lhsT=a_tok[(c0 + 4) % NA][:, gq * 128:(gq + 1) * 128],
                                   rhs=pm_bf[:, gq, 0:8], start=False, stop=True, skip_group_check=True)
                return ins
            jr = range(-1 if st["has_prev"] else 0, 5 if st["has_next"] else 4)
            rd = [(f"a{(c0 + j) % NA}", c0 + j) for j in jr] + pm_keys
            S.op("pe", mm, reads=rd, writes=[bk])
            S.op("act", lambda E: E.copy(out=dT[:, gq, :], in_=fmb[b][:]), reads=[bk], writes=[(f"dT_{gq}", g)])

        def SGh(g, h):
            st = sts[g]
            ms = g % 2
            b = nxt("fm", NFM)
            bk = f"fmb{b}"

            def mm(E):
                ins = None
                for j in range(4):
                    c = st["c0"] + j
                    sz = chunks[c]["ridx"] % NZ
                    ins = E.matmul(fmb[b][:, j * 128:(j + 1) * 128], lhsT=z_tok[sz][:, h * 128:(h + 1) * 128],
                                   rhs=wst_bf[:, h, :], start=True, stop=True)
                return ins
            rd = [(f"z{chunks[st['c0'] + j]['ridx'] % NZ}", st["c0"] + j) for j in range(4)] + ["wst"]
            S.op("pe", mm, reads=rd, writes=[bk])
            ts = nxt("t1", 2)
            S.op("dve", lambda E: E.scalar_tensor_tensor(
                out=t1[ts][:].rearrange("p (j q) -> p j q", j=4),
                in0=fmb[b][:].rearrange("p (j q) -> p j q", j=4),
                scalar=cols[:, 12 + h:13 + h],
                in1=cmat[:, h, :].unsqueeze(1).to_broadcast([128, 4, 128]),
                op0=ALU.mult, op1=ALU.add), reads=[bk, "cols", "cmat"], writes=[(f"t1_{ts}", (g, h))])
            S.op("pool", lambda E: E.tensor_tensor(out=mixT[ms][:, 4 + h, :], in0=t1[ts][:], in1=usg[:, h, :],
                                                   op=ALU.mult),
                 reads=[(f"t1_{ts}", (g, h)), (f"usg{h}", g)], writes=[(f"mixT{ms}_{4 + h}", g)])

        def PWq(g, gq):
            ms = g % 2
            b = nxt("fm", NFM)
            bk = f"fmb{b}"
            S.op("pe", lambda E: E.matmul(fmb[b][:], lhsT=poolw_bf[:, gq, :], rhs=dT[:, gq, :],
                                          start=True, stop=True),
                 reads=[(f"dT_{gq}", g), "poolw"], writes=[bk])
            S.op("dve", lambda E: E.scalar_tensor_tensor(
                out=mixT[ms][:, gq, :], in0=fmb[b][:], scalar=cols[:, 8 + gq:9 + gq], in1=sga[:, gq, :],
                op0=ALU.mult, op1=ALU.mult), reads=[bk, "cols", (f"sga{gq}", g)], writes=[(f"mixT{ms}_{gq}", g)])

        def WO_main(g, j):
            st = sts[g]
            ms = g % 2
            c = st["c0"] + j
            sl = xslot[c]
            xk = f"xs{sl}"
            for hf in range(2):
                b = nxt("tok", NTOK)
                bk = f"tokb{b}"

                def mm(E):
                    ins = None
                    for k in range(KD):
                        ins = E.matmul(tokb[b][:], lhsT=mixT[ms][:, k, j * 128:(j + 1) * 128],
                                       rhs=w_out_bf[:, k, hf * 512:(hf + 1) * 512],
                                       start=(k == 0), stop=(k == KD - 1))
                    return ins
                S.op("pe", mm, reads=[(f"mixT{ms}_{k}", g) for k in range(KD)] + wout_keys, writes=[bk])
                S.op("dve", lambda E: E.tensor_tensor(out=xs[sl][:, hf * 512:(hf + 1) * 512], in0=tokb[b][:],
                                                      in1=xs[sl][:, hf * 512:(hf + 1) * 512], op=ALU.add),
                     reads=[bk, (xk, (c, "x" if hf == 0 else "xo0"))], writes=[(xk, (c, "xo0" if hf == 0 else "xo"))])
            S.op("act", lambda E: E.activation(out=junk[:], in_=xs[sl][:], func=AF.Square, scale=1.0 / 32.0,
                                               accum_out=scol(c, 3)),
                 reads=[(xk, (c, "xo"))], writes=[f"ms2_{c}", "junk"])
            rstd_ops(c, 3, 14, 15, f"ms2_{c}", f"rstd2_{c}")

            def tail():
                S.op("dve", lambda E: E.scalar_tensor_tensor(out=xs[sl][:], in0=xs[sl][:], scalar=scol(c, 15),
                                                             in1=fgb[:], op0=ALU.mult, op1=ALU.mult),
                     reads=[(xk, (c, "xo")), f"rstd2_{c}", "fgb"], writes=[(xk, (c, "y"))])
                r = chunks[c]["ridx"]
                S.op("sp", lambda E: E.dma_start(out=yout[r * 128:(r + 1) * 128, :], in_=xs[sl][:]),
                     reads=[(xk, (c, "y"))], dsem=f"st{sl}")
                x_done[c] = True
                ensure_loaded(fe["a1"] + LA)
            return tail

        load_ident()
        load_win(0)
        load_win(3)
        ensure_loaded(3)
        for c_ in range(3):
            fe_a1(c_)
        fe["a1"] = 3
        for c_ in range(2):
            fe_a2(c_)
        fe["a2"] = 2
        fe_tick()
        fe_tick()
        load_pm()
        load_win(1)
        late_loads = [lambda: (load_win_half(4, 0), load_win_half(2, 0)), load_small,
                      lambda: (load_win_half(4, 1), load_win_half(2, 1))]

        tma_done = [0]
        deferred = []
        for g, st in enumerate(sts):
            c0 = st["c0"]
            if g > 0:
                ensure_fe(c0 + 4)
            while tma_done[0] < c0 + 1:
                TM_a(tma_done[0])
                tma_done[0] += 1
            for j in range(4):
                if g == 0:
                    ensure_fe(c0 + j + 2)
                zfn = TM_v(c0 + j)
                if j < 3 or st["has_next"]:
                    TM_a(c0 + j + 1)
                    tma_done[0] = c0 + j + 2
                zfn()
                if late_loads:
                    late_loads.pop(0)()
            if g > 0:
                tails = []
                for j in range(4):
                    tails.append(WO_main(g - 1, j))
                    if j >= 1:
                        tails[j - 1]()
                deferred.append(tails[3])
            nxt_c0 = sts[g + 1]["c0"] if g + 1 < NST else None
            tick_target = (nxt_c0 + 3) if nxt_c0 is not None else NCH - 1
            n_ticks = max(0, tick_target - fe["b"] + 1)
            if n_ticks <= 3:
                tick_pos = {("P", 1), ("P", 3), ("SG", 0)}
            elif n_ticks == 4:
                tick_pos = {("P", 1), ("P", 3), ("SG", 0), ("SG", 1)}
            else:
                tick_pos = {("P", 0), ("P", 1), ("P", 2), ("P", 3), ("PW", 0), ("SG", 0), ("SG", 1)}
            base = [("ga", 0), ("P", 0), ("ga", 1), ("P", 1), ("ga", 2), ("P", 2), ("ga", 3), ("P", 3),
                    ("gb", 0), ("PW", 0), ("u", 0), ("PW", 1), ("gb", 1), ("SG", 0), ("u", 1), ("PW", 2),
                    ("gb", 2), ("SG", 1), ("u", 2), ("PW", 3), ("gb", 3), ("SG", 2), ("u", 3), ("tick4",),
                    ("SG", 3)]
            if g == NST - 1:
                base = [("gb", 0), ("P", 0), ("u", 0), ("P", 1), ("gb", 1), ("SG", 0), ("u", 1), ("P", 2),
                        ("gb", 2), ("SG", 1), ("u", 2), ("P", 3), ("gb", 3), ("SG", 2), ("u", 3), ("SG", 3),
                        ("ga", 0), ("PW", 0), ("ga", 1), ("PW", 1), ("ga", 2), ("PW", 2), ("ga", 3), ("PW", 3)]
            seq = []
            for it in base:
                seq.append(it)
                if it in tick_pos:
                    seq.append(("tick",))
            bi = 0
            for it in seq:
                if it[0] in ("ga", "gb", "u"):
                    FM_block(g, it[0], it[1])
                    if deferred:
                        deferred.pop(0)()
                    if g == 0 and bi == 0:
                        load_wout()
                    bi += 1
                elif it[0] == "P":
                    POOL(g, it[1])
                elif it[0] == "SG":
                    if g == 0 and it[1] == 0:
                        compute_cmat()
                    SGh(g, it[1])
                elif it[0] == "PW":
                    PWq(g, it[1])
                elif it[0] == "tick":
                    if nxt_c0 is not None and fe["b"] <= nxt_c0 + 3:
                        fe_tick()
                    elif nxt_c0 is None and fe["b"] < NCH:
                        fe_tick()
                elif it[0] == "tick4":
                    if nxt_c0 is not None:
                        ensure_fe(nxt_c0 + 3)
                        ensure_fe(nxt_c0 + 4)
        tails = []
        for j in range(4):
            tails.append(WO_main(NST - 1, j))
            if j >= 1:
                tails[j - 1]()
        tails[3]()
        assert not deferred
        S.final_wait("sp")
        print(f"[build] nwaits={S.nwaits} counts={ {k: v for k, v in S.cnt.items() if k in ('pe', 'act', 'dve', 'pool')} }")
    return nc, dict(NCH=NCH, nreal=nreal, chunks=chunks)


def _pool_mats(first_is_start, last_is_end):
    out = {}
    for gq, w in enumerate(WINDOWS):
        h = w // 2
        s = np.arange(128)[:, None]
        t = np.arange(128)[None, :]
        inwin = ((s >= t - h) & (s < t + h)).astype(np.float64)
        eye = (s == t).astype(np.float64)
        out[("cur", gq)] = (inwin / w - eye).astype(np.float32)
        out[("prev", gq)] = ((s - 128 >= t - h).astype(np.float64) / w).astype(np.float32)
        out[("next", gq)] = ((s + 128 < t + h).astype(np.float64) / w).astype(np.float32)
        cnt_start = np.minimum(w, t + h).astype(np.float64)
        out[("cur_start", gq)] = (inwin / cnt_start - eye).astype(np.float32)
        cnt_end = np.minimum(w, 128 - t + h).astype(np.float64)
        out[("cur_end", gq)] = (inwin / cnt_end - eye).astype(np.float32)
    return out


def _pm_for_core(seg_start_end):
    m = _pool_mats(None, None)

    def wide(cur_key, gq):
        return np.concatenate([m[("next", gq)][:, 120:128], m[(cur_key, gq)], m[("prev", gq)][:, 0:8]], axis=1)
    mats = [wide("cur", gq) for gq in range(4)]
    for (fs, le) in seg_start_end:
        mats += [wide("cur_start" if fs else "cur", gq) for gq in range(4)]
        mats += [wide("cur_end" if le else "cur", gq) for gq in range(4)]
    arr = np.stack(mats, axis=0)
    assert arr.shape == (NPM, 128, PMW)
    return np.ascontiguousarray(arr.transpose(1, 0, 2).reshape(128, -1))


def _ext_segment(x_seq, lo, hi):
    S_ = x_seq.shape[0]
    out = np.zeros((hi - lo + 256, x_seq.shape[1]), np.float32)
    a = max(lo - 128, 0)
    b = min(hi + 128, S_)
    out[a - (lo - 128):b - (lo - 128)] = x_seq[a:b]
    return out


def make_in_maps(x_prompt, x_sample, norm_g, w_in, pool_w, pool_scale, sgu_ln_g, sgu_ln_b,
                 w_spatial, b_spatial, w_out, final_g, ncores=NCORES, seg_chunks=SEG_CHUNKS):
    f = np.float32
    w_in = np.ascontiguousarray(w_in, f)
    w_out = np.ascontiguousarray(w_out, f)
    cols = np.concatenate([np.asarray(norm_g, f).reshape(8, 128).T, np.asarray(pool_scale, f).reshape(4, 128).T,
                           np.asarray(sgu_ln_g, f).reshape(4, 128).T], axis=1)
    cols = np.ascontiguousarray(cols)
    fgb = np.ascontiguousarray(np.broadcast_to(np.asarray(final_g, f)[None, :], (128, D)))
    gbc = np.ascontiguousarray(np.broadcast_to(np.asarray(norm_g, f)[None, :], (128, D)))
    lnbb = np.ascontiguousarray(np.broadcast_to(np.asarray(sgu_ln_b, f)[None, :], (128, 512)))
    bsb = np.ascontiguousarray(np.broadcast_to(np.asarray(b_spatial, f).reshape(1, 512), (128, 512)))
    poolw = np.ascontiguousarray(np.asarray(pool_w, f).transpose(1, 0, 2).reshape(128, 512))
    wst = np.ascontiguousarray(np.asarray(w_spatial, f).transpose(2, 0, 1).reshape(128, 512))
    ident = np.eye(128, dtype=f)
    n0, n1 = seg_chunks
    L0, L1 = n0 * 128, n1 * 128
    halves = x_prompt.shape[1] // L0
    in_maps = []
    for c in range(ncores):
        b, hf = c // halves, c % halves
        seg0 = _ext_segment(np.asarray(x_prompt[b], f), hf * L0, (hf + 1) * L0)
        seg1 = np.asarray(x_sample[c], f)
        xin = np.concatenate([seg0, seg1], axis=0)
        pm = _pm_for_core([(hf == 0, hf == halves - 1), (True, True)])
        in_maps.append(dict(xin=xin, w_in=w_in, w_out=w_out, pm=pm, poolw=poolw, wst=wst, ident=ident,
                            cols=cols, fgb=fgb, gbc=gbc, lnbb=lnbb, bsb=bsb))
    return in_maps


_NC_CACHE = {}


def kernel(x_prompt, x_sample, norm_g, w_in, pool_w, pool_scale, sgu_ln_g, sgu_ln_b,
           w_spatial, b_spatial, w_out, final_g):
    x_prompt = np.asarray(x_prompt)
    x_sample = np.asarray(x_sample)
    in_maps = make_in_maps(x_prompt, x_sample, norm_g, w_in, pool_w, pool_scale, sgu_ln_g, sgu_ln_b,
                           w_spatial, b_spatial, w_out, final_g)
    if "nc" not in _NC_CACHE:
        _NC_CACHE["nc"] = build_nc()[0]
    nc = _NC_CACHE["nc"]
    res = run_bass_kernel_spmd(nc, in_maps, core_ids=list(range(NCORES)))
    n0, n1 = SEG_CHUNKS
    L0, L1 = n0 * 128, n1 * 128
    y_prompt = np.empty(x_prompt.shape, np.float32)
    y_sample = np.empty(x_sample.shape, np.float32)
    halves = x_prompt.shape[1] // L0
    for c in range(NCORES):
        y = res.results[c]["yout"]
        b, hf = c // halves, c % halves
        y_prompt[b, hf * L0:(hf + 1) * L0] = y[:L0]
        y_sample[c] = y[L0:L0 + L1]
    return (y_prompt, y_sample)
```

```python
import numpy as np
from contextlib import ExitStack
import concourse.bass as bass
import concourse.mybir as mybir
from concourse.bass_utils import run_bass_kernel_spmd

F32 = mybir.dt.float32
BF16 = mybir.dt.bfloat16
AF = mybir.ActivationFunctionType
ALU = mybir.AluOpType

D = 1024
KD = 8
INW = 2560
EPS = 1e-6
WINDOWS = (2, 4, 8, 16)
NCORES = 8
SEG_CHUNKS = (32, 16)
LA = 2
NX = 13
HALO_FREE_SEGS = (1,)
NPM = 20
PMW = 144


class Sched:
    def __init__(self, nc, es):
        self.nc = nc
        self.es = es
        self.eng = dict(pe=nc.tensor, act=nc.scalar, dve=nc.vector, pool=nc.gpsimd, sp=nc.sync)
        self.sems = {}
        self.cnt = {}
        for e in ("pe", "act", "dve", "pool"):
            self.new_sem(e)
        self.waited = {e: {} for e in self.eng}
        self.lastw = {}
        self.readers = {}
        self.tags = {}
        self.nwaits = 0

    def new_sem(self, name):
        self.sems[name] = self.es.enter_context(self.nc.semaphore(name))
        self.cnt[name] = 0

    def _norm(self, reads, writes):
        rk, wk = [], []
        for it in reads:
            if isinstance(it, tuple):
                assert self.tags.get(it[0]) == it[1], f"stale read {it} has {self.tags.get(it[0])}"
                rk.append(it[0])
            else:
                rk.append(it)
        for it in writes:
            if isinstance(it, tuple):
                self.tags[it[0]] = it[1]
                wk.append(it[0])
            else:
                wk.append(it)
        return rk, wk

    def _waits(self, eng, reads, writes):
        need = {}

        def add(t):
            if t is None:
                return
            s, v = t
            if eng == "pe" and s == "pe":
                return
            if need.get(s, 0) < v:
                need[s] = v

        for k in reads:
            add(self.lastw.get(k))
        for k in writes:
            add(self.lastw.get(k))
            for s, v in self.readers.get(k, {}).items():
                add((s, v))
        E = self.eng[eng]
        w = self.waited[eng]
        for s, v in need.items():
            if w.get(s, 0) < v:
                E.wait_ge(self.sems[s], v)
                w[s] = v
                self.nwaits += 1
        return E

    def _record(self, t, reads, writes):
        for k in reads:
            r = self.readers.setdefault(k, {})
            if r.get(t[0], 0) < t[1]:
                r[t[0]] = t[1]
        for k in writes:
            self.lastw[k] = t
            self.readers[k] = {}

    def op(self, eng, fn, reads=(), writes=(), dsem=None):
        reads, writes = self._norm(reads, writes)
        E = self._waits(eng, reads, writes)
        ins = fn(E)
        if dsem is not None:
            if dsem not in self.sems:
                self.new_sem(dsem)
            sname, inc = dsem, 16
        else:
            sname, inc = eng, 1
        self.cnt[sname] += inc
        ins.then_inc(self.sems[sname], inc)
        t = (sname, self.cnt[sname])
        self._record(t, reads, writes)
        return t

    def dma_group(self, items, dsem, eng="sp"):
        if dsem not in self.sems:
            self.new_sem(dsem)
        for fn, reads, writes in items:
            E = self._waits(eng, reads, writes)
            ins = fn(E)
            self.cnt[dsem] += 16
            ins.then_inc(self.sems[dsem], 16)
        t = (dsem, self.cnt[dsem])
        for fn, reads, writes in items:
            self._record(t, reads, writes)
        return t

    def final_wait(self, eng):
        E = self.eng[eng]
        for s, v in self.cnt.items():
            if v > 0:
                E.wait_ge(self.sems[s], v)


def build_nc(seg_chunks=SEG_CHUNKS, nx=NX):
    nc = bass.Bass("TRN2", target_bir_lowering=False)
    chunks = []
    seg_first = []
    nreal = 0
    for s, n in enumerate(seg_chunks):
        assert n % 4 == 0
        seg_first.append(len(chunks))
        for e in range(n + 2):
            kind = "pre" if e == 0 else ("post" if e == n + 1 else "real")
            if kind != "real" and s in HALO_FREE_SEGS:
                continue
            chunks.append(dict(seg=s, kind=kind, e=e, ridx=None))
            if kind == "real":
                chunks[-1]["ridx"] = nreal
                nreal += 1
    NCH = len(chunks)
    sts = []
    for s, n in enumerate(seg_chunks):
        for i in range(n // 4):
            sts.append(dict(seg=s, c0=seg_first[s] + (0 if s in HALO_FREE_SEGS else 1) + 4 * i,
                            has_prev=not (s in HALO_FREE_SEGS and i == 0),
                            has_next=not (s in HALO_FREE_SEGS and i == n // 4 - 1)))
    NST = len(sts)
    for g, st in enumerate(sts):
        for j in range(4):
            chunks[st["c0"] + j]["st"] = g
            chunks[st["c0"] + j]["pos"] = j

    xin = nc.dram_tensor("xin", [NCH * 128, D], F32, kind="ExternalInput").ap()
    w_in = nc.dram_tensor("w_in", [D, INW], F32, kind="ExternalInput").ap()
    w_out = nc.dram_tensor("w_out", [D, D], F32, kind="ExternalInput").ap()
    d_pm = nc.dram_tensor("pm", [128, NPM * PMW], F32, kind="ExternalInput").ap()
    d_poolw = nc.dram_tensor("poolw", [128, 512], F32, kind="ExternalInput").ap()
    d_wst = nc.dram_tensor("wst", [128, 512], F32, kind="ExternalInput").ap()
    d_ident = nc.dram_tensor("ident", [128, 128], F32, kind="ExternalInput").ap()
    d_cols = nc.dram_tensor("cols", [128, 16], F32, kind="ExternalInput").ap()
    d_fg = nc.dram_tensor("fgb", [128, D], F32, kind="ExternalInput").ap()
    d_gb = nc.dram_tensor("gbc", [128, D], F32, kind="ExternalInput").ap()
    d_lnb = nc.dram_tensor("lnbb", [128, 512], F32, kind="ExternalInput").ap()
    d_bs = nc.dram_tensor("bsb", [128, 512], F32, kind="ExternalInput").ap()
    yout = nc.dram_tensor("yout", [nreal * 128, D], F32, kind="ExternalOutput").ap()

    es = ExitStack()
    with es:
        S = Sched(nc, es)

        def sb(name, shape, dt):
            return es.enter_context(nc.sbuf_tensor("s_" + name, shape, dt))

        xs = [sb(f"xs{i}", [128, D], F32) for i in range(nx)]
        w_in_bf = sb("w_in_bf", [128, KD, INW], BF16)
        w_out_bf = sb("w_out_bf", [128, KD, D], BF16)
        pm_bf = sb("pm_bf", [128, NPM, PMW], BF16)
        poolw_bf = sb("poolw_bf", [128, 4, 128], BF16)
        wst_bf = sb("wst_bf", [128, 4, 128], BF16)
        ident_bf = sb("ident_bf", [128, 128], BF16)
        lnb_bf = sb("lnb_bf", [128, 512], BF16)
        cols = sb("cols_sb", [128, 16], F32)
        fgb = sb("fgb_sb", [128, D], F32)
        gbc = sb("gbc_sb", [128, D], F32)
        cmat = sb("cmat", [128, 4, 128], F32)
        neghalf = sb("neghalf", [128, 1], F32)
        stat = sb("stat", [128, NCH * 16], F32)
        junk = sb("junk", [128, D], BF16)
        htok = [sb(f"htok{i}", [128, D], BF16) for i in range(2)]
        hT = [sb(f"hT{i}", [128, KD, 512], BF16) for i in range(2)]
        hTh = sb("hTh", [128, KD, 128], BF16)
        NA = 6
        a_tok = [sb(f"a_tok{i}", [128, 512], BF16) for i in range(NA)]
        NZ = 4
        z_tok = [sb(f"z_tok{i}", [128, 512], BF16) for i in range(NZ)]
        sga = sb("sga", [128, 4, 512], F32)
        sgb = [sb(f"sgb{i}", [128, 512], F32) for i in range(1)]
        usg = sb("usg", [128, 4, 512], F32)
        t1 = [sb(f"t1_{i}", [128, 512], F32) for i in range(2)]
        dT = sb("dT", [128, 4, 512], BF16)
        mixT = [sb(f"mixT{i}", [128, KD, 512], BF16) for i in range(2)]

        tp = es.enter_context(nc.psum_tensor("tp", [128, D], BF16))
        NTOK, NFM = 4, 3
        tokb = [es.enter_context(nc.psum_tensor(f"tokb{i}", [128, 512], F32)) for i in range(NTOK)]
        fmb = [es.enter_context(nc.psum_tensor(f"fmb{i}", [128, 512], F32)) for i in range(NFM)]
        ring = dict(tok=0, fm=0, xs=0, t1=0)

        def nxt(name, n):
            v = ring[name]
            ring[name] = (v + 1) % n
            return v

        S.op("pool", lambda E: E.memset(neghalf[:], -0.5), writes=["neghalf"])
        S.op("act", lambda E: E.activation(out=junk[:, 0:1], in_=neghalf[:], func=AF.Silu),
             reads=["neghalf"], writes=["junk"])
        S.dma_group([
            (lambda E: E.dma_start(out=cols[:], in_=d_cols[:, :]), [], ["cols"]),
            (lambda E: E.dma_start(out=gbc[:], in_=d_gb[:, :]), [], ["gbc"]),
        ], "cst")

        win_keys = {cg: [f"win{cg}"] for cg in range(5)}
        wout_keys = ["wout"]
        pm_keys = ["pm"]

        def load_win(cg):
            items = []
            for kp in range(0, KD, 2):
                src = w_in[kp * 128:(kp + 2) * 128, cg * 512:(cg + 1) * 512].rearrange("(k p) c -> p k c", p=128)
                dst = w_in_bf[:, kp:kp + 2, cg * 512:(cg + 1) * 512]
                items.append((lambda E, dst=dst, src=src: E.dma_start(out=dst, in_=src), [], [f"win{cg}"]))
            S.dma_group(items, f"w{cg}", eng="pool")

        def load_win_half(cg, hf):
            items = []
            c0_ = cg * 512 + hf * 256
            for kp in range(0, KD, 4):
                src = w_in[kp * 128:(kp + 4) * 128, c0_:c0_ + 256].rearrange("(k p) c -> p k c", p=128)
                dst = w_in_bf[:, kp:kp + 4, c0_:c0_ + 256]
                items.append((lambda E, dst=dst, src=src: E.dma_start(out=dst, in_=src), [], [f"win{cg}h{hf}"]))
            S.dma_group(items, f"w{cg}h{hf}", eng="pool")

        def load_wout():
            items = []
            for kp in range(0, KD, 2):
                src = w_out[kp * 128:(kp + 2) * 128, :].rearrange("(k p) c -> p k c", p=128)
                dst = w_out_bf[:, kp:kp + 2, :]
                items.append((lambda E, dst=dst, src=src: E.dma_start(out=dst, in_=src), [], ["wout"]))
            S.dma_group(items, "wo", eng="pool")
            S.dma_group([(lambda E: E.dma_start(out=fgb[:], in_=d_fg[:, :]), [], ["fgb"])], "cst3")

        def load_ident():
            S.dma_group([(lambda E: E.dma_start(out=ident_bf[:], in_=d_ident[:, :]), [], ["ident"])], "wid", eng="pool")

        def load_pm():
            items = []
            for j in range(0, NPM, 5):
                n = min(5, NPM - j)
                dst = pm_bf[:, j:j + n, :].rearrange("p m t -> p (m t)")
                src = d_pm[:, j * PMW:(j + n) * PMW]
                items.append((lambda E, dst=dst, src=src: E.dma_start(out=dst, in_=src), [], ["pm"]))
            S.dma_group(items, "wpm", eng="pool")

        def load_small():
            items = [
                (lambda E: E.dma_start(out=wst_bf[:].rearrange("p h q -> p (h q)"), in_=d_wst[:, :]), [], ["wst"]),
                (lambda E: E.dma_start(out=lnb_bf[:], in_=d_lnb[:, :]), [], ["lnb"]),
                (lambda E: E.dma_start(out=poolw_bf[:].rearrange("p g d -> p (g d)"), in_=d_poolw[:, :]), [], ["poolw"]),
            ]
            S.dma_group(items, "wsm", eng="pool")
            S.dma_group([
                (lambda E: E.dma_start(out=t1[0][:], in_=d_bs[:, :]), [], ["t1_0"]),
            ], "cst2")

        def compute_cmat():
            fb = nxt("fm", NFM)

            def cm_mm(E):
                ins = None
                for h in range(4):
                    ins = E.matmul(fmb[fb][:, h * 128:(h + 1) * 128], lhsT=lnb_bf[:, h * 128:(h + 1) * 128],
                                   rhs=wst_bf[:, h, :], start=True, stop=True)
                return ins
            S.op("pe", cm_mm, reads=["lnb", "wst"], writes=[f"fmb{fb}"])
            S.op("dve", lambda E: E.tensor_tensor(out=cmat[:].rearrange("p h q -> p (h q)"), in0=fmb[fb][:],
                                                  in1=t1[0][:], op=ALU.add),
                 reads=[f"fmb{fb}", "t1_0"], writes=["cmat"])

        slot_occ = {}
        x_done = {}

        xslot = {}
        loaded = [0]

        def ensure_loaded(upto):
            while loaded[0] <= min(upto, NCH - 1):
                c = loaded[0]
                sl = ring["xs"]
                occ = slot_occ.get(sl)
                if occ is not None and not x_done.get(occ, False):
                    break
                nxt("xs", nx)
                xslot[c] = sl
                slot_occ[sl] = c
                thr = []
                if 4 <= c < 6:
                    thr = win_keys[0]
                elif 6 <= c < 9:
                    thr = win_keys[3]
                elif 9 <= c < 13:
                    thr = ["win2h1"]
                S.op("sp", lambda E: E.dma_start(out=xs[sl][:], in_=xin[c * 128:(c + 1) * 128, :]),
                     reads=thr, writes=[(f"xs{sl}", (c, "x"))], dsem=f"ld{sl}")
                loaded[0] += 1

        def scol(c, j):
            return stat[:, c * 16 + j:c * 16 + j + 1]

        def rstd_ops(c, jin, jtmp, jout, key_in, key_out):
            S.op("pool", lambda E: E.tensor_scalar(out=scol(c, jtmp), in0=scol(c, jin), scalar1=1.0, scalar2=EPS,
                                                   op0=ALU.mult, op1=ALU.add), reads=[key_in], writes=[f"tmp{c}_{jtmp}"])
            S.op("pool", lambda E: E.tensor_tensor(out=scol(c, jout), in0=scol(c, jtmp), in1=neghalf[:],
                                                   op=ALU.pow), reads=[f"tmp{c}_{jtmp}", "neghalf"], writes=[key_out])

        def hT_view(c, k=None):
            ch = chunks[c]
            if ch["kind"] == "real":
                t = hT[ch["st"] % 2]
                lo = ch["pos"] * 128
                key = f"hT{ch['st'] % 2}_{ch['pos']}"
            else:
                t = hTh
                lo = 0
                key = "hTh"
            if k is None:
                return t[:, :, lo:lo + 128], key
            return t[:, k, lo:lo + 128], key

        fe = dict(a1=0, a2=0, b=0)

        def fe_a1(c):
            ensure_loaded(c + LA)
            assert c < loaded[0], f"x ring too small: chunk {c} not loadable"
            sl = xslot[c]
            S.op("act", lambda E: E.activation(out=junk[:], in_=xs[sl][:], func=AF.Square, scale=1.0 / 32.0,
                                               accum_out=scol(c, 0)),
                 reads=[(f"xs{sl}", (c, "x"))], writes=[f"ms{c}", "junk"])
            rstd_ops(c, 0, 1, 2, f"ms{c}", f"rstd{c}")

        def fe_a2(c):
            sl = xslot[c]
            hs = c % 2
            S.op("dve", lambda E: E.scalar_tensor_tensor(out=htok[hs][:], in0=xs[sl][:], scalar=scol(c, 2),
                                                         in1=gbc[:], op0=ALU.mult, op1=ALU.mult),
                 reads=[(f"xs{sl}", (c, "x")), f"rstd{c}", "gbc"], writes=[(f"htok{hs}", c)])
            if chunks[c]["kind"] != "real":
                x_done[c] = True

        def fe_b(c):
            hs = c % 2

            def tr(E):
                ins = None
                for k in range(KD):
                    ins = E.transpose(tp[:, k * 128:(k + 1) * 128], htok[hs][:, k * 128:(k + 1) * 128], ident_bf[:])
                return ins
            S.op("pe", tr, reads=[(f"htok{hs}", c), "ident"], writes=[("tp", c)])
            dst, hk = hT_view(c)
            S.op("act", lambda E: E.copy(out=dst, in_=tp[:].rearrange("p (k t) -> p k t", k=KD)),
                 reads=[("tp", c)], writes=[(hk, c)])

        def fe_can(c):
            ensure_loaded(c + LA)
            return c < loaded[0]

        def fe_tick():
            cb = fe["b"]
            if cb >= NCH:
                return
            while fe["a1"] <= cb:
                fe_a1(fe["a1"]); fe["a1"] += 1
            while fe["a2"] <= cb:
                fe_a2(fe["a2"]); fe["a2"] += 1
            fe_b(cb)
            fe["b"] += 1
            if fe["a2"] == cb + 1 and cb + 1 < NCH and (fe["a1"] > cb + 1 or fe_can(cb + 1)):
                if fe["a1"] == cb + 1:
                    fe_a1(cb + 1); fe["a1"] += 1
                fe_a2(cb + 1); fe["a2"] += 1
            if fe["a1"] == cb + 2 and cb + 2 < NCH and fe_can(cb + 2):
                fe_a1(cb + 2); fe["a1"] += 1
            ensure_loaded(fe["a1"] + LA)

        def ensure_fe(upto):
            while fe["b"] <= min(upto, NCH - 1):
                fe_tick()

        def TM_a(c):
            ensure_fe(c)
            b = nxt("tok", NTOK)
            bk = f"tokb{b}"

            def mm(E):
                ins = None
                for k in range(KD):
                    lhsT, _ = hT_view(c, k)
                    ins = E.matmul(tokb[b][:], lhsT=lhsT, rhs=w_in_bf[:, k, 0:512],
                                   start=(k == 0), stop=(k == KD - 1))
                return ins
            S.op("pe", mm, reads=[(hT_view(c)[1], c)] + win_keys[0], writes=[bk])
            sa = c % NA
            S.op("act", lambda E: E.copy(out=a_tok[sa][:], in_=tokb[b][:]), reads=[bk], writes=[(f"a{sa}", c)])

        def TM_v(c):
            ensure_fe(c)
            b = nxt("tok", NTOK)
            bk = f"tokb{b}"

            def mm(E):
                ins = None
                for k in range(KD):
                    lhsT, _ = hT_view(c, k)
                    ins = E.matmul(tokb[b][:], lhsT=lhsT, rhs=w_in_bf[:, k, 3 * 512:4 * 512],
                                   start=(k == 0), stop=(k == KD - 1))
                return ins
            S.op("pe", mm, reads=[(hT_view(c)[1], c)] + win_keys[3], writes=[bk])
            S.op("dve", lambda E: E.bn_stats(out=stat[:, c * 16 + 4:c * 16 + 10], in_=tokb[b][:]),
                 reads=[bk], writes=[f"bst{c}"])
            S.op("dve", lambda E: E.bn_aggr(out=stat[:, c * 16 + 10:c * 16 + 12], in_=stat[:, c * 16 + 4:c * 16 + 10]),
                 reads=[f"bst{c}"], writes=[f"mv{c}"])
            rstd_ops(c, 11, 12, 13, f"mv{c}", f"rstdv{c}")

            def zfn():
                sz = chunks[c]["ridx"] % NZ
                S.op("dve", lambda E: E.tensor_scalar(out=z_tok[sz][:], in0=tokb[b][:], scalar1=scol(c, 10),
                                                      scalar2=scol(c, 13), op0=ALU.subtract, op1=ALU.mult),
                     reads=[bk, f"mv{c}", f"rstdv{c}"], writes=[(f"z{sz}", c)])
            return zfn

        def FM_block(g, kind, j):
            cg = dict(ga=1, u=2, gb=4)[kind]
            col0 = cg * 512 + j * 128
            slot = g % 2
            st = sts[g]
            b = nxt("fm", NFM)
            bk = f"fmb{b}"

            def mm(E):
                ins = None
                for k in range(KD):
                    ins = E.matmul(fmb[b][:], lhsT=w_in_bf[:, k, col0:col0 + 128], rhs=hT[slot][:, k, :],
                                   start=(k == 0), stop=(k == KD - 1))
                return ins
            wk = win_keys[cg] if cg == 1 else [f"win{cg}h{j // 2}"]
            S.op("pe", mm, reads=[(f"hT{slot}_{p}", st["c0"] + p) for p in range(4)] + wk, writes=[bk])
            if kind == "ga":
                S.op("act", lambda E: E.activation(out=sga[:, j, :], in_=fmb[b][:], func=AF.Silu),
                     reads=[bk], writes=[(f"sga{j}", g)])
            elif kind == "gb":
                s2 = 0
                S.op("act", lambda E: E.activation(out=sgb[s2][:], in_=fmb[b][:], func=AF.Silu),
                     reads=[bk], writes=[(f"sgb{s2}", (g, j))])
            else:
                s2 = 0
                S.op("dve", lambda E: E.tensor_tensor(out=usg[:, j, :], in0=fmb[b][:], in1=sgb[s2][:], op=ALU.mult),
                     reads=[bk, (f"sgb{s2}", (g, j))], writes=[(f"usg{j}", g)])

        def pm_idx(c, gq):
            ch = chunks[c]
            n = seg_chunks[ch["seg"]]
            if ch["e"] == 1:
                return 4 + ch["seg"] * 8 + gq
            if ch["e"] == n:
                return 4 + ch["seg"] * 8 + 4 + gq
            return gq

        def POOL(g, gq):
            st = sts[g]
            c0 = st["c0"]
            b = nxt("fm", NFM)
            bk = f"fmb{b}"

            def mm(E):
                first = True
                ins = None
                if st["has_prev"]:
                    ins = E.matmul(fmb[b][:, 0:8], lhsT=a_tok[(c0 - 1) % NA][:, gq * 128:(gq + 1) * 128],
                                   rhs=pm_bf[:, gq, 136:144], start=True, stop=False, skip_group_check=True)
                    first = False
                for j in range(4):
                    c = c0 + j
                    lo = 8 if j == 0 else 0
                    hi = 136 if j == 3 else 144
                    o0 = j * 128 - 8 + lo
                    last = (j == 3 and not st["has_next"])
                    ins = E.matmul(fmb[b][:, o0:o0 + (hi - lo)], lhsT=a_tok[c % NA][:, gq * 128:(gq + 1) * 128],
                                   rhs=pm_bf[:, pm_idx(c, gq), lo:hi], start=first, stop=last,
                                   skip_group_check=True)
                    first = False
                if st["has_next"]:
                    ins = E.matmul(fmb[b][:, 504:512], lhsT=a_tok[(c0 + 4) % NA][:, gq * 128:(gq + 1) * 128],
                                   rhs=pm_bf[:, gq, 0:8], start=False, stop=True, skip_group_check=True)
                return ins
            jr = range(-1 if st["has_prev"] else 0, 5 if st["has_next"] else 4)
            rd = [(f"a{(c0 + j) % NA}", c0 + j) for j in jr] + pm_keys
            S.op("pe", mm, reads=rd, writes=[bk])
            S.op("act", lambda E: E.copy(out=dT[:, gq, :], in_=fmb[b][:]), reads=[bk], writes=[(f"dT_{gq}", g)])

        def SGh(g, h):
            st = sts[g]
            ms = g % 2
            b = nxt("fm", NFM)
            bk = f"fmb{b}"

            def mm(E):
                ins = None
                for j in range(4):
                    c = st["c0"] + j
                    sz = chunks[c]["ridx"] % NZ
                    ins = E.matmul(fmb[b][:, j * 128:(j + 1) * 128], lhsT=z_tok[sz][:, h * 128:(h + 1) * 128],
                                   rhs=wst_bf[:, h, :], start=True, stop=True)
                return ins
            rd = [(f"z{chunks[st['c0'] + j]['ridx'] % NZ}", st["c0"] + j) for j in range(4)] + ["wst"]
            S.op("pe", mm, reads=rd, writes=[bk])
            ts = nxt("t1", 2)
            S.op("dve", lambda E: E.scalar_tensor_tensor(
                out=t1[ts][:].rearrange("p (j q) -> p j q", j=4),
                in0=fmb[b][:].rearrange("p (j q) -> p j q", j=4),
                scalar=cols[:, 12 + h:13 + h],
                in1=cmat[:, h, :].unsqueeze(1).to_broadcast([128, 4, 128]),
                op0=ALU.mult, op1=ALU.add), reads=[bk, "cols", "cmat"], writes=[(f"t1_{ts}", (g, h))])
            S.op("pool", lambda E: E.tensor_tensor(out=mixT[ms][:, 4 + h, :], in0=t1[ts][:], in1=usg[:, h, :],
                                                   op=ALU.mult),
                 reads=[(f"t1_{ts}", (g, h)), (f"usg{h}", g)], writes=[(f"mixT{ms}_{4 + h}", g)])

        def PWq(g, gq):
            ms = g % 2
            b = nxt("fm", NFM)
            bk = f"fmb{b}"
            S.op("pe", lambda E: E.matmul(fmb[b][:], lhsT=poolw_bf[:, gq, :], rhs=dT[:, gq, :],
                                          start=True, stop=True),
                 reads=[(f"dT_{gq}", g), "poolw"], writes=[bk])
            S.op("dve", lambda E: E.scalar_tensor_tensor(
                out=mixT[ms][:, gq, :], in0=fmb[b][:], scalar=cols[:, 8 + gq:9 + gq], in1=sga[:, gq, :],
                op0=ALU.mult, op1=ALU.mult), reads=[bk, "cols", (f"sga{gq}", g)], writes=[(f"mixT{ms}_{gq}", g)])

        def WO_main(g, j):
            st = sts[g]
            ms = g % 2
            c = st["c0"] + j
            sl = xslot[c]
            xk = f"xs{sl}"
            for hf in range(2):
                b = nxt("tok", NTOK)
                bk = f"tokb{b}"

                def mm(E):
                    ins = None
                    for k in range(KD):
                        ins = E.matmul(tokb[b][:], lhsT=mixT[ms][:, k, j * 128:(j + 1) * 128],
                                       rhs=w_out_bf[:, k, hf * 512:(hf + 1) * 512],
                                       start=(k == 0), stop=(k == KD - 1))
                    return ins
                S.op("pe", mm, reads=[(f"mixT{ms}_{k}", g) for k in range(KD)] + wout_keys, writes=[bk])
                S.op("dve", lambda E: E.tensor_tensor(out=xs[sl][:, hf * 512:(hf + 1) * 512], in0=tokb[b][:],
                                                      in1=xs[sl][:, hf * 512:(hf + 1) * 512], op=ALU.add),
                     reads=[bk, (xk, (c, "x" if hf == 0 else "xo0"))], writes=[(xk, (c, "xo0" if hf == 0 else "xo"))])
            S.op("act", lambda E: E.activation(out=junk[:], in_=xs[sl][:], func=AF.Square, scale=1.0 / 32.0,
                                               accum_out=scol(c, 3)),
                 reads=[(xk, (c, "xo"))], writes=[f"ms2_{c}", "junk"])
            rstd_ops(c, 3, 14, 15, f"ms2_{c}", f"rstd2_{c}")

            def tail():
                S.op("dve", lambda E: E.scalar_tensor_tensor(out=xs[sl][:], in0=xs[sl][:], scalar=scol(c, 15),
                                                             in1=fgb[:], op0=ALU.mult, op1=ALU.mult),
                     reads=[(xk, (c, "xo")), f"rstd2_{c}", "fgb"], writes=[(xk, (c, "y"))])
                r = chunks[c]["ridx"]
                S.op("sp", lambda E: E.dma_start(out=yout[r * 128:(r + 1) * 128, :], in_=xs[sl][:]),
                     reads=[(xk, (c, "y"))], dsem=f"st{sl}")
                x_done[c] = True
                ensure_loaded(fe["a1"] + LA)
            return tail

        load_ident()
        load_win(0)
        load_win(3)
        ensure_loaded(3)
        for c_ in range(3):
            fe_a1(c_)
        fe["a1"] = 3
        for c_ in range(2):
            fe_a2(c_)
        fe["a2"] = 2
        fe_tick()
        fe_tick()
        load_pm()
        load_win(1)
        late_loads = [lambda: (load_win_half(4, 0), load_win_half(2, 0)), load_small,
                      lambda: (load_win_half(4, 1), load_win_half(2, 1))]

        tma_done = [0]
        deferred = []
        for g, st in enumerate(sts):
            c0 = st["c0"]
            if g > 0:
                ensure_fe(c0 + 4)
            while tma_done[0] < c0 + 1:
                TM_a(tma_done[0])
                tma_done[0] += 1
            for j in range(4):
                if g == 0:
                    ensure_fe(c0 + j + 2)
                zfn = TM_v(c0 + j)
                if j < 3 or st["has_next"]:
                    TM_a(c0 + j + 1)
                    tma_done[0] = c0 + j + 2
                zfn()
                if late_loads:
                    late_loads.pop(0)()
            if g > 0:
                tails = []
                for j in range(4):
                    tails.append(WO_main(g - 1, j))
                    if j >= 1:
                        tails[j - 1]()
                deferred.append(tails[3])
            nxt_c0 = sts[g + 1]["c0"] if g + 1 < NST else None
            tick_target = (nxt_c0 + 3) if nxt_c0 is not None else NCH - 1
            n_ticks = max(0, tick_target - fe["b"] + 1)
            if n_ticks <= 3:
                tick_pos = {("P", 1), ("P", 3), ("SG", 0)}
            elif n_ticks == 4:
                tick_pos = {("P", 1), ("P", 3), ("SG", 0), ("SG", 1)}
            else:
                tick_pos = {("P", 0), ("P", 1), ("P", 2), ("P", 3), ("PW", 0), ("SG", 0), ("SG", 1)}
            base = [("ga", 0), ("P", 0), ("ga", 1), ("P", 1), ("ga", 2), ("P", 2), ("ga", 3), ("P", 3),
                    ("gb", 0), ("PW", 0), ("u", 0), ("PW", 1), ("gb", 1), ("SG", 0), ("u", 1), ("PW", 2),
                    ("gb", 2), ("SG", 1), ("u", 2), ("PW", 3), ("gb", 3), ("SG", 2), ("u", 3), ("tick4",),
                    ("SG", 3)]
            if g == NST - 1:
                base = [("gb", 0), ("P", 0), ("u", 0), ("P", 1), ("gb", 1), ("SG", 0), ("u", 1), ("P", 2),
                        ("gb", 2), ("SG", 1), ("u", 2), ("P", 3), ("gb", 3), ("SG", 2), ("u", 3), ("SG", 3),
                        ("ga", 0), ("PW", 0), ("ga", 1), ("PW", 1), ("ga", 2), ("ga", 3), ("PW", 2), ("PW", 3)]
            seq = []
            for it in base:
                seq.append(it)
                if it in tick_pos:
                    seq.append(("tick",))
            bi = 0
            for it in seq:
                if it[0] in ("ga", "gb", "u"):
                    FM_block(g, it[0], it[1])
                    if deferred:
                        deferred.pop(0)()
                    if g == 0 and bi == 0:
                        load_wout()
                    bi += 1
                elif it[0] == "P":
                    POOL(g, it[1])
                elif it[0] == "SG":
                    if g == 0 and it[1] == 0:
                        compute_cmat()
                    SGh(g, it[1])
                elif it[0] == "PW":
                    PWq(g, it[1])
                elif it[0] == "tick":
                    if nxt_c0 is not None and fe["b"] <= nxt_c0 + 3:
                        fe_tick()
                    elif nxt_c0 is None and fe["b"] < NCH:
                        fe_tick()
                elif it[0] == "tick4":
                    if nxt_c0 is not None:
                        ensure_fe(nxt_c0 + 3)
                        ensure_fe(nxt_c0 + 4)
        tails = []
        for j in range(4):
            tails.append(WO_main(NST - 1, j))
            if j >= 1:
                tails[j - 1]()
        tails[3]()
        assert not deferred
        S.final_wait("sp")
        print(f"[build] nwaits={S.nwaits} counts={ {k: v for k, v in S.cnt.items() if k in ('pe', 'act', 'dve', 'pool')} }")
    return nc, dict(NCH=NCH, nreal=nreal, chunks=chunks)


def _pool_mats(first_is_start, last_is_end):
    out = {}
    for gq, w in enumerate(WINDOWS):
        h = w // 2
        s = np.arange(128)[:, None]
        t = np.arange(128)[None, :]
        inwin = ((s >= t - h) & (s < t + h)).astype(np.float64)
        eye = (s == t).astype(np.float64)
        out[("cur", gq)] = (inwin / w - eye).astype(np.float32)
        out[("prev", gq)] = ((s - 128 >= t - h).astype(np.float64) / w).astype(np.float32)
        out[("next", gq)] = ((s + 128 < t + h).astype(np.float64) / w).astype(np.float32)
        cnt_start = np.minimum(w, t + h).astype(np.float64)
        out[("cur_start", gq)] = (inwin / cnt_start - eye).astype(np.float32)
        cnt_end = np.minimum(w, 128 - t + h).astype(np.float64)
        out[("cur_end", gq)] = (inwin / cnt_end - eye).astype(np.float32)
    return out


def _pm_for_core(seg_start_end):
    m = _pool_mats(None, None)

    def wide(cur_key, gq):
        return np.concatenate([m[("next", gq)][:, 120:128], m[(cur_key, gq)], m[("prev", gq)][:, 0:8]], axis=1)
    mats = [wide("cur", gq) for gq in range(4)]
    for (fs, le) in seg_start_end:
        mats += [wide("cur_start" if fs else "cur", gq) for gq in range(4)]
        mats += [wide("cur_end" if le else "cur", gq) for gq in range(4)]
    arr = np.stack(mats, axis=0)
    assert arr.shape == (NPM, 128, PMW)
    return np.ascontiguousarray(arr.transpose(1, 0, 2).reshape(128, -1))


def _ext_segment(x_seq, lo, hi):
    S_ = x_seq.shape[0]
    out = np.zeros((hi - lo + 256, x_seq.shape[1]), np.float32)
    a = max(lo - 128, 0)
    b = min(hi + 128, S_)
    out[a - (lo - 128):b - (lo - 128)] = x_seq[a:b]
    return out


def make_in_maps(x_prompt, x_sample, norm_g, w_in, pool_w, pool_scale, sgu_ln_g, sgu_ln_b,
                 w_spatial, b_spatial, w_out, final_g, ncores=NCORES, seg_chunks=SEG_CHUNKS):
    f = np.float32
    w_in = np.ascontiguousarray(w_in, f)
    w_out = np.ascontiguousarray(w_out, f)
    cols = np.concatenate([np.asarray(norm_g, f).reshape(8, 128).T, np.asarray(pool_scale, f).reshape(4, 128).T,
                           np.asarray(sgu_ln_g, f).reshape(4, 128).T], axis=1)
    cols = np.ascontiguousarray(cols)
    fgb = np.ascontiguousarray(np.broadcast_to(np.asarray(final_g, f)[None, :], (128, D)))
    gbc = np.ascontiguousarray(np.broadcast_to(np.asarray(norm_g, f)[None, :], (128, D)))
    lnbb = np.ascontiguousarray(np.broadcast_to(np.asarray(sgu_ln_b, f)[None, :], (128, 512)))
    bsb = np.ascontiguousarray(np.broadcast_to(np.asarray(b_spatial, f).reshape(1, 512), (128, 512)))
    poolw = np.ascontiguousarray(np.asarray(pool_w, f).transpose(1, 0, 2).reshape(128, 512))
    wst = np.ascontiguousarray(np.asarray(w_spatial, f).transpose(2, 0, 1).reshape(128, 512))
    ident = np.eye(128, dtype=f)
    n0, n1 = seg_chunks
    L0, L1 = n0 * 128, n1 * 128
    halves = x_prompt.shape[1] // L0
    in_maps = []
    for c in range(ncores):
        b, hf = c // halves, c % halves
        seg0 = _ext_segment(np.asarray(x_prompt[b], f), hf * L0, (hf + 1) * L0)
        seg1 = np.asarray(x_sample[c], f)
        xin = np.concatenate([seg0, seg1], axis=0)
        pm = _pm_for_core([(hf == 0, hf == halves - 1), (True, True)])
        in_maps.append(dict(xin=xin, w_in=w_in, w_out=w_out, pm=pm, poolw=poolw, wst=wst, ident=ident,
                            cols=cols, fgb=fgb, gbc=gbc, lnbb=lnbb, bsb=bsb))
    return in_maps


_NC_CACHE = {}


def kernel(x_prompt, x_sample, norm_g, w_in, pool_w, pool_scale, sgu_ln_g, sgu_ln_b,
           w_spatial, b_spatial, w_out, final_g):
    x_prompt = np.asarray(x_prompt)
    x_sample = np.asarray(x_sample)
    in_maps = make_in_maps(x_prompt, x_sample, norm_g, w_in, pool_w, pool_scale, sgu_ln_g, sgu_ln_b,
                           w_spatial, b_spatial, w_out, final_g)
    if "nc" not in _NC_CACHE:
        _NC_CACHE["nc"] = build_nc()[0]
    nc = _NC_CACHE["nc"]
    res = run_bass_kernel_spmd(nc, in_maps, core_ids=list(range(NCORES)))
    n0, n1 = SEG_CHUNKS
    L0, L1 = n0 * 128, n1 * 128
    y_prompt = np.empty(x_prompt.shape, np.float32)
    y_sample = np.empty(x_sample.shape, np.float32)
    halves = x_prompt.shape[1] // L0
    for c in range(NCORES):
        y = res.results[c]["yout"]
        b, hf = c // halves, c % halves
        y_prompt[b, hf * L0:(hf + 1) * L0] = y[:L0]
        y_sample[c] = y[L0:L0 + L1]
    return (y_prompt, y_sample)
```
